# Optimizing a Trainium2 kernel written in Bass

```python
import math
import jax, jax.numpy as jnp
from jax import lax
import numpy as np

D_MODEL = 1024
BATCH = 8
SEQ = 4096
DEPTH = 2

CHUNK = 64
Q_BLOCK = 128
RMS_EPS = 1e-6

N_BUCKETS = 32
MAX_DISTANCE = 128

A_HEADS = 4
A_HEAD_DIM = 64
A_V_DIM = 2 * A_HEAD_DIM
A_WIDTH = A_HEADS * A_V_DIM

B_HEADS = 8
B_HEAD_DIM = 64
B_WIDTH = B_HEADS * B_HEAD_DIM
IDX_HEADS = 8
IDX_DIM = 64
TOPK_MAX = 256

BIAS_HEADS = A_HEADS + B_HEADS
L0_WIDTH = A_WIDTH + B_WIDTH
L0_SIZES = (A_HEADS * 2 * A_HEAD_DIM, A_HEADS * 2 * A_HEAD_DIM, A_WIDTH,
            B_WIDTH, B_HEAD_DIM, B_HEAD_DIM,
            IDX_HEADS * IDX_DIM, IDX_DIM, IDX_HEADS,
            L0_WIDTH)
L0_COLS = sum(L0_SIZES)

C_HEADS = 16
C_NOPE = 64
C_ROPE = 32
C_V = 64
Q_LORA = 384
KV_LORA = 256
C_WIDTH = C_HEADS * C_V
ROPE_THETA = 10000.0
L1_SIZES = (Q_LORA, KV_LORA, C_ROPE, C_WIDTH)
L1_COLS = sum(L1_SIZES)

N_EVEN = (DEPTH + 1) // 2
N_ODD = DEPTH // 2

kernel_name = 'hybrid_diff_dsa_mla_stream_encoder'


def rms_norm(x, g):
    xf = x.astype(jnp.float32)
    y = xf * lax.rsqrt(jnp.mean(xf * xf, axis=-1, keepdims=True) + RMS_EPS)
    return (y * g.astype(jnp.float32)).astype(x.dtype)


def split_cols(t, sizes):
    return jnp.split(t, np.cumsum(sizes)[:-1].tolist(), axis=-1)


def to_blocks(t):
    b, s = t.shape[:2]
    return t.reshape((b, s // Q_BLOCK, Q_BLOCK) + t.shape[2:]).swapaxes(0, 1)


def from_blocks(o):
    nb, b, q = o.shape[:3]
    return o.swapaxes(0, 1).reshape((b, nb * q) + o.shape[3:])


def chunk_mask(qpos, kpos):
    return (kpos[None, :] // CHUNK) <= (qpos[:, None] // CHUNK)


def t5_bucket(rel):
    nb = N_BUCKETS // 2
    ret = jnp.where(rel > 0, nb, 0)
    n = jnp.abs(rel)
    max_exact = nb // 2
    n_f = jnp.maximum(n, 1).astype(jnp.float32)
    large = max_exact + (jnp.log(n_f / max_exact) / math.log(MAX_DISTANCE / max_exact)
                         * (nb - max_exact)).astype(jnp.int32)
    large = jnp.minimum(large, nb - 1)
    return ret + jnp.where(n < max_exact, n, large)


def rope_tables(s):
    inv = ROPE_THETA ** (-jnp.arange(0, C_ROPE, 2, dtype=jnp.float32) / C_ROPE)
    ang = jnp.arange(s, dtype=jnp.float32)[:, None] * inv[None, :]
    return jnp.cos(ang), jnp.sin(ang)


def apply_rope(x, cos, sin):
    x1, x2 = jnp.split(x, 2, axis=-1)
    c = cos.astype(x.dtype)
    s = sin.astype(x.dtype)
    return jnp.concatenate([x1 * c - x2 * s, x1 * s + x2 * c], axis=-1)


def diff_attention(q, k, v, lam, bias_tab):
    s_len = q.shape[1]
    nb = s_len // Q_BLOCK
    kpos = jnp.arange(s_len)
    scale = A_HEAD_DIM ** -0.5

    def block(args):
        q_blk, bi = args
        qpos = bi * Q_BLOCK + jnp.arange(Q_BLOCK)
        s = jnp.einsum('bqhmd,bkhmd->bhmqk', q_blk, k).astype(jnp.float32) * scale
        bias = bias_tab[t5_bucket(kpos[None, :] - qpos[:, None])].astype(jnp.float32)
        s = s + jnp.transpose(bias, (2, 0, 1))[None, :, None]
        s = jnp.where(chunk_mask(qpos, kpos), s, -jnp.inf)
        p = jax.nn.softmax(s, axis=-1)
        p = p[:, :, 0] - lam * p[:, :, 1]
        return jnp.einsum('bhqk,bkhe->bqhe', p.astype(v.dtype), v)

    return from_blocks(lax.map(block, (to_blocks(q), jnp.arange(nb))))


def dsa_attention(q, k, v, q_idx, k_idx, w_idx, bias_tab, top_k):
    b, s_len = q.shape[:2]
    nb = s_len // Q_BLOCK
    kpos = jnp.arange(s_len)
    scale = B_HEAD_DIM ** -0.5

    def block(args):
        q_blk, qi_blk, w_blk, bi = args
        qpos = bi * Q_BLOCK + jnp.arange(Q_BLOCK)
        logits = jnp.einsum('bqhd,bkd->bqhk', qi_blk, k_idx).astype(jnp.float32)
        score = jnp.einsum('bqh,bqhk->bqk', w_blk.astype(jnp.float32), jax.nn.relu(logits))
        score = jnp.where(chunk_mask(qpos, kpos)[None], score, -jnp.inf)
        _, sel = lax.top_k(score, top_k)
        valid = (sel // CHUNK) <= (qpos // CHUNK)[None, :, None]
        flat = sel.reshape(b, -1, 1)
        k_sel = jnp.take_along_axis(k, flat, axis=1).reshape(b, Q_BLOCK, top_k, B_HEAD_DIM)
        v_sel = jnp.take_along_axis(v, flat, axis=1).reshape(b, Q_BLOCK, top_k, B_HEAD_DIM)
        s = jnp.einsum('bqhd,bqkd->bhqk', q_blk, k_sel).astype(jnp.float32) * scale
        bias = bias_tab[t5_bucket(sel - qpos[None, :, None])].astype(jnp.float32)
        s = s + jnp.transpose(bias, (0, 3, 1, 2))
        s = jnp.where(valid[:, None], s, -jnp.inf)
        p = jax.nn.softmax(s, axis=-1)
        return jnp.einsum('bhqk,bqkd->bqhd', p.astype(v.dtype), v_sel)

    xs = (to_blocks(q), to_blocks(q_idx), to_blocks(w_idx), jnp.arange(nb))
    return from_blocks(lax.map(block, xs))


def mla_attention(q_nope, q_rope, k_nope, k_rope, v):
    s_len = q_nope.shape[1]
    nb = s_len // Q_BLOCK
    kpos = jnp.arange(s_len)
    scale = (C_NOPE + C_ROPE) ** -0.5

    def block(args):
        qn, qr, bi = args
        qpos = bi * Q_BLOCK + jnp.arange(Q_BLOCK)
        s = (jnp.einsum('bqhd,bkhd->bhqk', qn, k_nope)
             + jnp.einsum('bqhr,bkr->bhqk', qr, k_rope)).astype(jnp.float32) * scale
        s = jnp.where(chunk_mask(qpos, kpos), s, -jnp.inf)
        p = jax.nn.softmax(s, axis=-1)
        return jnp.einsum('bhqk,bkhd->bqhd', p.astype(v.dtype), v)

    xs = (to_blocks(q_nope), to_blocks(q_rope), jnp.arange(nb))
    return from_blocks(lax.map(block, xs))


def diff_dsa_layer(x, norm_g, w_in, lq1, lk1, lq2, lk2, subln_g, w_o, rel_bias, layer, top_k):
    b, s_len, _ = x.shape
    h = rms_norm(x, norm_g)
    proj = h @ w_in
    qa, ka, va, qb, kb, vb, qi, ki, wi, gate = split_cols(proj, L0_SIZES)
    lam_init = 0.8 - 0.6 * math.exp(-0.3 * layer)
    f32 = jnp.float32
    lam = (jnp.exp(jnp.sum(lq1.astype(f32) * lk1.astype(f32)))
           - jnp.exp(jnp.sum(lq2.astype(f32) * lk2.astype(f32))) + lam_init)
    o_a = diff_attention(qa.reshape(b, s_len, A_HEADS, 2, A_HEAD_DIM),
                         ka.reshape(b, s_len, A_HEADS, 2, A_HEAD_DIM),
                         va.reshape(b, s_len, A_HEADS, A_V_DIM),
                         lam, rel_bias[:, :A_HEADS])
    o_a = rms_norm(o_a, subln_g) * (1.0 - lam_init)
    o_b = dsa_attention(qb.reshape(b, s_len, B_HEADS, B_HEAD_DIM), kb, vb,
                        qi.reshape(b, s_len, IDX_HEADS, IDX_DIM), ki, wi,
                        rel_bias[:, A_HEADS:], top_k)
    mix = jnp.concatenate([o_a.reshape(b, s_len, A_WIDTH), o_b.reshape(b, s_len, B_WIDTH)], axis=-1)
    return x + (mix * jax.nn.silu(gate)) @ w_o


def mla_layer(x, norm_g, w_in, q_norm, w_uq, kv_norm, w_ukv, w_o, cos, sin):
    b, s_len, _ = x.shape
    h = rms_norm(x, norm_g)
    proj = h @ w_in
    q_lat, kv_lat, k_rope, gate = split_cols(proj, L1_SIZES)
    q = (rms_norm(q_lat, q_norm) @ w_uq).reshape(b, s_len, C_HEADS, C_NOPE + C_ROPE)
    kv = (rms_norm(kv_lat, kv_norm) @ w_ukv).reshape(b, s_len, C_HEADS, C_NOPE + C_V)
    q_nope, q_rope = q[..., :C_NOPE], q[..., C_NOPE:]
    k_nope, v = kv[..., :C_NOPE], kv[..., C_NOPE:]
    q_rope = apply_rope(q_rope, cos[:, None, :], sin[:, None, :])
    k_rope = apply_rope(k_rope, cos, sin)
    o = mla_attention(q_nope, q_rope, k_nope, k_rope, v).reshape(b, s_len, C_WIDTH)
    return x + (o * jax.nn.silu(gate)) @ w_o


def setup_inputs(seed: int = 0) -> dict:
    key = jax.random.key(seed)
    ks = jax.random.split(key, 20)
    f32 = jnp.float32

    def dense(k, shape, fan_in):
        return jax.random.normal(k, shape, f32) * (fan_in ** -0.5)

    def gain(k, shape):
        return 1.0 + 0.02 * jax.random.normal(k, shape, f32)

    return {
        'x': jax.random.normal(ks[0], (BATCH, SEQ, D_MODEL), f32),
        'rel_bias': 0.2 * jax.random.normal(ks[1], (N_BUCKETS, BIAS_HEADS), f32),
        'e_norm': gain(ks[2], (N_EVEN, D_MODEL)),
        'e_w_in': dense(ks[3], (N_EVEN, D_MODEL, L0_COLS), D_MODEL),
        'e_lam_q1': 0.1 * jax.random.normal(ks[4], (N_EVEN, A_HEAD_DIM), f32),
        'e_lam_k1': 0.1 * jax.random.normal(ks[5], (N_EVEN, A_HEAD_DIM), f32),
        'e_lam_q2': 0.1 * jax.random.normal(ks[6], (N_EVEN, A_HEAD_DIM), f32),
        'e_lam_k2': 0.1 * jax.random.normal(ks[7], (N_EVEN, A_HEAD_DIM), f32),
        'e_subln': gain(ks[8], (N_EVEN, A_V_DIM)),
        'e_w_o': dense(ks[9], (N_EVEN, L0_WIDTH, D_MODEL), L0_WIDTH),
        'o_norm': gain(ks[10], (N_ODD, D_MODEL)),
        'o_w_in': dense(ks[11], (N_ODD, D_MODEL, L1_COLS), D_MODEL),
        'o_q_norm': gain(ks[12], (N_ODD, Q_LORA)),
        'o_w_uq': dense(ks[13], (N_ODD, Q_LORA, C_HEADS * (C_NOPE + C_ROPE)), Q_LORA),
        'o_kv_norm': gain(ks[14], (N_ODD, KV_LORA)),
        'o_w_ukv': dense(ks[15], (N_ODD, KV_LORA, C_HEADS * (C_NOPE + C_V)), KV_LORA),
        'o_w_o': dense(ks[16], (N_ODD, C_WIDTH, D_MODEL), C_WIDTH),
        'final_norm': gain(ks[17], (D_MODEL,)),
    }


def reference(x, rel_bias, e_norm, e_w_in, e_lam_q1, e_lam_k1, e_lam_q2, e_lam_k2, e_subln, e_w_o,
              o_norm, o_w_in, o_q_norm, o_w_uq, o_kv_norm, o_w_ukv, o_w_o, final_norm):
    s_len = x.shape[1]
    top_k = min(TOPK_MAX, s_len // 4)
    cos, sin = rope_tables(s_len)
    h = x
    for layer in range(DEPTH):
        i = layer // 2
        if layer % 2 == 0:
            h = diff_dsa_layer(h, e_norm[i], e_w_in[i], e_lam_q1[i], e_lam_k1[i], e_lam_q2[i],
                               e_lam_k2[i], e_subln[i], e_w_o[i], rel_bias, layer, top_k)
        else:
            h = mla_layer(h, o_norm[i], o_w_in[i], o_q_norm[i], o_w_uq[i], o_kv_norm[i],
                          o_w_ukv[i], o_w_o[i], cos, sin)
    return rms_norm(h, final_norm)
```

```python
import math
from contextlib import ExitStack
import numpy as np
import ml_dtypes
import concourse.bass as bass
import concourse.mybir as mybir
from concourse.bass_utils import run_bass_kernel_spmd

F32 = mybir.dt.float32
BF16 = mybir.dt.bfloat16
AF = mybir.ActivationFunctionType
ALU = mybir.AluOpType
AX = mybir.AxisListType

D = 1024
EPS = 1e-6
L0_COLS = 3784
L1_COLS = 1696
NEG = -30000.0

SAME_ENGINE_SYNC = {"act", "dve", "pool"}


class Prog:
    QUEUES = ("pe", "act", "dve", "pool", "sp")
    NDSEM = 44
    NSP = 30

    def __init__(self, nc, es):
        self.nc = nc
        self.sem = {}
        self.count = {}
        for q in self.QUEUES:
            self.sem[q] = es.enter_context(nc.semaphore("s_" + q))
            self.count[q] = 0
        self.dsem = [es.enter_context(nc.semaphore("sd%d" % i)) for i in range(self.NDSEM)]
        self.dcount = [0] * self.NDSEM
        self.reset()

    def reset(self):
        self.ops = {q: [] for q in self.QUEUES}
        self.last_w = {}
        self.readers = {}
        self.nid = {}
        self.gmap = {}
        self.nsp = 0
        self.npool = 0

    def add(self, q, fn, r=(), w=(), dma=False, grp=None):
        if dma:
            if grp is None:
                grp = w[0] if len(w) else r[0]
            if grp not in self.gmap:
                if q == "pool":
                    self.gmap[grp] = self.NDSEM - 1 - self.npool
                    self.npool += 1
                else:
                    self.gmap[grp] = self.nsp
                    self.nsp += 1
                assert self.nsp <= self.NSP and self.npool <= self.NDSEM - self.NSP, "too many DMA groups"
            ident = ("g", grp)
        else:
            ident = q
        idx = self.nid.get(ident, 0)
        self.nid[ident] = idx + 1
        deps = {}

        def dep(e, i):
            if e == ident and not dma and q not in SAME_ENGINE_SYNC:
                return
            if deps.get(e, -1) < i:
                deps[e] = i

        for k in r:
            if k in self.last_w:
                dep(*self.last_w[k])
        for k in w:
            if k in self.last_w:
                dep(*self.last_w[k])
            for e, i in self.readers.get(k, {}).items():
                dep(e, i)
        for k in r:
            self.readers.setdefault(k, {})[ident] = idx
        for k in w:
            self.last_w[k] = (ident, idx)
            self.readers[k] = {}
        self.ops[q].append(dict(fn=fn, ident=ident, idx=idx, deps=deps, dma=dma, sig=dma))

    def _semof(self, ident):
        if isinstance(ident, tuple):
            return self.dsem[self.gmap[ident[1]]]
        return self.sem[ident]

    def emit(self):
        nc = self.nc
        byid = {}
        for q in self.QUEUES:
            for op in self.ops[q]:
                byid.setdefault(op["ident"], {})[op["idx"]] = op
        for q in self.QUEUES:
            for op in self.ops[q]:
                for e, i in op["deps"].items():
                    byid[e][i]["sig"] = True
        val = {}
        final = {}
        for ident, d in byid.items():
            isd = isinstance(ident, tuple)
            c = self.dcount[self.gmap[ident[1]]] if isd else self.count[ident]
            v = {}
            for i in range(len(d)):
                if d[i]["sig"]:
                    c += 16 if isd else 1
                v[i] = c
            val[ident] = v
            final[ident] = c
            if isd:
                self.dcount[self.gmap[ident[1]]] = c
            else:
                self.count[ident] = c
        with nc.Block() as block:
            engs = dict(pe=block.tensor, act=block.scalar, dve=block.vector, pool=block.gpsimd, sp=block.sync)
            for q in self.QUEUES:
                ops = self.ops[q]
                if not ops:
                    continue

                def body(e, ops=ops, q=q):
                    waited = {}
                    for op in ops:
                        for ident, i in op["deps"].items():
                            v = val[ident][i]
                            if waited.get(ident, -1) < v:
                                e.wait_ge(self._semof(ident), v)
                                waited[ident] = v
                        ins = op["fn"](e)
                        if op["sig"]:
                            ins.then_inc(self._semof(op["ident"]), 16 if op["dma"] else 1)
                    for ident in {o["ident"]: 1 for o in ops if o["dma"]}:
                        if waited.get(ident, -1) < final[ident]:
                            e.wait_ge(self._semof(ident), final[ident])

                engs[q](body)
        self.reset()


def _sl(i, n):
    return slice(i * n, (i + 1) * n)


def phase_proj0(P, nc, S, T):
    NT = S // 128
    NG = S // 512
    with ExitStack() as es:
        def sb(name, shape, dt):
            return es.enter_context(nc.sbuf_tensor(name, shape, dt))

        def ps(name, shape, dt):
            return es.enter_context(nc.psum_tensor(name, shape, dt))

        wsb = sb("p0_w", [128, 8, L0_COLS], BF16)
        wkk = sb("p0_wkk", [128, 8, 128], BF16)
        wvw = sb("p0_wvw", [128, 8, 72], BF16)
        wst = [sb("p0_wst%d" % i, [128, L0_COLS], F32) for i in range(2)]
        gcol = sb("p0_g", [128, 8], F32)
        ident = sb("p0_id", [128, 128], BF16)
        xt = [sb("p0_x%d" % i, [128, D], F32) for i in range(2)]
        hb = [sb("p0_h%d" % i, [128, D], BF16) for i in range(2)]
        hT = [sb("p0_hT%d" % i, [128, 8, 512], BF16) for i in range(2)]
        junk = sb("p0_junk", [128, D], BF16)
        ssq = sb("p0_ssq", [128, NT], F32)
        rs = sb("p0_rs", [128, NT], F32)
        ev = [sb("p0_ev%d" % i, [128, 512], BF16) for i in range(4)]
        evw = [sb("p0_evw%d" % i, [128, 8], F32) for i in range(2)]
        evb = [sb("p0_evb%d" % i, [128, 64], BF16) for i in range(2)]
        psT = [ps("p0_psT%d" % i, [128, D], BF16) for i in range(2)]
        psF = [ps("p0_psF%d" % i, [128, 512], F32) for i in range(4)]
        psW = [ps("p0_psW%d" % i, [128, 72], F32) for i in range(2)]

        P.add("sp", lambda e: e.dma_start(out=ident[:], in_=T["c_ident"][:, :]), w=["ident"], dma=True)
        P.add("sp", lambda e: e.dma_start(out=gcol[:], in_=T["e_norm"].rearrange("(k p) -> p k", p=128),
                                          allow_slow_non_contiguous=True),
              w=["gcol"], dma=True)
        done_tiles = set()

        def tile_ops(t):
            if t in done_tiles:
                return
            done_tiles.add(t)
            xs = t % 2
            hs = (t // 4) % 2
            t4 = t % 4
            P.add("sp", lambda e, t=t, xs=xs: e.dma_start(out=xt[xs][:], in_=T["x"][_sl(t, 128), :]),
                  w=[("xt", xs)], dma=True)
            P.add("act", lambda e, t=t, xs=xs: e.activation(out=junk[:], in_=xt[xs][:], func=AF.Square,
                                                            accum_out=ssq[:, t:t + 1]),
                  r=[("xt", xs)], w=["junk", ("ssq", t)])
            P.add("act", lambda e, t=t: e.activation(out=rs[:, t:t + 1], in_=ssq[:, t:t + 1], func=AF.Sqrt,
                                                     bias=EPS, scale=1.0 / D),
                  r=[("ssq", t)], w=[("rs", t)])
            P.add("dve", lambda e, t=t: e.reciprocal(out=rs[:, t:t + 1], in_=rs[:, t:t + 1]),
                  r=[("rs", t)], w=[("rs", t)])
            P.add("dve", lambda e, t=t, xs=xs: e.tensor_scalar(out=hb[xs][:], in0=xt[xs][:], scalar1=rs[:, t:t + 1],
                                                               scalar2=None, op0=ALU.mult),
                  r=[("xt", xs), ("rs", t)], w=[("hb", xs)])
            for k in range(8):
                P.add("pe", lambda e, k=k, xs=xs: e.transpose(out=psT[xs][:, _sl(k, 128)], in_=hb[xs][:, _sl(k, 128)],
                                                              identity=ident[:]),
                      r=[("hb", xs), "ident"], w=[("psT", xs)])
            P.add("act", lambda e, xs=xs, hs=hs, t4=t4: e.copy(
                out=hT[hs][:, :, _sl(t4, 128)], in_=psT[xs][:].rearrange("p (k t) -> p k t", k=8)),
                r=[("psT", xs)], w=[("hT", hs)])

        tile_ops(0)
        tile_ops(1)
        for k in range(8):
            s = k % 2
            P.add("sp", lambda e, k=k, s=s: e.dma_start(out=wst[s][:], in_=T["e_w_in"][_sl(k, 128), :]),
                  w=[("wst", s)], dma=True)
            P.add("dve", lambda e, k=k, s=s: e.tensor_scalar(out=wsb[:, k, :], in0=wst[s][:], scalar1=gcol[:, k:k + 1],
                                                            scalar2=None, op0=ALU.mult),
                  r=[("wst", s), "gcol"], w=[("w", k)])
            P.add("pool", lambda e, k=k: e.tensor_copy(out=wkk[:, k, 0:64], in_=wsb[:, k, 2048:2112]),
                  r=[("w", k)], w=[("wkk", k)])
            P.add("pool", lambda e, k=k: e.tensor_copy(out=wkk[:, k, 64:128], in_=wsb[:, k, 2688:2752]),
                  r=[("w", k)], w=[("wkk", k)])
            P.add("pool", lambda e, k=k: e.tensor_copy(out=wvw[:, k, 0:64], in_=wsb[:, k, 2112:2176]),
                  r=[("w", k)], w=[("wvw", k)])
            P.add("pool", lambda e, k=k: e.tensor_copy(out=wvw[:, k, 64:72], in_=wsb[:, k, 2752:2760]),
                  r=[("w", k)], w=[("wvw", k)])
        wall = [("w", k) for k in range(8)]
        wkk_all = [("wkk", k) for k in range(8)]
        wvw_all = [("wvw", k) for k in range(8)]

        chunks = []
        for c in range(4):
            chunks.append((c * 128, "qaT", c * 128, "q"))
        for c in range(4):
            chunks.append((512 + c * 128, "kaT", c * 128, "c"))
        for c in range(4):
            chunks.append((1536 + c * 128, "qbT", c * 128, "q"))
        for c in range(4):
            chunks.append((2176 + c * 128, "qiT", c * 128, "c"))
        chunks.append((None, "kk", 0, "c"))
        for c in range(8):
            chunks.append((2760 + c * 128, "gT", c * 128, "silu"))

        nev = 0
        nF = 0
        for g in range(NG):
            hs = g % 2
            for t4 in range(4):
                t = g * 4 + t4
                xs = t % 2
                tile_ops(t)
            for (c0, dst, drow, mode) in chunks:
                pf = nF % 4
                nF += 1
                for k in range(8):
                    if c0 is None:
                        lhs = (lambda k=k: wkk[:, k, :])
                        rk = wkk_all
                    else:
                        lhs = (lambda k=k, c0=c0: wsb[:, k, c0:c0 + 128])
                        rk = wall
                    P.add("pe", lambda e, k=k, lhs=lhs, pf=pf, hs=hs: e.matmul(
                        psF[pf][:], lhsT=lhs(), rhs=hT[hs][:, k, :], start=(k == 0), stop=(k == 7)),
                        r=[("hT", hs), rk[k]], w=[("psF", pf)])
                es_ = nev % 4
                nev += 1
                if mode == "silu":
                    P.add("act", lambda e, pf=pf, es_=es_: e.activation(out=ev[es_][:], in_=psF[pf][:], func=AF.Silu),
                          r=[("psF", pf)], w=[("ev", es_)])
                elif mode == "q":
                    P.add("dve", lambda e, pf=pf, es_=es_: e.tensor_scalar(out=ev[es_][:], in0=psF[pf][:], scalar1=0.125,
                                                                          scalar2=None, op0=ALU.mult),
                          r=[("psF", pf)], w=[("ev", es_)])
                else:
                    P.add("dve", lambda e, pf=pf, es_=es_: e.tensor_copy(out=ev[es_][:], in_=psF[pf][:]),
                          r=[("psF", pf)], w=[("ev", es_)])
                if dst == "kk":
                    P.add("pool", lambda e, es_=es_, g=g: e.dma_start(out=T["kbT"][:, _sl(g, 512)], in_=ev[es_][0:64, :]),
                          r=[("ev", es_)], dma=True)
                    P.add("pool", lambda e, es_=es_, g=g: e.dma_start(out=T["kiT"][:, _sl(g, 512)], in_=ev[es_][64:128, :]),
                          r=[("ev", es_)], dma=True)
                else:
                    P.add("pool", lambda e, es_=es_, g=g, dst=dst, drow=drow: e.dma_start(
                        out=T[dst][drow:drow + 128, _sl(g, 512)], in_=ev[es_][:]),
                        r=[("ev", es_)], dma=True)
            for t4 in range(4):
                t = g * 4 + t4
                pf = nF % 4
                nF += 1
                pw = t % 2
                for k in range(8):
                    P.add("pe", lambda e, k=k, pf=pf, hs=hs, t4=t4: e.matmul(
                        psF[pf][:], lhsT=hT[hs][:, k, _sl(t4, 128)], rhs=wsb[:, k, 1024:1536], start=(k == 0), stop=(k == 7)),
                        r=[("hT", hs), wall[k]], w=[("psF", pf)])
                for k in range(8):
                    P.add("pe", lambda e, k=k, pw=pw, hs=hs, t4=t4: e.matmul(
                        psW[pw][:], lhsT=hT[hs][:, k, _sl(t4, 128)], rhs=wvw[:, k, :], start=(k == 0), stop=(k == 7)),
                        r=[("hT", hs), wvw_all[k]], w=[("psW", pw)])
                es_ = nev % 4
                nev += 1
                P.add("act", lambda e, pf=pf, es_=es_: e.copy(out=ev[es_][:], in_=psF[pf][:]),
                      r=[("psF", pf)], w=[("ev", es_)])
                P.add("pool", lambda e, es_=es_, t=t: e.dma_start(out=T["va"][_sl(t, 128), :], in_=ev[es_][:]),
                      r=[("ev", es_)], dma=True)
                P.add("dve", lambda e, pw=pw: e.tensor_copy(out=evb[pw][:], in_=psW[pw][:, 0:64]),
                      r=[("psW", pw)], w=[("evb", pw)])
                P.add("dve", lambda e, pw=pw: e.tensor_copy(out=evw[pw][:], in_=psW[pw][:, 64:72]),
                      r=[("psW", pw)], w=[("evw", pw)])
                P.add("pool", lambda e, pw=pw, t=t: e.dma_start(out=T["vb"][_sl(t, 128), :], in_=evb[pw][:]),
                      r=[("evb", pw)], dma=True)
                P.add("pool", lambda e, pw=pw, t=t: e.dma_start(out=T["wi"][_sl(t, 128), :], in_=evw[pw][:]),
                      r=[("evw", pw)], dma=True)
        P.emit()


HB = [0, 2, 4, 6, 1, 3, 5, 7]


def load_bias_tiles(P, nc, sb, T, pfx):
    gsb = sb(pfx + "gsb", [128, 2, 12, 128], F32)
    cm = sb(pfx + "cm", [128, 128], F32)
    c12 = sb(pfx + "c12", [128, 12], F32)
    bT = sb(pfx + "bT", [128, 2, 12, 128], BF16)
    tmp = sb(pfx + "btmp", [128, 128], F32)
    P.add("sp", lambda e: e.dma_start(out=gsb[:], in_=T["c_gath"]), w=["gsb"], dma=True)
    P.add("sp", lambda e: e.dma_start(out=cm[:], in_=T["c_mask"][:, :]), w=["cm"], dma=True)
    P.add("sp", lambda e: e.dma_start(out=c12[:], in_=T["rel_bias"][15, :].partition_broadcast(128)), w=["c12"], dma=True)
    for h in range(12):
        ho = h if h < 4 else 4 + HB.index(h - 4)
        P.add("dve", lambda e, h=h: e.tensor_scalar(out=tmp[:], in0=gsb[:, 0, h, :], scalar1=c12[:, h:h + 1], scalar2=None,
                                                    op0=ALU.subtract), r=["gsb", "c12"], w=["btmp"])
        P.add("dve", lambda e, h=h, ho=ho: e.tensor_tensor(out=bT[:, 0, ho, :], in0=tmp[:], in1=cm[:], op=ALU.add),
              r=["btmp", "cm"], w=["bT"])
        P.add("dve", lambda e, h=h, ho=ho: e.tensor_scalar(out=bT[:, 1, ho, :], in0=gsb[:, 1, h, :], scalar1=c12[:, h:h + 1],
                                                           scalar2=None, op0=ALU.subtract), r=["gsb", "c12"], w=["bT"])
    return bT


def phase_diff(P, nc, S, T):
    NT = S // 128
    NG = S // 512
    LAM_INIT = 0.8 - 0.6 * math.exp(-0.3 * 0)
    with ExitStack() as es:
        def sb(name, shape, dt):
            return es.enter_context(nc.sbuf_tensor(name, shape, dt))

        def ps(name, shape, dt):
            return es.enter_context(nc.psum_tensor(name, shape, dt))

        ident = sb("da_id", [128, 128], BF16)
        onesb = sb("da_1b", [128, 128], BF16)
        onesf = sb("da_1f", [128, 128], F32)
        bT = load_bias_tiles(P, nc, sb, T, "da_")
        lp = sb("da_lp", [128, 4, 64], F32)
        lpp = sb("da_lpp", [128, 2, 64], F32)
        ls = sb("da_ls", [128, 2], F32)
        le = sb("da_le", [128, 2], F32)
        nlam = sb("da_nlam", [128, 1], F32)
        gs = sb("da_gs", [128, 1], F32)
        ka = [sb("da_ka%d" % i, [128, S], BF16) for i in range(2)]
        va = [sb("da_va%d" % i, [128, NT, 128], BF16) for i in range(2)]
        qb = [sb("da_q%d" % i, [128, 2, 512], BF16) for i in range(2)]
        gb = [sb("da_g%d" % i, [128, 512], BF16) for i in range(2)]
        pT = [sb("da_pT%d" % i, [128, 512], BF16) for i in range(4)]
        rzt = sb("da_rz", [128, 512], F32)
        tm = [sb("da_tm%d" % i, [128, 512], F32) for i in range(2)]
        ot = sb("da_o", [128, 512], F32)
        sq = sb("da_sq", [128, 512], F32)
        rt = sb("da_rt", [128, 512], F32)
        mx = [sb("da_mx%d" % i, [128, 512], BF16) for i in range(2)]
        Sp = [ps("da_S%d" % i, [128, 512], F32) for i in range(3)]
        Op = [ps("da_O%d" % i, [128, 512], F32) for i in range(2)]
        Zp = [ps("da_Z%d" % i, [128, 512], F32) for i in range(2)]
        Rp = ps("da_R", [128, 512], F32)

        P.add("sp", lambda e: e.dma_start(out=ident[:], in_=T["c_ident"][:, :]), w=["ident"], dma=True)
        P.add("pool", lambda e: e.memset(onesb[:], 1.0), w=["onesb"])
        P.add("pool", lambda e: e.memset(onesf[:], 1.0), w=["onesf"])
        for i_ in range(2):
            P.add("pool", lambda e, i_=i_: e.memset(qb[i_][:], 0.0), w=[("qb", i_)])
        for n, nm in enumerate(["e_lam_q1", "e_lam_k1", "e_lam_q2", "e_lam_k2"]):
            P.add("sp", lambda e, n=n, nm=nm: e.dma_start(out=lp[:, n, :], in_=T[nm].partition_broadcast(128)),
                  w=[("lp", n)], dma=True)
        P.add("sp", lambda e: e.dma_start(out=gs[:], in_=T["e_subln"].rearrange("(p o) -> p o", o=1)), w=["gs"], dma=True)
        for n in range(2):
            P.add("dve", lambda e, n=n: e.tensor_tensor(out=lpp[:, n, :], in0=lp[:, 2 * n, :], in1=lp[:, 2 * n + 1, :], op=ALU.mult),
                  r=[("lp", 2 * n), ("lp", 2 * n + 1)], w=[("lpp", n)])
            P.add("dve", lambda e, n=n: e.reduce_sum(out=ls[:, n:n + 1], in_=lpp[:, n, :], axis=AX.X),
                  r=[("lpp", n)], w=[("ls", n)])
            P.add("act", lambda e, n=n: e.activation(out=le[:, n:n + 1], in_=ls[:, n:n + 1], func=AF.Exp),
                  r=[("ls", n)], w=[("le", n)])
        P.add("dve", lambda e: e.tensor_tensor(out=nlam[:], in0=le[:, 1:2], in1=le[:, 0:1], op=ALU.subtract),
              r=[("le", 0), ("le", 1)], w=["nlam"])
        P.add("dve", lambda e: e.tensor_scalar(out=nlam[:], in0=nlam[:], scalar1=-LAM_INIT, scalar2=None, op0=ALU.add),
              r=["nlam"], w=["nlam"])
        P.add("dve", lambda e: e.tensor_scalar(out=gs[:], in0=gs[:], scalar1=1.0 - LAM_INIT, scalar2=None, op0=ALU.mult),
              r=["gs"], w=["gs"])

        steps = []
        for h in range(4):
            for J in range(NG):
                for m in range(2):
                    last = 4 * J + 3
                    for i in range(last + 1):
                        steps.append((h, J, m, i, last))

        def loads(h, J, m, i):
            if J == 0 and m == 0 and i == 0:
                s1 = h % 2
                P.add("sp", lambda e: e.dma_start(out=ka[s1][:], in_=T["kaT"][_sl(h, 128), :]), w=[("ka", s1)], dma=True)
                for c8 in range(0, NT, 8):
                    n8 = min(8, NT - c8)
                    P.add("sp", lambda e, c8=c8, n8=n8: e.dma_start(
                        out=va[s1][:, c8:c8 + n8, :],
                        in_=T["va"][c8 * 128:(c8 + n8) * 128, _sl(h, 128)].rearrange("(t p) e -> p t e", p=128)),
                        w=[("va", s1, c8 // 8)], dma=True)
            if m == 0 and i == 0:
                s2 = (h * NG + J) % 2
                P.add("sp", lambda e: e.dma_start(out=qb[s2][0:64, 0, :], in_=T["qaT"][h * 128:h * 128 + 64, _sl(J, 512)]),
                      w=[("qb", s2)], dma=True)
                P.add("sp", lambda e: e.dma_start(out=qb[s2][64:128, 1, :], in_=T["qaT"][h * 128 + 64:h * 128 + 128, _sl(J, 512)]),
                      w=[("qb", s2)], dma=True)
                P.add("sp", lambda e: e.dma_start(out=gb[s2][:], in_=T["gT"][_sl(h, 128), _sl(J, 512)]), w=[("gb", s2)], dma=True)

        def qk(n):
            h, J, m, i, last = steps[n]
            loads(h, J, m, i)
            c0 = max(0, i - 4 * J) * 128
            sbk = n % 3
            qs = (h * NG + J) % 2
            adds = []
            if i >= 4 * J:
                adds.append((c0, 0))
            if i + 1 >= 4 * J and i + 1 <= last:
                adds.append(((i + 1 - 4 * J) * 128, 1))
            P.add("pe", lambda e: e.matmul(Sp[sbk][:, c0:512], lhsT=ka[h % 2][:, _sl(i, 128)],
                                           rhs=qb[qs][:, m, c0:512], start=True, stop=(len(adds) == 0)),
                  r=[("ka", h % 2), ("qb", qs)], w=[("Sp", sbk)])
            for a, (cs, r) in enumerate(adds):
                P.add("pe", lambda e, cs=cs, r=r, a=a: e.matmul(Sp[sbk][:, cs:cs + 128], lhsT=ident[:], rhs=bT[:, r, h, :],
                                                                start=False, stop=(a == len(adds) - 1)),
                      r=["ident", "bT"], w=[("Sp", sbk)])

        def pv(n):
            h, J, m, i, last = steps[n]
            c0 = max(0, i - 4 * J) * 128
            sbk = n % 3
            pt = n % 4
            osl = (2 * (h * NG + J) + m) % 2
            P.add("act", lambda e: e.activation(out=pT[pt][:, c0:512], in_=Sp[sbk][:, c0:512], func=AF.Exp),
                  r=[("Sp", sbk)], w=[("pT", pt)])
            P.add("pe", lambda e: e.matmul(Op[osl][:, c0:512], lhsT=va[h % 2][:, i, :], rhs=pT[pt][:, c0:512],
                                           start=(i == 0), stop=(i == last)),
                  r=[("va", h % 2, i // 8), ("pT", pt)], w=[("Op", osl)])
            P.add("pe", lambda e: e.matmul(Zp[osl][:, c0:512], lhsT=onesb[:], rhs=pT[pt][:, c0:512],
                                           start=(i == 0), stop=(i == last)),
                  r=["onesb", ("pT", pt)], w=[("Zp", osl)])
            if i != last:
                return
            P.add("dve", lambda e: e.reciprocal(out=rzt[:], in_=Zp[osl][:]), r=[("Zp", osl)], w=["rzt"])
            P.add("dve", lambda e: e.tensor_tensor(out=tm[m][:], in0=Op[osl][:], in1=rzt[:], op=ALU.mult),
                  r=[("Op", osl), "rzt"], w=[("tm", m)])
            if m == 0:
                return
            gsl = (h * NG + J) % 2
            P.add("dve", lambda e: e.scalar_tensor_tensor(out=ot[:], in0=tm[1][:], scalar=nlam[:, 0:1], in1=tm[0][:],
                                                          op0=ALU.mult, op1=ALU.add),
                  r=[("tm", 0), ("tm", 1), "nlam"], w=["ot"])
            P.add("act", lambda e: e.activation(out=sq[:], in_=ot[:], func=AF.Square), r=["ot"], w=["sq"])
            def epi_b():
                P.add("pe", lambda e: e.matmul(Rp[:], lhsT=onesf[:], rhs=sq[:], start=True, stop=True),
                      r=["onesf", "sq"], w=["Rp"])
                P.add("act", lambda e: e.activation(out=rt[:], in_=Rp[:], func=AF.Sqrt, bias=EPS, scale=1.0 / 128),
                      r=["Rp"], w=["rt"])
                P.add("dve", lambda e: e.reciprocal(out=rt[:], in_=rt[:]), r=["rt"], w=["rt"])
                P.add("dve", lambda e: e.scalar_tensor_tensor(out=ot[:], in0=ot[:], scalar=gs[:, 0:1], in1=rt[:],
                                                              op0=ALU.mult, op1=ALU.mult),
                      r=["ot", "gs", "rt"], w=["ot"])
                P.add("dve", lambda e: e.tensor_tensor(out=mx[gsl][:], in0=ot[:], in1=gb[gsl][:], op=ALU.mult),
                      r=["ot", ("gb", gsl)], w=[("mx", gsl)])
                P.add("pool", lambda e: e.dma_start(out=T["mixT"][_sl(h, 128), _sl(J, 512)], in_=mx[gsl][:]),
                      r=[("mx", gsl)], dma=True)
            deferred.append((n + 6, epi_b))

        deferred = []

        def run_deferred(n):
            while deferred and deferred[0][0] <= n:
                deferred.pop(0)[1]()

        LA = 2
        for n in range(min(LA, len(steps))):
            qk(n)
        for n in range(len(steps)):
            if n + LA < len(steps):
                qk(n + LA)
            run_deferred(n)
            pv(n)
        run_deferred(len(steps) + 10)
        P.emit()


NIT = 20


def phase_dsa(P, nc, S, T, topk):
    NT = S // 128
    with ExitStack() as es:
        def sb(name, shape, dt):
            return es.enter_context(nc.sbuf_tensor(name, shape, dt))

        def ps(name, shape, dt):
            return es.enter_context(nc.psum_tensor(name, shape, dt))

        ident4 = sb("ds_id4", [128, 4, 128], BF16)
        onesf = sb("ds_1f", [128, 64], F32)
        pw2 = sb("ds_pw2", [128, NIT], F32)
        ki2 = sb("ds_ki2", [128, S], BF16)
        kb2 = sb("ds_kb2", [128, S], BF16)
        vb1 = sb("ds_vb1", [128, NT, 66], BF16)
        wis = sb("ds_wi", [128, NT, 8], F32)
        qi = [sb("ds_qi%d" % i, [128, 8, 128], BF16) for i in range(2)]
        qb = [sb("ds_qb%d" % i, [128, 8, 128], BF16) for i in range(3)]
        junk2 = sb("ds_junk2", [128, S], BF16)
        gB = [sb("ds_gB%d" % i, [64, 8, 128], BF16) for i in range(4)]
        acc = [sb("ds_acc%d" % i, [128, S], F32) for i in range(2)]
        R = [sb("ds_R%d" % i, [128, 512], F32) for i in range(4)]
        NM = [sb("ds_NM%d" % i, [128, S], BF16) for i in range(2)]
        junk = sb("ds_junk", [128, S], BF16)
        pT = [sb("ds_pT%d" % i, [128, 512], BF16) for i in range(4)]
        sm = sb("ds_sm", [128, 8], F32)
        Wk = sb("ds_Wk", [128, NIT], F32)
        rz = sb("ds_rz", [128, 1024], F32)
        osb = [sb("ds_osb%d" % i, [64, 512], F32) for i in range(2)]
        on = sb("ds_on", [64, 512], F32)
        mxB = [sb("ds_mxB%d" % i, [64, 512], BF16) for i in range(2)]
        Zp = [ps("ds_Z%d" % i, [128, 512], F32) for i in range(3)]
        Sp = [ps("ds_S%d" % i, [128, 512], F32) for i in range(3)]
        Op = [ps("ds_O%d" % i, [128, 512], F32) for i in range(2)]
        P.add("pool", lambda e: e.memset(onesf[:], 1.0), w=["onesf"])
        for i_ in range(3):
            P.add("pool", lambda e, i_=i_: e.memset(qb[i_][:], 0.0), w=[("qb", i_)])
        for i_ in range(2):
            P.add("pool", lambda e, i_=i_: e.memset(qi[i_][:], 0.0), w=[("qi", i_)])
        for half in range(2):
            P.add("sp", lambda e, half=half: e.dma_start(out=ki2[_sl(half, 64), :], in_=T["kiT"][:, :]), w=["ki2"], dma=True)

        def wis_chunk(c8):
            n8 = min(8, NT - c8)
            P.add("sp", lambda e: e.dma_start(
                out=wis[:, c8:c8 + n8, :], in_=T["wi"][c8 * 128:(c8 + n8) * 128, :].rearrange("(t p) h -> p t h", p=128)),
                w=[("wis", c8 // 8)], dma=True)
        wis_chunk(0)
        late = {}

        def late_setup():
            for half in range(2):
                P.add("sp", lambda e, half=half: e.dma_start(out=kb2[_sl(half, 64), :], in_=T["kbT"][:, :]), w=["kb2"], dma=True)
            for a in range(4):
                P.add("sp", lambda e, a=a: e.dma_start(out=ident4[:, a, :], in_=T["c_ident"][:, :]), w=["ident4"], dma=True)
            late["bT"] = load_bias_tiles(P, nc, sb, T, "ds_")
            P.add("sp", lambda e: e.dma_start(out=pw2[:], in_=T["c_pow2"].partition_broadcast(128)), w=["pw2"], dma=True)
            P.add("pool", lambda e: e.memset(vb1[:, :, 64:66], 1.0), w=["vb1"])
            for c8 in range(0, NT, 8):
                n8 = min(8, NT - c8)
                P.add("sp", lambda e, c8=c8, n8=n8: e.dma_start(
                    out=vb1[:, c8:c8 + n8, 0:64], in_=T["vb"][c8 * 128:(c8 + n8) * 128, :].rearrange("(t p) d -> p t d", p=128)),
                    w=["vb1"], dma=True)
                if c8 > 0:
                    wis_chunk(c8)

        cnt = dict(z=0, r=0, s=0, p=0)

        def indexer_items(j):
            s = j % 2
            s3 = j % 3
            nk = (j + 1) * 128
            nch = (nk + 511) // 512

            def u_load():
                P.add("sp", lambda e: e.dma_start(out=qi[s][0:64, 0:8:2, :],
                                                  in_=T["qiT"].rearrange("(c p) s -> p c s", p=128)[0:64, :, _sl(j, 128)]),
                      w=[("qi", s)], dma=True)
                P.add("sp", lambda e: e.dma_start(out=qi[s][64:128, 1:8:2, :],
                                                  in_=T["qiT"].rearrange("(c p) s -> p c s", p=128)[64:128, :, _sl(j, 128)]),
                      w=[("qi", s)], dma=True)
                P.add("sp", lambda e: e.dma_start(out=qb[s3][0:64, 0:8:2, :],
                                                  in_=T["qbT"].rearrange("(c p) s -> p c s", p=128)[0:64, :, _sl(j, 128)]),
                      w=[("qb", s3)], dma=True)
                P.add("sp", lambda e: e.dma_start(out=qb[s3][64:128, 1:8:2, :],
                                                  in_=T["qbT"].rearrange("(c p) s -> p c s", p=128)[64:128, :, _sl(j, 128)]),
                      w=[("qb", s3)], dma=True)
                P.add("sp", lambda e: e.dma_start(out=gB[j % 4][:], in_=T["gT"][512:1024, _sl(j, 128)].rearrange("(h d) q -> d h q", d=64)),
                      w=[("gB", j % 4)], dma=True)

            def mk(h, kc):
                st = {}
                N = min(512, nk - kc * 512)
                k0 = kc * 512

                def pe():
                    zs = cnt["z"] % 3
                    cnt["z"] += 1
                    st["zs"] = zs
                    P.add("pe", lambda e: e.matmul(Zp[zs][:, 0:N], lhsT=qi[s][:, h, :], rhs=ki2[:, k0:k0 + N],
                                                   start=True, stop=True),
                          r=[("qi", s), "ki2"], w=[("Zp", zs)])

                def rest():
                    zs = st["zs"]
                    rs_ = cnt["r"] % 4
                    cnt["r"] += 1
                    P.add("act", lambda e: e.activation(out=R[rs_][:, 0:N], in_=Zp[zs][:, 0:N], func=AF.Relu),
                          r=[("Zp", zs)], w=[("R", rs_)])
                    if h == 0:
                        P.add("pool", lambda e: e.tensor_scalar(out=acc[s][:, k0:k0 + N], in0=R[rs_][:, 0:N], scalar1=wis[:, j, 0:1],
                                                               scalar2=None, op0=ALU.mult),
                              r=[("R", rs_), ("wis", j // 8)], w=[("acc", s)])
                    else:
                        P.add("dve", lambda e: e.scalar_tensor_tensor(out=acc[s][:, k0:k0 + N], in0=R[rs_][:, 0:N],
                                                                      scalar=wis[:, j, h:h + 1], in1=acc[s][:, k0:k0 + N],
                                                                      op0=ALU.mult, op1=ALU.add),
                              r=[("R", rs_), ("wis", j // 8), ("acc", s)], w=[("acc", s)])
                    if h == 7 and kc == nch - 1:
                        P.add("dve", lambda e: e.memset(acc[s][0:64, nk - 64:nk], -1e30), w=[("acc", s)])
                return (pe, rest)
            items = [mk(h, kc) for h in range(8) for kc in range(nch)]
            return u_load, items

        def bisect_units(j):
            s = j % 2
            nk = (j + 1) * 128
            A = [("acc", s)]
            units = []

            def u_nm():
                P.add("dve", lambda e: e.tensor_scalar(out=NM[s][:, 0:nk], in0=acc[s][:, 0:nk], scalar1=sm[:, 1:2], scalar2=NEG,
                                                       op0=ALU.is_lt, op1=ALU.mult),
                      r=A + ["lo"], w=[("NM", s)])
                if "dbg_nm" in T:
                    P.add("pool", lambda e: e.dma_start(out=T["dbg_nm"][_sl(j, 128), 0:nk], in_=NM[s][:, 0:nk]), r=[("NM", s)], dma=True)
                    P.add("pool", lambda e: e.dma_start(out=T["dbg_acc"][_sl(j, 128), 0:nk], in_=acc[s][:, 0:nk]), r=[("acc", s)], dma=True)

            if nk <= topk:
                def u0():
                    P.add("dve", lambda e: e.memset(sm[:, 1:2], -1e29), w=["lo"])
                    u_nm()
                return [u0]
            assert nk - 64 >= topk
            h1 = max(64, int(round(0.42 * nk / 64.0)) * 64)
            n2 = nk - h1

            def u_init():
                P.add("dve", lambda e: e.reduce_max(out=sm[:, 0:1], in_=acc[s][:, 0:nk], axis=AX.X), r=A, w=["hi"])
                P.add("dve", lambda e: e.tensor_reduce(out=sm[:, 1:2], in_=acc[s][:, 0:nk - 64], axis=AX.X, op=ALU.min),
                      r=A, w=["lo"])
                P.add("dve", lambda e: e.tensor_tensor(out=sm[:, 2:3], in0=sm[:, 0:1], in1=sm[:, 1:2], op=ALU.subtract),
                      r=["hi", "lo"], w=["w0"])
                P.add("dve", lambda e: e.tensor_scalar(out=Wk[:], in0=pw2[:], scalar1=sm[:, 2:3], scalar2=None, op0=ALU.mult),
                      r=["pw2", "w0"], w=["Wk"])
                P.add("dve", lambda e: e.tensor_tensor(out=sm[:, 3:4], in0=sm[:, 1:2], in1=Wk[:, 0:1], op=ALU.add),
                      r=["lo", "Wk"], w=["mid"])
            units.append(u_init)

            def mk(k):
                def u():
                    kk_ = k + 1 if k + 1 < NIT else k
                    P.add("dve", lambda e: e.tensor_scalar(out=junk[:, 0:h1], in0=acc[s][:, 0:h1], scalar1=sm[:, 3:4],
                                                           scalar2=-(topk - 0.5 - n2 / 2.0),
                                                           op0=ALU.is_ge, op1=ALU.add, accum_out=sm[:, 4:5]),
                          r=A + ["mid"], w=["junk", "cnt"])
                    P.add("act", lambda e: e.activation(out=junk2[:, 0:n2], in_=acc[s][:, h1:nk], func=AF.Sign, scale=-1.0,
                                                        bias=sm[:, 3:4], accum_out=sm[:, 6:7]),
                          r=A + ["mid"], w=["junk2", "sgn"])
                    P.add("dve", lambda e: e.tensor_tensor(out=sm[:, 7:8], in0=sm[:, 3:4], in1=Wk[:, kk_:kk_ + 1], op=ALU.subtract),
                          r=["mid", "Wk"], w=["m2"])
                    P.add("dve", lambda e: e.scalar_tensor_tensor(out=sm[:, 5:6], in0=sm[:, 4:5], scalar=2.0, in1=sm[:, 6:7],
                                                                  op0=ALU.mult, op1=ALU.is_ge),
                          r=["cnt", "sgn"], w=["pw"])
                    dst = (3, "mid") if k + 1 < NIT else (1, "lo")
                    P.add("dve", lambda e: e.scalar_tensor_tensor(out=sm[:, dst[0]:dst[0] + 1], in0=sm[:, 5:6], scalar=Wk[:, k:k + 1],
                                                                  in1=sm[:, 7:8], op0=ALU.mult, op1=ALU.add),
                          r=["pw", "Wk", "m2"], w=[dst[1]])
                    if k + 1 >= NIT:
                        u_nm()
                return u
            for k in range(NIT):
                units.append(mk(k))
            return units

        def att_qk(j, i, b):
            s = j % 2
            s3 = j % 3
            ss = cnt["s"] % 3
            cnt["s"] += 1
            near = i >= j - 1
            P.add("pe", lambda e: e.matmul(Sp[ss][:], lhsT=NM[s][:, _sl(i, 128)],
                                           rhs=ident4[:].rearrange("p a q -> p (a q)"), start=True, stop=False),
                  r=[("NM", s), "ident4"], w=[("Sp", ss)])
            for hh in range(4):
                h = HB[b * 4 + hh]
                c, half = h // 2, h % 2
                P.add("pe", lambda e, hh=hh, c=c, half=half: e.matmul(
                    Sp[ss][:, _sl(hh, 128)], lhsT=kb2[:, _sl(i, 128)], rhs=qb[s3][:, 2 * c + half, :],
                    start=False, stop=(hh == 3 and not near)),
                    r=["kb2", ("qb", s3)], w=[("Sp", ss)])
            if near:
                r_ = j - i
                P.add("pe", lambda e: e.matmul(
                    Sp[ss][:], lhsT=ident4[:, 0, :], rhs=late["bT"][:, r_, 4 + 4 * b:8 + 4 * b, :].rearrange("p h q -> p (h q)"),
                    start=False, stop=True),
                    r=["ident4", "bT"], w=[("Sp", ss)])
            return ss

        def att_pv(j, i, b, ss):
            pp = cnt["p"] % 4
            cnt["p"] += 1
            P.add("act", lambda e: e.activation(out=pT[pp][:], in_=Sp[ss][:], func=AF.Exp),
                  r=[("Sp", ss)], w=[("pT", pp)])
            P.add("pe", lambda e: e.matmul(Op[b][0:65, :], lhsT=vb1[:, i, 0:65], rhs=pT[pp][:],
                                           start=(i == 0), stop=(i == j)),
                  r=["vb1", ("pT", pp)], w=[("Op", b)])

        def att_epi_a(j, b):
            P.add("dve", lambda e: e.reciprocal(out=rz[64:65, _sl(b, 512)], in_=Op[b][64:65, :]), r=[("Op", b)], w=[("rz", b)])
            P.add("act", lambda e: e.copy(out=osb[b][:], in_=Op[b][0:64, :]), r=[("Op", b)], w=[("osb", b)])

        def att_epi_b(j, b):
            s = j % 4
            zs = cnt["z"] % 3
            cnt["z"] += 1
            BC = Zp[zs]
            P.add("pe", lambda e: e.matmul(BC[0:64, :], lhsT=onesf[64:65, :], rhs=rz[64:65, _sl(b, 512)], start=True, stop=True),
                  r=["onesf", ("rz", b)], w=[("Zp", zs)])
            P.add("dve", lambda e: e.tensor_tensor(out=on[:], in0=osb[b][:], in1=BC[0:64, :], op=ALU.mult),
                  r=[("osb", b), ("Zp", zs)], w=["on"])
            P.add("dve", lambda e: e.tensor_tensor(out=mxB[b][:].rearrange("p (h q) -> p h q", h=4),
                                                   in0=on[:].rearrange("p (h q) -> p h q", h=4),
                                                   in1=gB[s][:, b:8:2, :], op=ALU.mult),
                  r=["on", ("gB", s)], w=[("mxB", b)])
            P.add("pool", lambda e: e.dma_start(
                out=T["mixT"][512:1024, _sl(j, 128)].rearrange("(h d) q -> d h q", d=64)[:, b:8:2, :],
                in_=mxB[b][:].rearrange("p (h q) -> p h q", h=4)),
                r=[("mxB", b)], dma=True)

        LA_ = 2
        pending = []

        def run_slot(t):
            Bt = bisect_units(t) if 0 <= t < NT else []
            ja = t - 1 if 0 <= t - 1 < NT else None
            steps_ = [(i, b) for i in range(ja + 1) for b in range(2)] if ja is not None else []
            info = {}
            nA = len(steps_)
            u_load, items = indexer_items(t + 1) if 0 <= t + 1 < NT else (None, [])
            nI = len(items)
            nr = max(len(Bt), 1)
            if u_load is not None:
                u_load()
            qk_done = [0]

            def qk_upto(m):
                while qk_done[0] < min(nA, m):
                    n_ = qk_done[0]
                    info[n_] = att_qk(ja, *steps_[n_])
                    qk_done[0] += 1

            for r in range(nr):
                if r < len(Bt):
                    Bt[r]()
                il, ih = r * nI // nr, (r + 1) * nI // nr
                al, ah = r * nA // nr, (r + 1) * nA // nr
                if pending and r == min(2, nr - 1):
                    pending.pop(0)()
                first = True
                x = il
                while first or x < ih:
                    batch = items[x:min(ih, x + 3)]
                    x += 3
                    for pe_, _ in batch:
                        pe_()
                    if first:
                        for n_ in range(al, ah):
                            qk_upto(n_ + LA_ + 1)
                            att_pv(ja, steps_[n_][0], steps_[n_][1], info[n_])
                        first = False
                    for _, rest_ in batch:
                        rest_()
            if ja is not None:
                att_epi_a(ja, 0)
                att_epi_a(ja, 1)
                pending.append(lambda: (att_epi_b(ja, 0), att_epi_b(ja, 1)))

        for t in range(-1, NT + 1):
            run_slot(t)
            if t == -1:
                late_setup()
        while pending:
            pending.pop(0)()
        P.emit()


def phase_outproj(P, nc, S, T, mix, wname, xin, xout, final):
    NT = S // 128
    NG = S // 512
    pfx = "o%d_" % (1 if final else 0)
    with ExitStack() as es:
        def sb(name, shape, dt):
            return es.enter_context(nc.sbuf_tensor(pfx + name, shape, dt))

        def ps(name, shape, dt):
            return es.enter_context(nc.psum_tensor(pfx + name, shape, dt))

        wo = sb("wo", [128, 8, D], BF16)
        wst = [sb("wst%d" % i, [128, D], F32) for i in range(4)]
        mt = [sb("mt%d" % i, [128, 8, 512], BF16) for i in range(2)]
        xt = [sb("xt%d" % i, [128, D], F32) for i in range(2)]
        xo = [sb("xo%d" % i, [128, D], F32) for i in range(2)]
        acc = [ps("acc%d" % i, [128, 512], F32) for i in range(4)]
        if final:
            gbc = sb("gbc", [128, D], F32)
            junk = sb("junk", [128, D], BF16)
            ssq = sb("ssq", [128, NT], F32)
            P.add("sp", lambda e: e.dma_start(out=gbc[:], in_=T["final_norm"].partition_broadcast(128)), w=["gbc"], dma=True)
        for k in range(8):
            s = k % 4
            P.add("sp", lambda e, k=k, s=s: e.dma_start(out=wst[s][:], in_=T[wname][_sl(k, 128), :]), w=[("wst", s)], dma=True)
            P.add("act", lambda e, k=k, s=s: e.copy(out=wo[:, k, :], in_=wst[s][:]), r=[("wst", s)], w=[("wo", k)])
        wall = [("wo", k) for k in range(8)]
        na = 0
        for g in range(NG):
            ms = g % 2
            P.add("sp", lambda e, g=g, ms=ms: e.dma_start(
                out=mt[ms][:], in_=T[mix].rearrange("(c p) s -> p c s", p=128)[:, :, _sl(g, 512)]), w=[("mt", ms)], dma=True)
            for t4 in range(4):
                t = g * 4 + t4
                xs = t % 2
                P.add("sp", lambda e, t=t, xs=xs: e.dma_start(out=xt[xs][:], in_=T[xin][_sl(t, 128), :]), w=[("xt", xs)], dma=True)
                for nh in range(2):
                    a = na % 4
                    na += 1
                    for k in range(8):
                        P.add("pe", lambda e, k=k, a=a, ms=ms, t4=t4, nh=nh: e.matmul(
                            acc[a][:], lhsT=mt[ms][:, k, _sl(t4, 128)], rhs=wo[:, k, _sl(nh, 512)], start=(k == 0), stop=(k == 7)),
                            r=[("mt", ms)] + wall, w=[("acc", a)])
                    P.add("dve", lambda e, a=a, xs=xs, nh=nh: e.tensor_tensor(
                        out=xo[xs][:, _sl(nh, 512)], in0=acc[a][:], in1=xt[xs][:, _sl(nh, 512)], op=ALU.add),
                        r=[("acc", a), ("xt", xs)], w=[("xo", xs, nh)])
                X = [("xo", xs, 0), ("xo", xs, 1)]
                if not final:
                    P.add("pool", lambda e, t=t, xs=xs: e.dma_start(out=T[xout][_sl(t, 128), :], in_=xo[xs][:]),
                          r=X, dma=True, grp=("xo", xs))
                else:
                    P.add("act", lambda e, t=t, xs=xs: e.activation(out=junk[:], in_=xo[xs][:], func=AF.Square,
                                                                    accum_out=ssq[:, t:t + 1]), r=X, w=["junk", ("ssq", t)])
                    P.add("act", lambda e, t=t: e.activation(out=ssq[:, t:t + 1], in_=ssq[:, t:t + 1], func=AF.Sqrt,
                                                             bias=EPS, scale=1.0 / D), r=[("ssq", t)], w=[("ssq", t)])
                    P.add("dve", lambda e, t=t: e.reciprocal(out=ssq[:, t:t + 1], in_=ssq[:, t:t + 1]),
                          r=[("ssq", t)], w=[("ssq", t)])
                    P.add("dve", lambda e, t=t, xs=xs: e.scalar_tensor_tensor(
                        out=xo[xs][:], in0=xo[xs][:], scalar=ssq[:, t:t + 1], in1=gbc[:], op0=ALU.mult, op1=ALU.mult),
                        r=X + [("ssq", t), "gbc"], w=X)
                    P.add("pool", lambda e, t=t, xs=xs: e.dma_start(out=T[xout][_sl(t, 128), :], in_=xo[xs][:]),
                          r=X, dma=True, grp=("xo", xs))
        P.emit()


def phase_proj1(P, nc, S, T):
    NT = S // 128
    NG = S // 512
    with ExitStack() as es:
        def sb(name, shape, dt):
            return es.enter_context(nc.sbuf_tensor("p1_" + name, shape, dt))

        def ps(name, shape, dt):
            return es.enter_context(nc.psum_tensor("p1_" + name, shape, dt))

        w1 = sb("w", [128, 8, L1_COLS], BF16)
        wrot = sb("wrot", [128, 8, 32], BF16)
        wqn = sb("wqn", [128, 3, 8, 128], BF16)
        wqr = sb("wqr", [128, 3, 4, 128], BF16)
        wqrr = sb("wqrr", [128, 3, 4, 128], BF16)
        wkk = sb("wkk", [128, 2, 16, 64], BF16)
        wkv = sb("wkv", [128, 2, 16, 64], BF16)
        wst = [sb("wst%d" % i, [128, 2048], F32) for i in range(2)]
        gcol = sb("g", [128, 8], F32)
        qn = sb("qn", [128, 3], F32)
        kvn = sb("kvn", [128, 2], F32)
        cs = sb("cos", [128, S], F32)
        sn = sb("sin", [128, S], F32)
        onesf = sb("1f", [128, 128], F32)
        ident = sb("id", [128, 128], BF16)
        xt = [sb("x%d" % i, [128, D], F32) for i in range(2)]
        hb = [sb("h%d" % i, [128, D], BF16) for i in range(2)]
        hT = [sb("hT%d" % i, [128, 8, 512], BF16) for i in range(2)]
        junk = sb("junk", [128, D], BF16)
        ssq = sb("ssq", [128, NT], F32)
        rs = sb("rs", [128, NT], F32)
        qlT = sb("qlT", [128, 3, 512], BF16)
        kvT = sb("kvT", [128, 2, 512], BF16)
        sqq = [sb("sqq%d" % i, [128, 512], F32) for i in range(2)]
        sqkv = sb("sqkv", [128, 2, 512], F32)
        rq = sb("rq", [128, 512], F32)
        rkv = sb("rkv", [128, 512], F32)
        rkt = sb("rkt", [128, 4], F32)
        ra2 = [sb("ra%d" % i, [128, 512], F32) for i in range(2)]
        rb2 = [sb("rb%d" % i, [128, 512], F32) for i in range(2)]
        ev = [sb("ev%d" % i, [128, 512], BF16) for i in range(4)]
        psT = [ps("psT%d" % i, [128, D], BF16) for i in range(2)]
        psF = [ps("psF%d" % i, [128, 512], F32) for i in range(3)]
        Rq = ps("Rq", [128, 512], F32)
        Rkv = ps("Rkv", [128, 512], F32)
        Rt = ps("Rt", [128, 4], F32)

        def ld(dst, src, key, **kw):
            P.add("sp", lambda e: e.dma_start(out=dst, in_=src, **kw), w=[key], dma=True)

        ld(ident[:], T["c_ident"][:, :], "ident")
        P.add("pool", lambda e: e.memset(onesf[:], 1.0), w=["onesf"])
        ld(gcol[:], T["o_norm"].rearrange("(k p) -> p k", p=128), "gcol", allow_slow_non_contiguous=True)
        ld(qn[:], T["o_q_norm"].rearrange("(k p) -> p k", p=128), "qn", allow_slow_non_contiguous=True)
        ld(kvn[:], T["o_kv_norm"].rearrange("(k p) -> p k", p=128), "kvn", allow_slow_non_contiguous=True)
        nst = [0]

        def stage(src, ncols, fn_list, rkeys):
            s_ = nst[0] % 2
            nst[0] += 1
            P.add("sp", lambda e: e.dma_start(out=wst[s_][:, 0:ncols], in_=src), w=[("wst", s_)], dma=True)
            for (q, fn, wk) in fn_list:
                P.add(q, (lambda e, fn=fn: fn(e, wst[s_])), r=[("wst", s_)] + rkeys, w=[wk])

        done_tiles = set()

        def tile_ops(t):
            if t in done_tiles:
                return
            done_tiles.add(t)
            xs = t % 2
            hs = (t // 4) % 2
            t4 = t % 4
            P.add("sp", lambda e: e.dma_start(out=xt[xs][:], in_=T["x1"][_sl(t, 128), :]), w=[("xt", xs)], dma=True)
            P.add("act", lambda e: e.activation(out=junk[:], in_=xt[xs][:], func=AF.Square, accum_out=ssq[:, t:t + 1]),
                  r=[("xt", xs)], w=["junk", ("ssq", t)])
            P.add("act", lambda e: e.activation(out=rs[:, t:t + 1], in_=ssq[:, t:t + 1], func=AF.Sqrt, bias=EPS, scale=1.0 / D),
                  r=[("ssq", t)], w=[("rs", t)])
            P.add("dve", lambda e: e.reciprocal(out=rs[:, t:t + 1], in_=rs[:, t:t + 1]), r=[("rs", t)], w=[("rs", t)])
            P.add("dve", lambda e: e.tensor_scalar(out=hb[xs][:], in0=xt[xs][:], scalar1=rs[:, t:t + 1], scalar2=None, op0=ALU.mult),
                  r=[("xt", xs), ("rs", t)], w=[("hb", xs)])
            for k in range(8):
                P.add("pe", lambda e, k=k: e.transpose(out=psT[xs][:, _sl(k, 128)], in_=hb[xs][:, _sl(k, 128)], identity=ident[:]),
                      r=[("hb", xs), "ident"], w=[("psT", xs)])
            P.add("act", lambda e: e.copy(out=hT[hs][:, :, _sl(t4, 128)], in_=psT[xs][:].rearrange("p (k t) -> p k t", k=8)),
                  r=[("psT", xs)], w=[("hT", hs)])

        tile_ops(0)
        tile_ops(1)
        for k in range(8):
            stage(T["o_w_in"][_sl(k, 128), :], L1_COLS, [
                ("dve", lambda e, st, k=k: e.tensor_scalar(out=w1[:, k, :], in0=st[:, 0:L1_COLS], scalar1=gcol[:, k:k + 1],
                                                          scalar2=None, op0=ALU.mult), ("w1", k)),
                ("dve", lambda e, st, k=k: e.tensor_scalar(out=wrot[:, k, 0:16], in0=st[:, 656:672], scalar1=gcol[:, k:k + 1],
                                                          scalar2=-1.0, op0=ALU.mult, op1=ALU.mult), ("wrot", k)),
                ("dve", lambda e, st, k=k: e.tensor_scalar(out=wrot[:, k, 16:32], in0=st[:, 640:656], scalar1=gcol[:, k:k + 1],
                                                          scalar2=None, op0=ALU.mult), ("wrot", k)),
            ], ["gcol"])
        for c in range(3):
            def v3(st):
                return st[:, 0:1536].rearrange("p (h j) -> p h j", j=96)
            stage(T["o_w_uq"][_sl(c, 128), :], 1536, [
                ("dve", lambda e, st, c=c: e.tensor_scalar(out=wqn[:, c, :, :].rearrange("p a (b j) -> p (a b) j", j=64),
                                                          in0=v3(st)[:, :, 0:64], scalar1=qn[:, c:c + 1],
                                                          scalar2=None, op0=ALU.mult), ("wqn", c)),
                ("dve", lambda e, st, c=c: e.tensor_scalar(out=wqr[:, c, :, :].rearrange("p a (b j) -> p (a b) j", j=32),
                                                          in0=v3(st)[:, :, 64:96], scalar1=qn[:, c:c + 1],
                                                          scalar2=None, op0=ALU.mult), ("wqr", c)),
                ("dve", lambda e, st, c=c: e.tensor_scalar(out=wqrr[:, c, :, :].rearrange("p a (b j) -> p (a b) j", j=32)[:, :, 0:16],
                                                          in0=v3(st)[:, :, 80:96], scalar1=qn[:, c:c + 1],
                                                          scalar2=-1.0, op0=ALU.mult, op1=ALU.mult), ("wqrr", c)),
                ("dve", lambda e, st, c=c: e.tensor_scalar(out=wqrr[:, c, :, :].rearrange("p a (b j) -> p (a b) j", j=32)[:, :, 16:32],
                                                          in0=v3(st)[:, :, 64:80], scalar1=qn[:, c:c + 1],
                                                          scalar2=None, op0=ALU.mult), ("wqrr", c)),
            ], ["qn"])
        for c in range(2):
            def v4(st):
                return st[:, 0:2048].rearrange("p (h j) -> p h j", j=128)
            stage(T["o_w_ukv"][_sl(c, 128), :], 2048, [
                ("dve", lambda e, st, c=c: e.tensor_scalar(out=wkk[:, c, :, :], in0=v4(st)[:, :, 0:64], scalar1=kvn[:, c:c + 1],
                                                          scalar2=None, op0=ALU.mult), ("wkk", c)),
                ("dve", lambda e, st, c=c: e.tensor_scalar(out=wkv[:, c, :, :], in0=v4(st)[:, :, 64:128], scalar1=kvn[:, c:c + 1],
                                                          scalar2=None, op0=ALU.mult), ("wkv", c)),
            ], ["kvn"])
        for i_ in range(4):
            ld(cs[32 * i_:32 * i_ + 32, :], T["c_cos"][:, :], "cos")
            ld(sn[32 * i_:32 * i_ + 32, :], T["c_sin"][:, :], "sin")
        W1 = [("w1", k) for k in range(8)]
        WROT = [("wrot", k) for k in range(8)]
        WQN = [("wqn", c) for c in range(3)]
        WQR = [("wqr", c) for c in range(3)]
        WQRR = [("wqrr", c) for c in range(3)]
        WKK = [("wkk", c) for c in range(2)]
        WKV = [("wkv", c) for c in range(2)]
        cnt = dict(f=0, e=0, q=0)

        def nf():
            cnt["f"] += 1
            return (cnt["f"] - 1) % 3

        def ne():
            cnt["e"] += 1
            return (cnt["e"] - 1) % 4

        def mm8(pf, M, lhs_fn, rkeys, hs):
            for k in range(8):
                P.add("pe", lambda e, k=k: e.matmul(psF[pf][0:M, :], lhsT=lhs_fn(k), rhs=hT[hs][:, k, :], start=(k == 0), stop=(k == 7)),
                      r=[("hT", hs), rkeys[k]], w=[("psF", pf)])

        def store(q_tile, dst):
            P.add("pool", lambda e: e.dma_start(out=dst, in_=q_tile[0]), r=[q_tile[1]], dma=True, grp=q_tile[1])

        def group(g):
            hs = g % 2
            G = slice(g * 512, (g + 1) * 512)
            for t4 in range(4):
                tile_ops(g * 4 + t4)
            for c in range(3):
                def qlat(c=c):
                    pf = nf()
                    mm8(pf, 128, lambda k: w1[:, k, c * 128:(c + 1) * 128], W1, hs)
                    sq_ = cnt["q"] % 2
                    cnt["q"] += 1
                    P.add("act", lambda e: e.copy(out=qlT[:, c, :], in_=psF[pf][:]), r=[("psF", pf)], w=[("qlT", c)])
                    P.add("act", lambda e: e.activation(out=sqq[sq_][:], in_=psF[pf][:], func=AF.Square), r=[("psF", pf)], w=[("sqq", sq_)])
                    P.add("pe", lambda e: e.matmul(Rq[:], lhsT=onesf[:], rhs=sqq[sq_][:], start=(c == 0), stop=(c == 2)),
                          r=["onesf", ("sqq", sq_)], w=["Rq"])
                qlat()
            P.add("act", lambda e: e.activation(out=rq[:], in_=Rq[:], func=AF.Sqrt, bias=EPS, scale=1.0 / 384), r=["Rq"], w=["rq"])
            P.add("dve", lambda e: e.reciprocal(out=rq[:], in_=rq[:]), r=["rq"], w=["rq"])
            for c in range(2):
                def kvlat(c=c):
                    pf = nf()
                    mm8(pf, 128, lambda k: w1[:, k, 384 + c * 128:384 + (c + 1) * 128], W1, hs)
                    P.add("act", lambda e: e.copy(out=kvT[:, c, :], in_=psF[pf][:]), r=[("psF", pf)], w=[("kvT", c)])
                    P.add("act", lambda e: e.activation(out=sqkv[:, c, :], in_=psF[pf][:], func=AF.Square), r=[("psF", pf)], w=[("sqkv", c)])
                    P.add("pe", lambda e: e.matmul(Rkv[:], lhsT=onesf[:], rhs=sqkv[:, c, :], start=(c == 0), stop=(c == 1)),
                          r=["onesf", ("sqkv", c)], w=["Rkv"])
                kvlat()
            for t4 in range(4):
                for c in range(2):
                    P.add("pe", lambda e, t4=t4, c=c: e.matmul(Rt[:, t4:t4 + 1], lhsT=sqkv[:, c, _sl(t4, 128)], rhs=onesf[:, 0:1],
                                                               start=(c == 0), stop=(c == 1)),
                          r=["onesf", ("sqkv", 0), ("sqkv", 1)], w=["Rt"])
            P.add("act", lambda e: e.activation(out=rkv[:], in_=Rkv[:], func=AF.Sqrt, bias=EPS, scale=1.0 / 256), r=["Rkv"], w=["rkv"])
            P.add("dve", lambda e: e.reciprocal(out=rkv[:], in_=rkv[:]), r=["rkv"], w=["rkv"])
            P.add("act", lambda e: e.activation(out=rkt[:], in_=Rt[:], func=AF.Sqrt, bias=EPS, scale=1.0 / 256), r=["Rt"], w=["rkt"])
            P.add("dve", lambda e: e.reciprocal(out=rkt[:], in_=rkt[:]), r=["rkt"], w=["rkt"])
            def krope():
                ra, rb = ra2[0], rb2[0]
                pa, pb = nf(), nf()
                mm8(pa, 32, lambda k: w1[:, k, 640:672], W1, hs)
                mm8(pb, 32, lambda k: wrot[:, k, :], WROT, hs)
                e_ = ne()
                P.add("dve", lambda e: e.tensor_tensor(out=ra[0:32, :], in0=psF[pa][0:32, :], in1=cs[0:32, G], op=ALU.mult),
                      r=[("psF", pa), "cos"], w=[("ra", 0)])
                P.add("dve", lambda e: e.tensor_tensor(out=rb[0:32, :], in0=psF[pb][0:32, :], in1=sn[0:32, G], op=ALU.mult),
                      r=[("psF", pb), "sin"], w=[("rb", 0)])
                P.add("dve", lambda e: e.tensor_tensor(out=ev[e_][0:32, :], in0=ra[0:32, :], in1=rb[0:32, :], op=ALU.add),
                      r=[("ra", 0), ("rb", 0)], w=[("ev", e_)])
                store((ev[e_][0:32, :], ("ev", e_)), T["kTr"][:, G])
            krope()
            for c in range(8):
                def gate(c=c):
                    pf = nf()
                    mm8(pf, 128, lambda k: w1[:, k, 672 + c * 128:672 + (c + 1) * 128], W1, hs)
                    e_ = ne()
                    P.add("act", lambda e: e.activation(out=ev[e_][:], in_=psF[pf][:], func=AF.Silu), r=[("psF", pf)], w=[("ev", e_)])
                    store((ev[e_][:], ("ev", e_)), T["g1T"][_sl(c, 128), G])
                gate()
            QL = [("qlT", c) for c in range(3)]
            KV = [("kvT", c) for c in range(2)]
            for qd in range(4):
                def qrope(qd=qd):
                    ra, rb = ra2[qd % 2], rb2[qd % 2]
                    rak, rbk = ("ra", qd % 2), ("rb", qd % 2)
                    pa, pb = nf(), nf()
                    for c in range(3):
                        P.add("pe", lambda e, c=c: e.matmul(psF[pa][:], lhsT=wqr[:, c, qd, :], rhs=qlT[:, c, :], start=(c == 0), stop=(c == 2)),
                              r=QL + WQR, w=[("psF", pa)])
                    for c in range(3):
                        P.add("pe", lambda e, c=c: e.matmul(psF[pb][:], lhsT=wqrr[:, c, qd, :], rhs=qlT[:, c, :], start=(c == 0), stop=(c == 2)),
                              r=QL + WQRR, w=[("psF", pb)])
                    e_ = ne()
                    P.add("dve", lambda e: e.tensor_tensor(out=ra[:], in0=psF[pa][:], in1=cs[:, G], op=ALU.mult),
                          r=[("psF", pa), "cos"], w=[rak])
                    P.add("dve", lambda e: e.tensor_tensor(out=rb[:], in0=psF[pb][:], in1=sn[:, G], op=ALU.mult),
                          r=[("psF", pb), "sin"], w=[rbk])
                    P.add("pool", lambda e: e.tensor_tensor(out=ra[:], in0=ra[:], in1=rb[:], op=ALU.add), r=[rak, rbk], w=[rak])
                    P.add("pool", lambda e: e.tensor_tensor(out=ev[e_][:], in0=ra[:], in1=rq[:], op=ALU.mult),
                          r=[rak, "rq"], w=[("ev", e_)])
                    for i_ in range(4):
                        store((ev[e_][32 * i_:32 * i_ + 32, :], ("ev", e_)), T["qT"][4 * qd + i_, 64:96, G])
                qrope()
            for pr in range(8):
                def qnope(pr=pr):
                    pf = nf()
                    for c in range(3):
                        P.add("pe", lambda e, c=c: e.matmul(psF[pf][:], lhsT=wqn[:, c, pr, :], rhs=qlT[:, c, :], start=(c == 0), stop=(c == 2)),
                              r=QL + WQN, w=[("psF", pf)])
                    e_ = ne()
                    P.add("dve", lambda e: e.tensor_tensor(out=ev[e_][:], in0=psF[pf][:], in1=rq[:], op=ALU.mult),
                          r=[("psF", pf), "rq"], w=[("ev", e_)])
                    for i_ in range(2):
                        store((ev[e_][64 * i_:64 * i_ + 64, :], ("ev", e_)), T["qT"][2 * pr + i_, 0:64, G])
                qnope()

                def kpair(pr=pr):
                    pf = nf()
                    for c in range(2):
                        P.add("pe", lambda e, c=c: e.matmul(psF[pf][:], lhsT=wkk[:, c, 2 * pr:2 * pr + 2, :].rearrange("p h j -> p (h j)"),
                                                            rhs=kvT[:, c, :], start=(c == 0), stop=(c == 1)),
                              r=KV + WKK, w=[("psF", pf)])
                    e_ = ne()
                    P.add("dve", lambda e: e.tensor_tensor(out=ev[e_][:], in0=psF[pf][:], in1=rkv[:], op=ALU.mult),
                          r=[("psF", pf), "rkv"], w=[("ev", e_)])
                    for i_ in range(2):
                        store((ev[e_][64 * i_:64 * i_ + 64, :], ("ev", e_)), T["kT"][2 * pr + i_, :, G])
                kpair()
            for t4 in range(4):
                for half in range(2):
                    def vtile(t4=t4, half=half):
                        pf = nf()
                        for c in range(2):
                            P.add("pe", lambda e, c=c: e.matmul(
                                psF[pf][:], lhsT=kvT[:, c, _sl(t4, 128)],
                                rhs=wkv[:, c, 8 * half:8 * half + 8, :].rearrange("p h j -> p (h j)"), start=(c == 0), stop=(c == 1)),
                                r=KV + WKV, w=[("psF", pf)])
                        e_ = ne()
                        P.add("act", lambda e: e.activation(out=ev[e_][:], in_=psF[pf][:], func=AF.Copy, scale=rkt[:, t4:t4 + 1]),
                              r=[("psF", pf), "rkt"], w=[("ev", e_)])
                        store((ev[e_][:], ("ev", e_)), T["v1"][g * 512 + t4 * 128:g * 512 + (t4 + 1) * 128, _sl(half, 512)])
                    vtile()

        for g in range(NG):
            group(g)
        P.emit()


def phase_mla(P, nc, S, T):
    NT = S // 128
    NG = S // 512
    SCALE = 96 ** -0.5
    with ExitStack() as es:
        def sb(name, shape, dt):
            return es.enter_context(nc.sbuf_tensor("ml_" + name, shape, dt))

        def ps(name, shape, dt):
            return es.enter_context(nc.psum_tensor("ml_" + name, shape, dt))

        ident = sb("id", [128, 128], BF16)
        onesf = sb("1f", [128, 64], F32)
        cmf = sb("cmf", [128, 128], F32)
        cmb = sb("cmb", [128, 128], BF16)
        kt = [sb("kt%d" % i, [96, S], BF16) for i in range(2)]
        vt = [sb("vt%d" % i, [128, NT, 66], BF16) for i in range(2)]
        qb = [sb("q%d" % i, [96, 512], BF16) for i in range(2)]
        gb = [sb("g%d" % i, [64, 512], BF16) for i in range(3)]
        pT = [sb("pT%d" % i, [128, 512], BF16) for i in range(4)]
        rz2 = [sb("rz%d" % i, [128, 512], F32) for i in range(2)]
        osb2 = [sb("osb%d" % i, [64, 512], F32) for i in range(2)]
        on = sb("on", [64, 512], F32)
        mx = [sb("mx%d" % i, [64, 512], BF16) for i in range(2)]
        Sp = [ps("S%d" % i, [128, 512], F32) for i in range(3)]
        Op = [ps("O%d" % i, [128, 512], F32) for i in range(2)]
        BC = ps("BC", [128, 512], F32)

        P.add("sp", lambda e: e.dma_start(out=ident[:], in_=T["c_ident"][:, :]), w=["ident"], dma=True)
        P.add("sp", lambda e: e.dma_start(out=cmf[:], in_=T["c_mask"][:, :]), w=["cmf"], dma=True)
        P.add("dve", lambda e: e.tensor_copy(out=cmb[:], in_=cmf[:]), r=["cmf"], w=["cmb"])
        P.add("pool", lambda e: e.memset(onesf[:], 1.0), w=["onesf"])
        for i_ in range(2):
            P.add("pool", lambda e, i_=i_: e.memset(vt[i_][:, :, 64:66], 1.0), w=[("vt1", i_)])

        steps = []
        for hd in range(16):
            for J in range(NG):
                last = 4 * J + 3
                for i in range(last + 1):
                    steps.append((hd, J, i, last))

        def loads(hd, J, i):
            if J == 0 and i == 0:
                s1 = hd % 2
                P.add("sp", lambda e: e.dma_start(out=kt[s1][64:96, :], in_=T["kTr"][:, :]), w=[("kt", s1)], dma=True)
                P.add("sp", lambda e: e.dma_start(out=kt[s1][0:64, :], in_=T["kT"][hd, :, :]), w=[("kt", s1)], dma=True)
                for c8 in range(0, NT, 8):
                    n8 = min(8, NT - c8)
                    P.add("sp", lambda e, c8=c8, n8=n8: e.dma_start(
                        out=vt[s1][:, c8:c8 + n8, 0:64],
                        in_=T["v1"][c8 * 128:(c8 + n8) * 128, _sl(hd, 64)].rearrange("(t p) d -> p t d", p=128)),
                        w=[("vt", s1, c8 // 8)], dma=True)
            if i == 0:
                s2 = (hd * NG + J) % 2
                P.add("sp", lambda e: e.dma_start(out=qb[s2][:], in_=T["qT"][hd, :, _sl(J, 512)]), w=[("qb", s2)], dma=True)
                g3 = (hd * NG + J) % 3
                P.add("sp", lambda e: e.dma_start(out=gb[g3][:], in_=T["g1T"][_sl(hd, 64), _sl(J, 512)]), w=[("gb", g3)], dma=True)

        def qk(n):
            hd, J, i, last = steps[n]
            loads(hd, J, i)
            c0 = max(0, i - 4 * J) * 128
            sbk = n % 3
            qs = (hd * NG + J) % 2
            diag = i >= 4 * J
            P.add("pe", lambda e: e.matmul(Sp[sbk][:, c0:512], lhsT=kt[hd % 2][:, _sl(i, 128)], rhs=qb[qs][:, c0:512],
                                           start=True, stop=not diag),
                  r=[("kt", hd % 2), ("qb", qs)], w=[("Sp", sbk)])
            if diag:
                P.add("pe", lambda e: e.matmul(Sp[sbk][:, c0:c0 + 128], lhsT=ident[:], rhs=cmb[:], start=False, stop=True),
                      r=["ident", "cmb"], w=[("Sp", sbk)])

        def pv(n):
            hd, J, i, last = steps[n]
            c0 = max(0, i - 4 * J) * 128
            sbk = n % 3
            pt = n % 4
            osl = (hd * NG + J) % 2
            vs = hd % 2
            P.add("act", lambda e: e.activation(out=pT[pt][:, c0:512], in_=Sp[sbk][:, c0:512], func=AF.Exp, scale=SCALE),
                  r=[("Sp", sbk)], w=[("pT", pt)])
            P.add("pe", lambda e: e.matmul(Op[osl][0:65, c0:512], lhsT=vt[vs][:, i, 0:65], rhs=pT[pt][:, c0:512],
                                           start=(i == 0), stop=(i == last)),
                  r=[("vt", vs, i // 8), ("vt1", vs), ("pT", pt)], w=[("Op", osl)])
            if i != last:
                return
            gsl = (hd * NG + J) % 2
            rz, osb = rz2[gsl], osb2[gsl]
            g3 = (hd * NG + J) % 3
            P.add("dve", lambda e: e.reciprocal(out=rz[64:65, :], in_=Op[osl][64:65, :]), r=[("Op", osl)], w=[("rz", gsl)])
            P.add("act", lambda e: e.copy(out=osb[:], in_=Op[osl][0:64, :]), r=[("Op", osl)], w=[("osb", gsl)])

            def epi_b():
                P.add("pe", lambda e: e.matmul(BC[0:64, :], lhsT=onesf[64:65, :], rhs=rz[64:65, :], start=True, stop=True),
                      r=["onesf", ("rz", gsl)], w=["BC"])
                P.add("dve", lambda e: e.tensor_tensor(out=on[:], in0=osb[:], in1=BC[0:64, :], op=ALU.mult),
                      r=[("osb", gsl), "BC"], w=["on"])
                P.add("dve", lambda e: e.tensor_tensor(out=mx[gsl][:], in0=on[:], in1=gb[g3][:], op=ALU.mult),
                      r=["on", ("gb", g3)], w=[("mx", gsl)])
                P.add("pool", lambda e: e.dma_start(out=T["mix1T"][_sl(hd, 64), _sl(J, 512)], in_=mx[gsl][:]),
                      r=[("mx", gsl)], dma=True)
            deferred.append((n + 6, epi_b))

        deferred = []

        def run_deferred(n):
            while deferred and deferred[0][0] <= n:
                deferred.pop(0)[1]()

        LA = 2
        for n in range(min(LA, len(steps))):
            qk(n)
        for n in range(len(steps)):
            if n + LA < len(steps):
                qk(n + LA)
            run_deferred(n)
            pv(n)
        run_deferred(len(steps) + 10)
        P.emit()


SCRATCH0 = dict(
    qaT=([512, None], BF16), kaT=([512, None], BF16), va=([None, 512], BF16),
    qbT=([512, None], BF16), kbT=([64, None], BF16), vb=([None, 64], BF16),
    qiT=([512, None], BF16), kiT=([64, None], BF16), wi=([None, 8], F32),
    gT=([1024, None], BF16), mixT=([1024, None], BF16), x1=([None, 1024], F32),
    qT=([16, 96, None], BF16), kT=([16, 64, None], BF16), kTr=([32, None], BF16), v1=([None, 1024], BF16),
    g1T=([1024, None], BF16), mix1T=([1024, None], BF16),
)


def build(S, topk, phases, outs):
    nc = bass.Bass("TRN2", target_bir_lowering=False)
    T = {}

    def din(name, shape, dt=F32):
        T[name] = nc.dram_tensor(name, shape, dt, kind="ExternalInput").ap()

    din("x", [S, D])
    din("e_norm", [D])
    din("e_w_in", [D, L0_COLS])
    din("c_ident", [128, 128], BF16)
    din("c_gath", [128, 2, 12, 128])
    din("c_mask", [128, 128])
    din("c_pow2", [NIT])
    din("rel_bias", [32, 12])
    for nm in ("e_lam_q1", "e_lam_k1", "e_lam_q2", "e_lam_k2"):
        din(nm, [64])
    din("e_subln", [128])
    din("e_w_o", [D, D])
    din("o_norm", [D])
    din("o_w_in", [D, L1_COLS])
    din("o_q_norm", [384])
    din("o_w_uq", [384, 1536])
    din("o_kv_norm", [256])
    din("o_w_ukv", [256, 2048])
    din("o_w_o", [D, D])
    din("final_norm", [D])
    din("c_cos", [32, S])
    din("c_sin", [32, S])
    for name, (shape, dt) in SCRATCH0.items():
        shp = [S if v is None else v for v in shape]
        kind = "ExternalOutput" if name in outs else "Internal"
        T[name] = nc.dram_tensor(name, shp, dt, kind=kind).ap()
    T["out"] = nc.dram_tensor("out", [S, D], F32, kind="ExternalOutput").ap()
    if "dbg_nm" in outs:
        T["dbg_nm"] = nc.dram_tensor("dbg_nm", [S, S], BF16, kind="ExternalOutput").ap()
        T["dbg_acc"] = nc.dram_tensor("dbg_acc", [S, S], F32, kind="ExternalOutput").ap()
    with ExitStack() as es:
        P = Prog(nc, es)
        if "proj0" in phases:
            phase_proj0(P, nc, S, T)
        if "diff" in phases:
            phase_diff(P, nc, S, T)
        if "dsa" in phases:
            phase_dsa(P, nc, S, T, topk)
        if "op0" in phases:
            phase_outproj(P, nc, S, T, "mixT", "e_w_o", "x", "x1", False)
        if "proj1" in phases:
            phase_proj1(P, nc, S, T)
        if "mla" in phases:
            phase_mla(P, nc, S, T)
        if "op1" in phases:
            phase_outproj(P, nc, S, T, "mix1T", "o_w_o", "x1", "out", True)
    return nc


def t5_bucket_np(rel):
    nb = 16
    ret = np.where(rel > 0, nb, 0)
    n = np.abs(rel)
    max_exact = nb // 2
    n_f = np.maximum(n, 1).astype(np.float32)
    large = max_exact + (np.log(n_f / max_exact) / math.log(128 / max_exact) * (nb - max_exact)).astype(np.int32)
    large = np.minimum(large, nb - 1)
    return ret + np.where(n < max_exact, n, large)


def host_consts(rel_bias):
    kk = np.arange(128)[:, None]
    qq = np.arange(128)[None, :]
    gath = np.zeros((2, 128, 12, 128), np.float32)
    for r in range(2):
        idx = t5_bucket_np((kk - r * 128) - qq)
        gath[r] = np.transpose(np.asarray(rel_bias)[idx], (0, 2, 1))
    cmask = np.where((kk // 64) <= (qq // 64), 0.0, NEG).astype(np.float32)
    pow2 = (0.5 ** np.arange(1, NIT + 1)).astype(np.float32)
    gath = np.ascontiguousarray(np.transpose(gath, (1, 0, 2, 3)))
    return dict(c_ident=np.eye(128).astype(ml_dtypes.bfloat16), c_gath=gath, c_mask=cmask, c_pow2=pow2)


def rope_consts(S):
    inv = (np.float32(10000.0) ** (-np.arange(0, 32, 2, dtype=np.float32) / np.float32(32))).astype(np.float32)
    ang = (np.arange(S, dtype=np.float32)[:, None] * inv[None, :]).astype(np.float32)
    c = np.cos(ang).astype(np.float32).T
    s_ = np.sin(ang).astype(np.float32).T
    return dict(c_cos=np.ascontiguousarray(np.concatenate([c, c], 0)), c_sin=np.ascontiguousarray(np.concatenate([s_, s_], 0)))


ALL_PHASES = ("proj0", "diff", "dsa", "op0", "proj1", "mla", "op1")
S_FULL = 4096
TOPK = 256


def kernel(x, rel_bias, e_norm, e_w_in, e_lam_q1, e_lam_k1, e_lam_q2, e_lam_k2, e_subln, e_w_o,
           o_norm, o_w_in, o_q_norm, o_w_uq, o_kv_norm, o_w_ukv, o_w_o, final_norm):
    f = lambda a: np.ascontiguousarray(np.asarray(a, dtype=np.float32))
    x = f(x)
    B = x.shape[0]
    shared = dict(rel_bias=f(rel_bias), e_norm=f(e_norm)[0], e_w_in=f(e_w_in)[0], e_lam_q1=f(e_lam_q1)[0],
                  e_lam_k1=f(e_lam_k1)[0], e_lam_q2=f(e_lam_q2)[0], e_lam_k2=f(e_lam_k2)[0], e_subln=f(e_subln)[0],
                  e_w_o=f(e_w_o)[0], o_norm=f(o_norm)[0], o_w_in=f(o_w_in)[0], o_q_norm=f(o_q_norm)[0],
                  o_w_uq=f(o_w_uq)[0], o_kv_norm=f(o_kv_norm)[0], o_w_ukv=f(o_w_ukv)[0], o_w_o=f(o_w_o)[0],
                  final_norm=f(final_norm))
    shared.update(host_consts(shared["rel_bias"]))
    shared.update(rope_consts(S_FULL))
    nc = build(S_FULL, TOPK, ALL_PHASES, ())
    in_maps = [dict(shared, x=x[b]) for b in range(B)]
    res = run_bass_kernel_spmd(nc, in_maps, core_ids=list(range(B)))
    return np.stack([np.asarray(r["out"], dtype=np.float32) for r in res.results], axis=0)
```

```python
import math
from contextlib import ExitStack
import numpy as np
import ml_dtypes
import concourse.bass as bass
import concourse.mybir as mybir
from concourse.bass_utils import run_bass_kernel_spmd

F32 = mybir.dt.float32
BF16 = mybir.dt.bfloat16
AF = mybir.ActivationFunctionType
ALU = mybir.AluOpType
AX = mybir.AxisListType

D = 1024
EPS = 1e-6
L0_COLS = 3784
L1_COLS = 1696
NEG = -30000.0

SAME_ENGINE_SYNC = {"act", "dve", "pool"}


class Prog:
    QUEUES = ("pe", "act", "dve", "pool", "sp")
    NDSEM = 44
    NSP = 30

    def __init__(self, nc, es):
        self.nc = nc
        self.sem = {}
        self.count = {}
        for q in self.QUEUES:
            self.sem[q] = es.enter_context(nc.semaphore("s_" + q))
            self.count[q] = 0
        self.dsem = [es.enter_context(nc.semaphore("sd%d" % i)) for i in range(self.NDSEM)]
        self.dcount = [0] * self.NDSEM
        self.reset()

    def reset(self):
        self.ops = {q: [] for q in self.QUEUES}
        self.last_w = {}
        self.readers = {}
        self.nid = {}
        self.gmap = {}
        self.nsp = 0
        self.npool = 0

    def add(self, q, fn, r=(), w=(), dma=False, grp=None):
        if dma:
            if grp is None:
                grp = w[0] if len(w) else r[0]
            if grp not in self.gmap:
                if q == "pool":
                    self.gmap[grp] = self.NDSEM - 1 - self.npool
                    self.npool += 1
                else:
                    self.gmap[grp] = self.nsp
                    self.nsp += 1
                assert self.nsp <= self.NSP and self.npool <= self.NDSEM - self.NSP, "too many DMA groups"
            ident = ("g", grp)
        else:
            ident = q
        idx = self.nid.get(ident, 0)
        self.nid[ident] = idx + 1
        deps = {}

        def dep(e, i):
            if e == ident and not dma and q not in SAME_ENGINE_SYNC:
                return
            if deps.get(e, -1) < i:
                deps[e] = i

        for k in r:
            if k in self.last_w:
                dep(*self.last_w[k])
        for k in w:
            if k in self.last_w:
                dep(*self.last_w[k])
            for e, i in self.readers.get(k, {}).items():
                dep(e, i)
        for k in r:
            self.readers.setdefault(k, {})[ident] = idx
        for k in w:
            self.last_w[k] = (ident, idx)
            self.readers[k] = {}
        self.ops[q].append(dict(fn=fn, ident=ident, idx=idx, deps=deps, dma=dma, sig=dma))

    def _semof(self, ident):
        if isinstance(ident, tuple):
            return self.dsem[self.gmap[ident[1]]]
        return self.sem[ident]

    def emit(self):
        nc = self.nc
        byid = {}
        for q in self.QUEUES:
            for op in self.ops[q]:
                byid.setdefault(op["ident"], {})[op["idx"]] = op
        for q in self.QUEUES:
            for op in self.ops[q]:
                for e, i in op["deps"].items():
                    byid[e][i]["sig"] = True
        val = {}
        final = {}
        for ident, d in byid.items():
            isd = isinstance(ident, tuple)
            c = self.dcount[self.gmap[ident[1]]] if isd else self.count[ident]
            v = {}
            for i in range(len(d)):
                if d[i]["sig"]:
                    c += 16 if isd else 1
                v[i] = c
            val[ident] = v
            final[ident] = c
            if isd:
                self.dcount[self.gmap[ident[1]]] = c
            else:
                self.count[ident] = c
        with nc.Block() as block:
            engs = dict(pe=block.tensor, act=block.scalar, dve=block.vector, pool=block.gpsimd, sp=block.sync)
            for q in self.QUEUES:
                ops = self.ops[q]
                if not ops:
                    continue

                def body(e, ops=ops, q=q):
                    waited = {}
                    for op in ops:
                        for ident, i in op["deps"].items():
                            v = val[ident][i]
                            if waited.get(ident, -1) < v:
                                e.wait_ge(self._semof(ident), v)
                                waited[ident] = v
                        ins = op["fn"](e)
                        if op["sig"]:
                            ins.then_inc(self._semof(op["ident"]), 16 if op["dma"] else 1)
                    for ident in {o["ident"]: 1 for o in ops if o["dma"]}:
                        if waited.get(ident, -1) < final[ident]:
                            e.wait_ge(self._semof(ident), final[ident])

                engs[q](body)
        self.reset()


def _sl(i, n):
    return slice(i * n, (i + 1) * n)


def phase_proj0(P, nc, S, T):
    NT = S // 128
    NG = S // 512
    with ExitStack() as es:
        def sb(name, shape, dt):
            return es.enter_context(nc.sbuf_tensor(name, shape, dt))

        def ps(name, shape, dt):
            return es.enter_context(nc.psum_tensor(name, shape, dt))

        wsb = sb("p0_w", [128, 8, L0_COLS], BF16)
        wkk = sb("p0_wkk", [128, 8, 128], BF16)
        wvw = sb("p0_wvw", [128, 8, 72], BF16)
        wst = [sb("p0_wst%d" % i, [128, L0_COLS], F32) for i in range(2)]
        gcol = sb("p0_g", [128, 8], F32)
        ident = sb("p0_id", [128, 128], BF16)
        xt = [sb("p0_x%d" % i, [128, D], F32) for i in range(2)]
        hb = [sb("p0_h%d" % i, [128, D], BF16) for i in range(2)]
        hT = [sb("p0_hT%d" % i, [128, 8, 512], BF16) for i in range(2)]
        junk = sb("p0_junk", [128, D], BF16)
        ssq = sb("p0_ssq", [128, NT], F32)
        rs = sb("p0_rs", [128, NT], F32)
        ev = [sb("p0_ev%d" % i, [128, 512], BF16) for i in range(4)]
        evw = [sb("p0_evw%d" % i, [128, 8], F32) for i in range(2)]
        evb = [sb("p0_evb%d" % i, [128, 64], BF16) for i in range(2)]
        psT = [ps("p0_psT%d" % i, [128, D], BF16) for i in range(2)]
        psF = [ps("p0_psF%d" % i, [128, 512], F32) for i in range(4)]
        psW = [ps("p0_psW%d" % i, [128, 72], F32) for i in range(2)]

        P.add("sp", lambda e: e.dma_start(out=ident[:], in_=T["c_ident"][:, :]), w=["ident"], dma=True)
        P.add("sp", lambda e: e.dma_start(out=gcol[:], in_=T["e_norm"].rearrange("(k p) -> p k", p=128),
                                          allow_slow_non_contiguous=True),
              w=["gcol"], dma=True)
        done_tiles = set()

        def tile_ops(t):
            if t in done_tiles:
                return
            done_tiles.add(t)
            xs = t % 2
            hs = (t // 4) % 2
            t4 = t % 4
            P.add("sp", lambda e, t=t, xs=xs: e.dma_start(out=xt[xs][:], in_=T["x"][_sl(t, 128), :]),
                  w=[("xt", xs)], dma=True)
            P.add("act", lambda e, t=t, xs=xs: e.activation(out=junk[:], in_=xt[xs][:], func=AF.Square,
                                                            accum_out=ssq[:, t:t + 1]),
                  r=[("xt", xs)], w=["junk", ("ssq", t)])
            P.add("act", lambda e, t=t: e.activation(out=rs[:, t:t + 1], in_=ssq[:, t:t + 1], func=AF.Sqrt,
                                                     bias=EPS, scale=1.0 / D),
                  r=[("ssq", t)], w=[("rs", t)])
            P.add("dve", lambda e, t=t: e.reciprocal(out=rs[:, t:t + 1], in_=rs[:, t:t + 1]),
                  r=[("rs", t)], w=[("rs", t)])
            P.add("dve", lambda e, t=t, xs=xs: e.tensor_scalar(out=hb[xs][:], in0=xt[xs][:], scalar1=rs[:, t:t + 1],
                                                               scalar2=None, op0=ALU.mult),
                  r=[("xt", xs), ("rs", t)], w=[("hb", xs)])
            for k in range(8):
                P.add("pe", lambda e, k=k, xs=xs: e.transpose(out=psT[xs][:, _sl(k, 128)], in_=hb[xs][:, _sl(k, 128)],
                                                              identity=ident[:]),
                      r=[("hb", xs), "ident"], w=[("psT", xs)])
            P.add("act", lambda e, xs=xs, hs=hs, t4=t4: e.copy(
                out=hT[hs][:, :, _sl(t4, 128)], in_=psT[xs][:].rearrange("p (k t) -> p k t", k=8)),
                r=[("psT", xs)], w=[("hT", hs)])

        tile_ops(0)
        tile_ops(1)
        for k in range(8):
            s = k % 2
            P.add("sp", lambda e, k=k, s=s: e.dma_start(out=wst[s][:], in_=T["e_w_in"][_sl(k, 128), :]),
                  w=[("wst", s)], dma=True)
            P.add("dve", lambda e, k=k, s=s: e.tensor_scalar(out=wsb[:, k, :], in0=wst[s][:], scalar1=gcol[:, k:k + 1],
                                                            scalar2=None, op0=ALU.mult),
                  r=[("wst", s), "gcol"], w=[("w", k)])
            P.add("pool", lambda e, k=k: e.tensor_copy(out=wkk[:, k, 0:64], in_=wsb[:, k, 2048:2112]),
                  r=[("w", k)], w=[("wkk", k)])
            P.add("pool", lambda e, k=k: e.tensor_copy(out=wkk[:, k, 64:128], in_=wsb[:, k, 2688:2752]),
                  r=[("w", k)], w=[("wkk", k)])
            P.add("pool", lambda e, k=k: e.tensor_copy(out=wvw[:, k, 0:64], in_=wsb[:, k, 2112:2176]),
                  r=[("w", k)], w=[("wvw", k)])
            P.add("pool", lambda e, k=k: e.tensor_copy(out=wvw[:, k, 64:72], in_=wsb[:, k, 2752:2760]),
                  r=[("w", k)], w=[("wvw", k)])
        wall = [("w", k) for k in range(8)]
        wkk_all = [("wkk", k) for k in range(8)]
        wvw_all = [("wvw", k) for k in range(8)]

        chunks = []
        for c in range(4):
            chunks.append((c * 128, "qaT", c * 128, "q"))
        for c in range(4):
            chunks.append((512 + c * 128, "kaT", c * 128, "c"))
        for c in range(4):
            chunks.append((1536 + c * 128, "qbT", c * 128, "q"))
        for c in range(4):
            chunks.append((2176 + c * 128, "qiT", c * 128, "c"))
        chunks.append((None, "kk", 0, "c"))
        for c in range(8):
            chunks.append((2760 + c * 128, "gT", c * 128, "silu"))

        nev = 0
        nF = 0
        for g in range(NG):
            hs = g % 2
            for t4 in range(4):
                t = g * 4 + t4
                xs = t % 2
                tile_ops(t)
            for (c0, dst, drow, mode) in chunks:
                pf = nF % 4
                nF += 1
                for k in range(8):
                    if c0 is None:
                        lhs = (lambda k=k: wkk[:, k, :])
                        rk = wkk_all
                    else:
                        lhs = (lambda k=k, c0=c0: wsb[:, k, c0:c0 + 128])
                        rk = wall
                    P.add("pe", lambda e, k=k, lhs=lhs, pf=pf, hs=hs: e.matmul(
                        psF[pf][:], lhsT=lhs(), rhs=hT[hs][:, k, :], start=(k == 0), stop=(k == 7)),
                        r=[("hT", hs), rk[k]], w=[("psF", pf)])
                es_ = nev % 4
                nev += 1
                if mode == "silu":
                    P.add("act", lambda e, pf=pf, es_=es_: e.activation(out=ev[es_][:], in_=psF[pf][:], func=AF.Silu),
                          r=[("psF", pf)], w=[("ev", es_)])
                elif mode == "q":
                    P.add("dve", lambda e, pf=pf, es_=es_: e.tensor_scalar(out=ev[es_][:], in0=psF[pf][:], scalar1=0.125,
                                                                          scalar2=None, op0=ALU.mult),
                          r=[("psF", pf)], w=[("ev", es_)])
                else:
                    P.add("dve", lambda e, pf=pf, es_=es_: e.tensor_copy(out=ev[es_][:], in_=psF[pf][:]),
                          r=[("psF", pf)], w=[("ev", es_)])
                if dst == "kk":
                    P.add("pool", lambda e, es_=es_, g=g: e.dma_start(out=T["kbT"][:, _sl(g, 512)], in_=ev[es_][0:64, :]),
                          r=[("ev", es_)], dma=True)
                    P.add("pool", lambda e, es_=es_, g=g: e.dma_start(out=T["kiT"][:, _sl(g, 512)], in_=ev[es_][64:128, :]),
                          r=[("ev", es_)], dma=True)
                else:
                    P.add("pool", lambda e, es_=es_, g=g, dst=dst, drow=drow: e.dma_start(
                        out=T[dst][drow:drow + 128, _sl(g, 512)], in_=ev[es_][:]),
                        r=[("ev", es_)], dma=True)
            for t4 in range(4):
                t = g * 4 + t4
                pf = nF % 4
                nF += 1
                pw = t % 2
                for k in range(8):
                    P.add("pe", lambda e, k=k, pf=pf, hs=hs, t4=t4: e.matmul(
                        psF[pf][:], lhsT=hT[hs][:, k, _sl(t4, 128)], rhs=wsb[:, k, 1024:1536], start=(k == 0), stop=(k == 7)),
                        r=[("hT", hs), wall[k]], w=[("psF", pf)])
                for k in range(8):
                    P.add("pe", lambda e, k=k, pw=pw, hs=hs, t4=t4: e.matmul(
                        psW[pw][:], lhsT=hT[hs][:, k, _sl(t4, 128)], rhs=wvw[:, k, :], start=(k == 0), stop=(k == 7)),
                        r=[("hT", hs), wvw_all[k]], w=[("psW", pw)])
                es_ = nev % 4
                nev += 1
                P.add("act", lambda e, pf=pf, es_=es_: e.copy(out=ev[es_][:], in_=psF[pf][:]),
                      r=[("psF", pf)], w=[("ev", es_)])
                P.add("pool", lambda e, es_=es_, t=t: e.dma_start(out=T["va"][_sl(t, 128), :], in_=ev[es_][:]),
                      r=[("ev", es_)], dma=True)
                P.add("dve", lambda e, pw=pw: e.tensor_copy(out=evb[pw][:], in_=psW[pw][:, 0:64]),
                      r=[("psW", pw)], w=[("evb", pw)])
                P.add("dve", lambda e, pw=pw: e.tensor_copy(out=evw[pw][:], in_=psW[pw][:, 64:72]),
                      r=[("psW", pw)], w=[("evw", pw)])
                P.add("pool", lambda e, pw=pw, t=t: e.dma_start(out=T["vb"][_sl(t, 128), :], in_=evb[pw][:]),
                      r=[("evb", pw)], dma=True)
                P.add("pool", lambda e, pw=pw, t=t: e.dma_start(out=T["wi"][_sl(t, 128), :], in_=evw[pw][:]),
                      r=[("evw", pw)], dma=True)
        P.emit()


HB = [0, 2, 4, 6, 1, 3, 5, 7]


def load_bias_tiles(P, nc, sb, T, pfx):
    gsb = sb(pfx + "gsb", [128, 2, 12, 128], F32)
    cm = sb(pfx + "cm", [128, 128], F32)
    c12 = sb(pfx + "c12", [128, 12], F32)
    bT = sb(pfx + "bT", [128, 2, 12, 128], BF16)
    tmp = sb(pfx + "btmp", [128, 128], F32)
    P.add("sp", lambda e: e.dma_start(out=gsb[:], in_=T["c_gath"]), w=["gsb"], dma=True)
    P.add("sp", lambda e: e.dma_start(out=cm[:], in_=T["c_mask"][:, :]), w=["cm"], dma=True)
    P.add("sp", lambda e: e.dma_start(out=c12[:], in_=T["rel_bias"][15, :].partition_broadcast(128)), w=["c12"], dma=True)
    for h in range(12):
        ho = h if h < 4 else 4 + HB.index(h - 4)
        P.add("dve", lambda e, h=h: e.tensor_scalar(out=tmp[:], in0=gsb[:, 0, h, :], scalar1=c12[:, h:h + 1], scalar2=None,
                                                    op0=ALU.subtract), r=["gsb", "c12"], w=["btmp"])
        P.add("dve", lambda e, h=h, ho=ho: e.tensor_tensor(out=bT[:, 0, ho, :], in0=tmp[:], in1=cm[:], op=ALU.add),
              r=["btmp", "cm"], w=["bT"])
        P.add("dve", lambda e, h=h, ho=ho: e.tensor_scalar(out=bT[:, 1, ho, :], in0=gsb[:, 1, h, :], scalar1=c12[:, h:h + 1],
                                                           scalar2=None, op0=ALU.subtract), r=["gsb", "c12"], w=["bT"])
    return bT


def phase_diff(P, nc, S, T):
    NT = S // 128
    NG = S // 512
    LAM_INIT = 0.8 - 0.6 * math.exp(-0.3 * 0)
    with ExitStack() as es:
        def sb(name, shape, dt):
            return es.enter_context(nc.sbuf_tensor(name, shape, dt))

        def ps(name, shape, dt):
            return es.enter_context(nc.psum_tensor(name, shape, dt))

        ident = sb("da_id", [128, 128], BF16)
        onesb = sb("da_1b", [128, 128], BF16)
        onesf = sb("da_1f", [128, 128], F32)
        bT = load_bias_tiles(P, nc, sb, T, "da_")
        lp = sb("da_lp", [128, 4, 64], F32)
        lpp = sb("da_lpp", [128, 2, 64], F32)
        ls = sb("da_ls", [128, 2], F32)
        le = sb("da_le", [128, 2], F32)
        nlam = sb("da_nlam", [128, 1], F32)
        gs = sb("da_gs", [128, 1], F32)
        ka = [sb("da_ka%d" % i, [128, S], BF16) for i in range(2)]
        va = [sb("da_va%d" % i, [128, NT, 128], BF16) for i in range(2)]
        qb = [sb("da_q%d" % i, [128, 2, 512], BF16) for i in range(2)]
        gb = [sb("da_g%d" % i, [128, 512], BF16) for i in range(2)]
        pT = [sb("da_pT%d" % i, [128, 512], BF16) for i in range(4)]
        rzt = sb("da_rz", [128, 512], F32)
        tm = [sb("da_tm%d" % i, [128, 512], F32) for i in range(2)]
        ot = sb("da_o", [128, 512], F32)
        sq = sb("da_sq", [128, 512], F32)
        rt = sb("da_rt", [128, 512], F32)
        mx = [sb("da_mx%d" % i, [128, 512], BF16) for i in range(2)]
        Sp = [ps("da_S%d" % i, [128, 512], F32) for i in range(3)]
        Op = [ps("da_O%d" % i, [128, 512], F32) for i in range(2)]
        Zp = [ps("da_Z%d" % i, [128, 512], F32) for i in range(2)]
        Rp = ps("da_R", [128, 512], F32)

        P.add("sp", lambda e: e.dma_start(out=ident[:], in_=T["c_ident"][:, :]), w=["ident"], dma=True)
        P.add("pool", lambda e: e.memset(onesb[:], 1.0), w=["onesb"])
        P.add("pool", lambda e: e.memset(onesf[:], 1.0), w=["onesf"])
        for i_ in range(2):
            P.add("pool", lambda e, i_=i_: e.memset(qb[i_][:], 0.0), w=[("qb", i_)])
        for n, nm in enumerate(["e_lam_q1", "e_lam_k1", "e_lam_q2", "e_lam_k2"]):
            P.add("sp", lambda e, n=n, nm=nm: e.dma_start(out=lp[:, n, :], in_=T[nm].partition_broadcast(128)),
                  w=[("lp", n)], dma=True)
        P.add("sp", lambda e: e.dma_start(out=gs[:], in_=T["e_subln"].rearrange("(p o) -> p o", o=1)), w=["gs"], dma=True)
        for n in range(2):
            P.add("dve", lambda e, n=n: e.tensor_tensor(out=lpp[:, n, :], in0=lp[:, 2 * n, :], in1=lp[:, 2 * n + 1, :], op=ALU.mult),
                  r=[("lp", 2 * n), ("lp", 2 * n + 1)], w=[("lpp", n)])
            P.add("dve", lambda e, n=n: e.reduce_sum(out=ls[:, n:n + 1], in_=lpp[:, n, :], axis=AX.X),
                  r=[("lpp", n)], w=[("ls", n)])
            P.add("act", lambda e, n=n: e.activation(out=le[:, n:n + 1], in_=ls[:, n:n + 1], func=AF.Exp),
                  r=[("ls", n)], w=[("le", n)])
        P.add("dve", lambda e: e.tensor_tensor(out=nlam[:], in0=le[:, 1:2], in1=le[:, 0:1], op=ALU.subtract),
              r=[("le", 0), ("le", 1)], w=["nlam"])
        P.add("dve", lambda e: e.tensor_scalar(out=nlam[:], in0=nlam[:], scalar1=-LAM_INIT, scalar2=None, op0=ALU.add),
              r=["nlam"], w=["nlam"])
        P.add("dve", lambda e: e.tensor_scalar(out=gs[:], in0=gs[:], scalar1=1.0 - LAM_INIT, scalar2=None, op0=ALU.mult),
              r=["gs"], w=["gs"])

        steps = []
        for h in range(4):
            for J in range(NG):
                for m in range(2):
                    last = 4 * J + 3
                    for i in range(last + 1):
                        steps.append((h, J, m, i, last))

        def loads(h, J, m, i):
            if J == 0 and m == 0 and i == 0:
                s1 = h % 2
                P.add("sp", lambda e: e.dma_start(out=ka[s1][:], in_=T["kaT"][_sl(h, 128), :]), w=[("ka", s1)], dma=True)
                for c8 in range(0, NT, 8):
                    n8 = min(8, NT - c8)
                    P.add("sp", lambda e, c8=c8, n8=n8: e.dma_start(
                        out=va[s1][:, c8:c8 + n8, :],
                        in_=T["va"][c8 * 128:(c8 + n8) * 128, _sl(h, 128)].rearrange("(t p) e -> p t e", p=128)),
                        w=[("va", s1, c8 // 8)], dma=True)
            if m == 0 and i == 0:
                s2 = (h * NG + J) % 2
                P.add("sp", lambda e: e.dma_start(out=qb[s2][0:64, 0, :], in_=T["qaT"][h * 128:h * 128 + 64, _sl(J, 512)]),
                      w=[("qb", s2)], dma=True)
                P.add("sp", lambda e: e.dma_start(out=qb[s2][64:128, 1, :], in_=T["qaT"][h * 128 + 64:h * 128 + 128, _sl(J, 512)]),
                      w=[("qb", s2)], dma=True)
                P.add("sp", lambda e: e.dma_start(out=gb[s2][:], in_=T["gT"][_sl(h, 128), _sl(J, 512)]), w=[("gb", s2)], dma=True)

        def qk(n):
            h, J, m, i, last = steps[n]
            loads(h, J, m, i)
            c0 = max(0, i - 4 * J) * 128
            sbk = n % 3
            qs = (h * NG + J) % 2
            adds = []
            if i >= 4 * J:
                adds.append((c0, 0))
            if i + 1 >= 4 * J and i + 1 <= last:
                adds.append(((i + 1 - 4 * J) * 128, 1))
            P.add("pe", lambda e: e.matmul(Sp[sbk][:, c0:512], lhsT=ka[h % 2][:, _sl(i, 128)],
                                           rhs=qb[qs][:, m, c0:512], start=True, stop=(len(adds) == 0)),
                  r=[("ka", h % 2), ("qb", qs)], w=[("Sp", sbk)])
            for a, (cs, r) in enumerate(adds):
                P.add("pe", lambda e, cs=cs, r=r, a=a: e.matmul(Sp[sbk][:, cs:cs + 128], lhsT=ident[:], rhs=bT[:, r, h, :],
                                                                start=False, stop=(a == len(adds) - 1)),
                      r=["ident", "bT"], w=[("Sp", sbk)])

        def pv(n):
            h, J, m, i, last = steps[n]
            c0 = max(0, i - 4 * J) * 128
            sbk = n % 3
            pt = n % 4
            osl = (2 * (h * NG + J) + m) % 2
            P.add("act", lambda e: e.activation(out=pT[pt][:, c0:512], in_=Sp[sbk][:, c0:512], func=AF.Exp),
                  r=[("Sp", sbk)], w=[("pT", pt)])
            P.add("pe", lambda e: e.matmul(Op[osl][:, c0:512], lhsT=va[h % 2][:, i, :], rhs=pT[pt][:, c0:512],
                                           start=(i == 0), stop=(i == last)),
                  r=[("va", h % 2, i // 8), ("pT", pt)], w=[("Op", osl)])
            P.add("pe", lambda e: e.matmul(Zp[osl][:, c0:512], lhsT=onesb[:], rhs=pT[pt][:, c0:512],
                                           start=(i == 0), stop=(i == last)),
                  r=["onesb", ("pT", pt)], w=[("Zp", osl)])
            if i != last:
                return
            P.add("dve", lambda e: e.reciprocal(out=rzt[:], in_=Zp[osl][:]), r=[("Zp", osl)], w=["rzt"])
            P.add("dve", lambda e: e.tensor_tensor(out=tm[m][:], in0=Op[osl][:], in1=rzt[:], op=ALU.mult),
                  r=[("Op", osl), "rzt"], w=[("tm", m)])
            if m == 0:
                return
            gsl = (h * NG + J) % 2
            P.add("dve", lambda e: e.scalar_tensor_tensor(out=ot[:], in0=tm[1][:], scalar=nlam[:, 0:1], in1=tm[0][:],
                                                          op0=ALU.mult, op1=ALU.add),
                  r=[("tm", 0), ("tm", 1), "nlam"], w=["ot"])
            P.add("act", lambda e: e.activation(out=sq[:], in_=ot[:], func=AF.Square), r=["ot"], w=["sq"])
            def epi_b():
                P.add("pe", lambda e: e.matmul(Rp[:], lhsT=onesf[:], rhs=sq[:], start=True, stop=True),
                      r=["onesf", "sq"], w=["Rp"])
                P.add("act", lambda e: e.activation(out=rt[:], in_=Rp[:], func=AF.Sqrt, bias=EPS, scale=1.0 / 128),
                      r=["Rp"], w=["rt"])
                P.add("dve", lambda e: e.reciprocal(out=rt[:], in_=rt[:]), r=["rt"], w=["rt"])
                P.add("dve", lambda e: e.scalar_tensor_tensor(out=ot[:], in0=ot[:], scalar=gs[:, 0:1], in1=rt[:],
                                                              op0=ALU.mult, op1=ALU.mult),
                      r=["ot", "gs", "rt"], w=["ot"])
                P.add("dve", lambda e: e.tensor_tensor(out=mx[gsl][:], in0=ot[:], in1=gb[gsl][:], op=ALU.mult),
                      r=["ot", ("gb", gsl)], w=[("mx", gsl)])
                P.add("pool", lambda e: e.dma_start(out=T["mixT"][_sl(h, 128), _sl(J, 512)], in_=mx[gsl][:]),
                      r=[("mx", gsl)], dma=True)
            deferred.append((n + 6, epi_b))

        deferred = []

        def run_deferred(n):
            while deferred and deferred[0][0] <= n:
                deferred.pop(0)[1]()

        LA = 2
        for n in range(min(LA, len(steps))):
            qk(n)
        for n in range(len(steps)):
            if n + LA < len(steps):
                qk(n + LA)
            run_deferred(n)
            pv(n)
        run_deferred(len(steps) + 10)
        P.emit()


NIT = 20


def phase_dsa(P, nc, S, T, topk):
    NT = S // 128
    with ExitStack() as es:
        def sb(name, shape, dt):
            return es.enter_context(nc.sbuf_tensor(name, shape, dt))

        def ps(name, shape, dt):
            return es.enter_context(nc.psum_tensor(name, shape, dt))

        ident4 = sb("ds_id4", [128, 4, 128], BF16)
        onesf = sb("ds_1f", [128, 64], F32)
        pw2 = sb("ds_pw2", [128, NIT], F32)
        ki2 = sb("ds_ki2", [128, S], BF16)
        kb2 = sb("ds_kb2", [128, S], BF16)
        vb1 = sb("ds_vb1", [128, NT, 66], BF16)
        wis = sb("ds_wi", [128, NT, 8], F32)
        qi = [sb("ds_qi%d" % i, [128, 8, 128], BF16) for i in range(2)]
        qb = [sb("ds_qb%d" % i, [128, 8, 128], BF16) for i in range(3)]
        junk2 = sb("ds_junk2", [128, S], BF16)
        gB = [sb("ds_gB%d" % i, [64, 8, 128], BF16) for i in range(4)]
        acc = [sb("ds_acc%d" % i, [128, S], F32) for i in range(2)]
        R = [sb("ds_R%d" % i, [128, 512], F32) for i in range(4)]
        NM = [sb("ds_NM%d" % i, [128, S], BF16) for i in range(2)]
        junk = sb("ds_junk", [128, S], BF16)
        pT = [sb("ds_pT%d" % i, [128, 512], BF16) for i in range(4)]
        sm = sb("ds_sm", [128, 8], F32)
        Wk = sb("ds_Wk", [128, NIT], F32)
        rz = sb("ds_rz", [128, 1024], F32)
        osb = [sb("ds_osb%d" % i, [64, 512], F32) for i in range(2)]
        on = sb("ds_on", [64, 512], F32)
        mxB = [sb("ds_mxB%d" % i, [64, 512], BF16) for i in range(2)]
        Zp = [ps("ds_Z%d" % i, [128, 512], F32) for i in range(3)]
        Sp = [ps("ds_S%d" % i, [128, 512], F32) for i in range(3)]
        Op = [ps("ds_O%d" % i, [128, 512], F32) for i in range(2)]
        P.add("pool", lambda e: e.memset(onesf[:], 1.0), w=["onesf"])
        for i_ in range(3):
            P.add("pool", lambda e, i_=i_: e.memset(qb[i_][:], 0.0), w=[("qb", i_)])
        for i_ in range(2):
            P.add("pool", lambda e, i_=i_: e.memset(qi[i_][:], 0.0), w=[("qi", i_)])
        for half in range(2):
            P.add("sp", lambda e, half=half: e.dma_start(out=ki2[_sl(half, 64), :], in_=T["kiT"][:, :]), w=["ki2"], dma=True)

        def wis_chunk(c8):
            n8 = min(8, NT - c8)
            P.add("sp", lambda e: e.dma_start(
                out=wis[:, c8:c8 + n8, :], in_=T["wi"][c8 * 128:(c8 + n8) * 128, :].rearrange("(t p) h -> p t h", p=128)),
                w=[("wis", c8 // 8)], dma=True)
        wis_chunk(0)
        late = {}

        def late_setup():
            for half in range(2):
                P.add("sp", lambda e, half=half: e.dma_start(out=kb2[_sl(half, 64), :], in_=T["kbT"][:, :]), w=["kb2"], dma=True)
            for a in range(4):
                P.add("sp", lambda e, a=a: e.dma_start(out=ident4[:, a, :], in_=T["c_ident"][:, :]), w=["ident4"], dma=True)
            late["bT"] = load_bias_tiles(P, nc, sb, T, "ds_")
            P.add("sp", lambda e: e.dma_start(out=pw2[:], in_=T["c_pow2"].partition_broadcast(128)), w=["pw2"], dma=True)
            P.add("pool", lambda e: e.memset(vb1[:, :, 64:66], 1.0), w=["vb1"])
            for c8 in range(0, NT, 8):
                n8 = min(8, NT - c8)
                P.add("sp", lambda e, c8=c8, n8=n8: e.dma_start(
                    out=vb1[:, c8:c8 + n8, 0:64], in_=T["vb"][c8 * 128:(c8 + n8) * 128, :].rearrange("(t p) d -> p t d", p=128)),
                    w=["vb1"], dma=True)
                if c8 > 0:
                    wis_chunk(c8)

        cnt = dict(z=0, r=0, s=0, p=0)

        def indexer_items(j):
            s = j % 2
            s3 = j % 3
            nk = (j + 1) * 128
            nch = (nk + 511) // 512

            def u_load():
                P.add("sp", lambda e: e.dma_start(out=qi[s][0:64, 0:8:2, :],
                                                  in_=T["qiT"].rearrange("(c p) s -> p c s", p=128)[0:64, :, _sl(j, 128)]),
                      w=[("qi", s)], dma=True)
                P.add("sp", lambda e: e.dma_start(out=qi[s][64:128, 1:8:2, :],
                                                  in_=T["qiT"].rearrange("(c p) s -> p c s", p=128)[64:128, :, _sl(j, 128)]),
                      w=[("qi", s)], dma=True)
                P.add("sp", lambda e: e.dma_start(out=qb[s3][0:64, 0:8:2, :],
                                                  in_=T["qbT"].rearrange("(c p) s -> p c s", p=128)[0:64, :, _sl(j, 128)]),
                      w=[("qb", s3)], dma=True)
                P.add("sp", lambda e: e.dma_start(out=qb[s3][64:128, 1:8:2, :],
                                                  in_=T["qbT"].rearrange("(c p) s -> p c s", p=128)[64:128, :, _sl(j, 128)]),
                      w=[("qb", s3)], dma=True)
                P.add("sp", lambda e: e.dma_start(out=gB[j % 4][:], in_=T["gT"][512:1024, _sl(j, 128)].rearrange("(h d) q -> d h q", d=64)),
                      w=[("gB", j % 4)], dma=True)

            def mk(h, kc):
                st = {}
                N = min(512, nk - kc * 512)
                k0 = kc * 512

                def pe():
                    zs = cnt["z"] % 3
                    cnt["z"] += 1
                    st["zs"] = zs
                    P.add("pe", lambda e: e.matmul(Zp[zs][:, 0:N], lhsT=qi[s][:, h, :], rhs=ki2[:, k0:k0 + N],
                                                   start=True, stop=True),
                          r=[("qi", s), "ki2"], w=[("Zp", zs)])

                def rest():
                    zs = st["zs"]
                    rs_ = cnt["r"] % 4
                    cnt["r"] += 1
                    P.add("act", lambda e: e.activation(out=R[rs_][:, 0:N], in_=Zp[zs][:, 0:N], func=AF.Relu),
                          r=[("Zp", zs)], w=[("R", rs_)])
                    if h == 0:
                        P.add("dve", lambda e: e.tensor_scalar(out=acc[s][:, k0:k0 + N], in0=R[rs_][:, 0:N], scalar1=wis[:, j, 0:1],
                                                               scalar2=None, op0=ALU.mult),
                              r=[("R", rs_), ("wis", j // 8)], w=[("acc", s)])
                    else:
                        P.add("dve", lambda e: e.scalar_tensor_tensor(out=acc[s][:, k0:k0 + N], in0=R[rs_][:, 0:N],
                                                                      scalar=wis[:, j, h:h + 1], in1=acc[s][:, k0:k0 + N],
                                                                      op0=ALU.mult, op1=ALU.add),
                              r=[("R", rs_), ("wis", j // 8), ("acc", s)], w=[("acc", s)])
                    if h == 7 and kc == nch - 1:
                        P.add("dve", lambda e: e.memset(acc[s][0:64, nk - 64:nk], -1e30), w=[("acc", s)])
                return (pe, rest)
            items = [mk(h, kc) for h in range(8) for kc in range(nch)]
            return u_load, items

        def bisect_units(j):
            s = j % 2
            nk = (j + 1) * 128
            A = [("acc", s)]
            units = []

            def u_nm():
                P.add("dve", lambda e: e.tensor_scalar(out=NM[s][:, 0:nk], in0=acc[s][:, 0:nk], scalar1=sm[:, 1:2], scalar2=NEG,
                                                       op0=ALU.is_lt, op1=ALU.mult),
                      r=A + ["lo"], w=[("NM", s)])
                if "dbg_nm" in T:
                    P.add("pool", lambda e: e.dma_start(out=T["dbg_nm"][_sl(j, 128), 0:nk], in_=NM[s][:, 0:nk]), r=[("NM", s)], dma=True)
                    P.add("pool", lambda e: e.dma_start(out=T["dbg_acc"][_sl(j, 128), 0:nk], in_=acc[s][:, 0:nk]), r=[("acc", s)], dma=True)

            if nk <= topk:
                def u0():
                    P.add("dve", lambda e: e.memset(sm[:, 1:2], -1e29), w=["lo"])
                    u_nm()
                return [u0]
            assert nk - 64 >= topk
            h1 = max(64, int(round(0.42 * nk / 64.0)) * 64)
            n2 = nk - h1

            def u_init():
                P.add("dve", lambda e: e.reduce_max(out=sm[:, 0:1], in_=acc[s][:, 0:nk], axis=AX.X), r=A, w=["hi"])
                P.add("dve", lambda e: e.tensor_reduce(out=sm[:, 1:2], in_=acc[s][:, 0:nk - 64], axis=AX.X, op=ALU.min),
                      r=A, w=["lo"])
                P.add("dve", lambda e: e.tensor_tensor(out=sm[:, 2:3], in0=sm[:, 0:1], in1=sm[:, 1:2], op=ALU.subtract),
                      r=["hi", "lo"], w=["w0"])
                P.add("dve", lambda e: e.tensor_scalar(out=Wk[:], in0=pw2[:], scalar1=sm[:, 2:3], scalar2=None, op0=ALU.mult),
                      r=["pw2", "w0"], w=["Wk"])
                P.add("dve", lambda e: e.tensor_tensor(out=sm[:, 3:4], in0=sm[:, 1:2], in1=Wk[:, 0:1], op=ALU.add),
                      r=["lo", "Wk"], w=["mid"])
            units.append(u_init)

            def mk(k):
                def u():
                    kk_ = k + 1 if k + 1 < NIT else k
                    P.add("dve", lambda e: e.tensor_scalar(out=junk[:, 0:h1], in0=acc[s][:, 0:h1], scalar1=sm[:, 3:4],
                                                           scalar2=-(topk - 0.5 - n2 / 2.0),
                                                           op0=ALU.is_ge, op1=ALU.add, accum_out=sm[:, 4:5]),
                          r=A + ["mid"], w=["junk", "cnt"])
                    P.add("act", lambda e: e.activation(out=junk2[:, 0:n2], in_=acc[s][:, h1:nk], func=AF.Sign, scale=-1.0,
                                                        bias=sm[:, 3:4], accum_out=sm[:, 6:7]),
                          r=A + ["mid"], w=["junk2", "sgn"])
                    P.add("dve", lambda e: e.tensor_tensor(out=sm[:, 7:8], in0=sm[:, 3:4], in1=Wk[:, kk_:kk_ + 1], op=ALU.subtract),
                          r=["mid", "Wk"], w=["m2"])
                    P.add("dve", lambda e: e.scalar_tensor_tensor(out=sm[:, 5:6], in0=sm[:, 4:5], scalar=2.0, in1=sm[:, 6:7],
                                                                  op0=ALU.mult, op1=ALU.is_ge),
                          r=["cnt", "sgn"], w=["pw"])
                    dst = (3, "mid") if k + 1 < NIT else (1, "lo")
                    P.add("dve", lambda e: e.scalar_tensor_tensor(out=sm[:, dst[0]:dst[0] + 1], in0=sm[:, 5:6], scalar=Wk[:, k:k + 1],
                                                                  in1=sm[:, 7:8], op0=ALU.mult, op1=ALU.add),
                          r=["pw", "Wk", "m2"], w=[dst[1]])
                    if k + 1 >= NIT:
                        u_nm()
                return u
            for k in range(NIT):
                units.append(mk(k))
            return units

        def att_qk(j, i, b):
            s = j % 2
            s3 = j % 3
            ss = cnt["s"] % 3
            cnt["s"] += 1
            near = i >= j - 1
            P.add("pe", lambda e: e.matmul(Sp[ss][:], lhsT=NM[s][:, _sl(i, 128)],
                                           rhs=ident4[:].rearrange("p a q -> p (a q)"), start=True, stop=False),
                  r=[("NM", s), "ident4"], w=[("Sp", ss)])
            for hh in range(4):
                h = HB[b * 4 + hh]
                c, half = h // 2, h % 2
                P.add("pe", lambda e, hh=hh, c=c, half=half: e.matmul(
                    Sp[ss][:, _sl(hh, 128)], lhsT=kb2[:, _sl(i, 128)], rhs=qb[s3][:, 2 * c + half, :],
                    start=False, stop=(hh == 3 and not near)),
                    r=["kb2", ("qb", s3)], w=[("Sp", ss)])
            if near:
                r_ = j - i
                P.add("pe", lambda e: e.matmul(
                    Sp[ss][:], lhsT=ident4[:, 0, :], rhs=late["bT"][:, r_, 4 + 4 * b:8 + 4 * b, :].rearrange("p h q -> p (h q)"),
                    start=False, stop=True),
                    r=["ident4", "bT"], w=[("Sp", ss)])
            return ss

        def att_pv(j, i, b, ss):
            pp = cnt["p"] % 4
            cnt["p"] += 1
            P.add("act", lambda e: e.activation(out=pT[pp][:], in_=Sp[ss][:], func=AF.Exp),
                  r=[("Sp", ss)], w=[("pT", pp)])
            P.add("pe", lambda e: e.matmul(Op[b][0:65, :], lhsT=vb1[:, i, 0:65], rhs=pT[pp][:],
                                           start=(i == 0), stop=(i == j)),
                  r=["vb1", ("pT", pp)], w=[("Op", b)])

        def att_epi_a(j, b):
            P.add("dve", lambda e: e.reciprocal(out=rz[64:65, _sl(b, 512)], in_=Op[b][64:65, :]), r=[("Op", b)], w=[("rz", b)])
            P.add("act", lambda e: e.copy(out=osb[b][:], in_=Op[b][0:64, :]), r=[("Op", b)], w=[("osb", b)])

        def att_epi_b(j, b):
            s = j % 4
            zs = cnt["z"] % 3
            cnt["z"] += 1
            BC = Zp[zs]
            P.add("pe", lambda e: e.matmul(BC[0:64, :], lhsT=onesf[64:65, :], rhs=rz[64:65, _sl(b, 512)], start=True, stop=True),
                  r=["onesf", ("rz", b)], w=[("Zp", zs)])
            P.add("dve", lambda e: e.tensor_tensor(out=on[:], in0=osb[b][:], in1=BC[0:64, :], op=ALU.mult),
                  r=[("osb", b), ("Zp", zs)], w=["on"])
            P.add("dve", lambda e: e.tensor_tensor(out=mxB[b][:].rearrange("p (h q) -> p h q", h=4),
                                                   in0=on[:].rearrange("p (h q) -> p h q", h=4),
                                                   in1=gB[s][:, b:8:2, :], op=ALU.mult),
                  r=["on", ("gB", s)], w=[("mxB", b)])
            P.add("pool", lambda e: e.dma_start(
                out=T["mixT"][512:1024, _sl(j, 128)].rearrange("(h d) q -> d h q", d=64)[:, b:8:2, :],
                in_=mxB[b][:].rearrange("p (h q) -> p h q", h=4)),
                r=[("mxB", b)], dma=True)

        LA_ = 2
        pending = []

        def run_slot(t):
            Bt = bisect_units(t) if 0 <= t < NT else []
            ja = t - 1 if 0 <= t - 1 < NT else None
            steps_ = [(i, b) for i in range(ja + 1) for b in range(2)] if ja is not None else []
            info = {}
            nA = len(steps_)
            u_load, items = indexer_items(t + 1) if 0 <= t + 1 < NT else (None, [])
            nI = len(items)
            nr = max(len(Bt), 1)
            if u_load is not None:
                u_load()
            qk_done = [0]

            def qk_upto(m):
                while qk_done[0] < min(nA, m):
                    n_ = qk_done[0]
                    info[n_] = att_qk(ja, *steps_[n_])
                    qk_done[0] += 1

            for r in range(nr):
                if r < len(Bt):
                    Bt[r]()
                il, ih = r * nI // nr, (r + 1) * nI // nr
                al, ah = r * nA // nr, (r + 1) * nA // nr
                if pending and r == min(2, nr - 1):
                    pending.pop(0)()
                first = True
                x = il
                while first or x < ih:
                    batch = items[x:min(ih, x + 3)]
                    x += 3
                    for pe_, _ in batch:
                        pe_()
                    if first:
                        for n_ in range(al, ah):
                            qk_upto(n_ + LA_ + 1)
                            att_pv(ja, steps_[n_][0], steps_[n_][1], info[n_])
                        first = False
                    for _, rest_ in batch:
                        rest_()
            if ja is not None:
                att_epi_a(ja, 0)
                att_epi_a(ja, 1)
                pending.append(lambda: (att_epi_b(ja, 0), att_epi_b(ja, 1)))

        for t in range(-1, NT + 1):
            run_slot(t)
            if t == -1:
                late_setup()
        while pending:
            pending.pop(0)()
        P.emit()


def phase_outproj(P, nc, S, T, mix, wname, xin, xout, final):
    NT = S // 128
    NG = S // 512
    pfx = "o%d_" % (1 if final else 0)
    with ExitStack() as es:
        def sb(name, shape, dt):
            return es.enter_context(nc.sbuf_tensor(pfx + name, shape, dt))

        def ps(name, shape, dt):
            return es.enter_context(nc.psum_tensor(pfx + name, shape, dt))

        wo = sb("wo", [128, 8, D], BF16)
        wst = [sb("wst%d" % i, [128, D], F32) for i in range(4)]
        mt = [sb("mt%d" % i, [128, 8, 512], BF16) for i in range(2)]
        xt = [sb("xt%d" % i, [128, D], F32) for i in range(2)]
        xo = [sb("xo%d" % i, [128, D], F32) for i in range(2)]
        acc = [ps("acc%d" % i, [128, 512], F32) for i in range(4)]
        if final:
            gbc = sb("gbc", [128, D], F32)
            junk = sb("junk", [128, D], BF16)
            ssq = sb("ssq", [128, NT], F32)
            P.add("sp", lambda e: e.dma_start(out=gbc[:], in_=T["final_norm"].partition_broadcast(128)), w=["gbc"], dma=True)
        for k in range(8):
            s = k % 4
            P.add("sp", lambda e, k=k, s=s: e.dma_start(out=wst[s][:], in_=T[wname][_sl(k, 128), :]), w=[("wst", s)], dma=True)
            P.add("act", lambda e, k=k, s=s: e.copy(out=wo[:, k, :], in_=wst[s][:]), r=[("wst", s)], w=[("wo", k)])
        wall = [("wo", k) for k in range(8)]
        na = 0
        for g in range(NG):
            ms = g % 2
            P.add("sp", lambda e, g=g, ms=ms: e.dma_start(
                out=mt[ms][:], in_=T[mix].rearrange("(c p) s -> p c s", p=128)[:, :, _sl(g, 512)]), w=[("mt", ms)], dma=True)
            for t4 in range(4):
                t = g * 4 + t4
                xs = t % 2
                P.add("sp", lambda e, t=t, xs=xs: e.dma_start(out=xt[xs][:], in_=T[xin][_sl(t, 128), :]), w=[("xt", xs)], dma=True)
                for nh in range(2):
                    a = na % 4
                    na += 1
                    for k in range(8):
                        P.add("pe", lambda e, k=k, a=a, ms=ms, t4=t4, nh=nh: e.matmul(
                            acc[a][:], lhsT=mt[ms][:, k, _sl(t4, 128)], rhs=wo[:, k, _sl(nh, 512)], start=(k == 0), stop=(k == 7)),
                            r=[("mt", ms)] + wall, w=[("acc", a)])
                    P.add("dve", lambda e, a=a, xs=xs, nh=nh: e.tensor_tensor(
                        out=xo[xs][:, _sl(nh, 512)], in0=acc[a][:], in1=xt[xs][:, _sl(nh, 512)], op=ALU.add),
                        r=[("acc", a), ("xt", xs)], w=[("xo", xs, nh)])
                X = [("xo", xs, 0), ("xo", xs, 1)]
                if not final:
                    P.add("pool", lambda e, t=t, xs=xs: e.dma_start(out=T[xout][_sl(t, 128), :], in_=xo[xs][:]),
                          r=X, dma=True, grp=("xo", xs))
                else:
                    P.add("act", lambda e, t=t, xs=xs: e.activation(out=junk[:], in_=xo[xs][:], func=AF.Square,
                                                                    accum_out=ssq[:, t:t + 1]), r=X, w=["junk", ("ssq", t)])
                    P.add("act", lambda e, t=t: e.activation(out=ssq[:, t:t + 1], in_=ssq[:, t:t + 1], func=AF.Sqrt,
                                                             bias=EPS, scale=1.0 / D), r=[("ssq", t)], w=[("ssq", t)])
                    P.add("dve", lambda e, t=t: e.reciprocal(out=ssq[:, t:t + 1], in_=ssq[:, t:t + 1]),
                          r=[("ssq", t)], w=[("ssq", t)])
                    P.add("dve", lambda e, t=t, xs=xs: e.scalar_tensor_tensor(
                        out=xo[xs][:], in0=xo[xs][:], scalar=ssq[:, t:t + 1], in1=gbc[:], op0=ALU.mult, op1=ALU.mult),
                        r=X + [("ssq", t), "gbc"], w=X)
                    P.add("pool", lambda e, t=t, xs=xs: e.dma_start(out=T[xout][_sl(t, 128), :], in_=xo[xs][:]),
                          r=X, dma=True, grp=("xo", xs))
        P.emit()


def phase_proj1(P, nc, S, T):
    NT = S // 128
    NG = S // 512
    with ExitStack() as es:
        def sb(name, shape, dt):
            return es.enter_context(nc.sbuf_tensor("p1_" + name, shape, dt))

        def ps(name, shape, dt):
            return es.enter_context(nc.psum_tensor("p1_" + name, shape, dt))

        w1 = sb("w", [128, 8, L1_COLS], BF16)
        wrot = sb("wrot", [128, 8, 32], BF16)
        wqn = sb("wqn", [128, 3, 8, 128], BF16)
        wqr = sb("wqr", [128, 3, 4, 128], BF16)
        wqrr = sb("wqrr", [128, 3, 4, 128], BF16)
        wkk = sb("wkk", [128, 2, 16, 64], BF16)
        wkv = sb("wkv", [128, 2, 16, 64], BF16)
        wst = [sb("wst%d" % i, [128, 2048], F32) for i in range(2)]
        gcol = sb("g", [128, 8], F32)
        qn = sb("qn", [128, 3], F32)
        kvn = sb("kvn", [128, 2], F32)
        cs = sb("cos", [128, S], F32)
        sn = sb("sin", [128, S], F32)
        onesf = sb("1f", [128, 128], F32)
        ident = sb("id", [128, 128], BF16)
        xt = [sb("x%d" % i, [128, D], F32) for i in range(2)]
        hb = [sb("h%d" % i, [128, D], BF16) for i in range(2)]
        hT = [sb("hT%d" % i, [128, 8, 512], BF16) for i in range(2)]
        junk = sb("junk", [128, D], BF16)
        ssq = sb("ssq", [128, NT], F32)
        rs = sb("rs", [128, NT], F32)
        qlT = sb("qlT", [128, 3, 512], BF16)
        kvT = sb("kvT", [128, 2, 512], BF16)
        sqq = [sb("sqq%d" % i, [128, 512], F32) for i in range(2)]
        sqkv = sb("sqkv", [128, 2, 512], F32)
        rq = sb("rq", [128, 512], F32)
        rkv = sb("rkv", [128, 512], F32)
        rkt = sb("rkt", [128, 4], F32)
        ra2 = [sb("ra%d" % i, [128, 512], F32) for i in range(2)]
        rb2 = [sb("rb%d" % i, [128, 512], F32) for i in range(2)]
        ev = [sb("ev%d" % i, [128, 512], BF16) for i in range(4)]
        psT = [ps("psT%d" % i, [128, D], BF16) for i in range(2)]
        psF = [ps("psF%d" % i, [128, 512], F32) for i in range(3)]
        Rq = ps("Rq", [128, 512], F32)
        Rkv = ps("Rkv", [128, 512], F32)
        Rt = ps("Rt", [128, 4], F32)

        def ld(dst, src, key, **kw):
            P.add("sp", lambda e: e.dma_start(out=dst, in_=src, **kw), w=[key], dma=True)

        ld(ident[:], T["c_ident"][:, :], "ident")
        P.add("pool", lambda e: e.memset(onesf[:], 1.0), w=["onesf"])
        ld(gcol[:], T["o_norm"].rearrange("(k p) -> p k", p=128), "gcol", allow_slow_non_contiguous=True)
        ld(qn[:], T["o_q_norm"].rearrange("(k p) -> p k", p=128), "qn", allow_slow_non_contiguous=True)
        ld(kvn[:], T["o_kv_norm"].rearrange("(k p) -> p k", p=128), "kvn", allow_slow_non_contiguous=True)
        nst = [0]

        def stage(src, ncols, fn_list, rkeys):
            s_ = nst[0] % 2
            nst[0] += 1
            P.add("sp", lambda e: e.dma_start(out=wst[s_][:, 0:ncols], in_=src), w=[("wst", s_)], dma=True)
            for (q, fn, wk) in fn_list:
                P.add(q, (lambda e, fn=fn: fn(e, wst[s_])), r=[("wst", s_)] + rkeys, w=[wk])

        done_tiles = set()

        def tile_ops(t):
            if t in done_tiles:
                return
            done_tiles.add(t)
            xs = t % 2
            hs = (t // 4) % 2
            t4 = t % 4
            P.add("sp", lambda e: e.dma_start(out=xt[xs][:], in_=T["x1"][_sl(t, 128), :]), w=[("xt", xs)], dma=True)
            P.add("act", lambda e: e.activation(out=junk[:], in_=xt[xs][:], func=AF.Square, accum_out=ssq[:, t:t + 1]),
                  r=[("xt", xs)], w=["junk", ("ssq", t)])
            P.add("act", lambda e: e.activation(out=rs[:, t:t + 1], in_=ssq[:, t:t + 1], func=AF.Sqrt, bias=EPS, scale=1.0 / D),
                  r=[("ssq", t)], w=[("rs", t)])
            P.add("dve", lambda e: e.reciprocal(out=rs[:, t:t + 1], in_=rs[:, t:t + 1]), r=[("rs", t)], w=[("rs", t)])
            P.add("dve", lambda e: e.tensor_scalar(out=hb[xs][:], in0=xt[xs][:], scalar1=rs[:, t:t + 1], scalar2=None, op0=ALU.mult),
                  r=[("xt", xs), ("rs", t)], w=[("hb", xs)])
            for k in range(8):
                P.add("pe", lambda e, k=k: e.transpose(out=psT[xs][:, _sl(k, 128)], in_=hb[xs][:, _sl(k, 128)], identity=ident[:]),
                      r=[("hb", xs), "ident"], w=[("psT", xs)])
            P.add("act", lambda e: e.copy(out=hT[hs][:, :, _sl(t4, 128)], in_=psT[xs][:].rearrange("p (k t) -> p k t", k=8)),
                  r=[("psT", xs)], w=[("hT", hs)])

        tile_ops(0)
        tile_ops(1)
        for k in range(8):
            stage(T["o_w_in"][_sl(k, 128), :], L1_COLS, [
                ("dve", lambda e, st, k=k: e.tensor_scalar(out=w1[:, k, :], in0=st[:, 0:L1_COLS], scalar1=gcol[:, k:k + 1],
                                                          scalar2=None, op0=ALU.mult), ("w1", k)),
                ("dve", lambda e, st, k=k: e.tensor_scalar(out=wrot[:, k, 0:16], in0=st[:, 656:672], scalar1=gcol[:, k:k + 1],
                                                          scalar2=-1.0, op0=ALU.mult, op1=ALU.mult), ("wrot", k)),
                ("dve", lambda e, st, k=k: e.tensor_scalar(out=wrot[:, k, 16:32], in0=st[:, 640:656], scalar1=gcol[:, k:k + 1],
                                                          scalar2=None, op0=ALU.mult), ("wrot", k)),
            ], ["gcol"])
        for c in range(3):
            def v3(st):
                return st[:, 0:1536].rearrange("p (h j) -> p h j", j=96)
            stage(T["o_w_uq"][_sl(c, 128), :], 1536, [
                ("dve", lambda e, st, c=c: e.tensor_scalar(out=wqn[:, c, :, :].rearrange("p a (b j) -> p (a b) j", j=64),
                                                          in0=v3(st)[:, :, 0:64], scalar1=qn[:, c:c + 1],
                                                          scalar2=None, op0=ALU.mult), ("wqn", c)),
                ("dve", lambda e, st, c=c: e.tensor_scalar(out=wqr[:, c, :, :].rearrange("p a (b j) -> p (a b) j", j=32),
                                                          in0=v3(st)[:, :, 64:96], scalar1=qn[:, c:c + 1],
                                                          scalar2=None, op0=ALU.mult), ("wqr", c)),
                ("dve", lambda e, st, c=c: e.tensor_scalar(out=wqrr[:, c, :, :].rearrange("p a (b j) -> p (a b) j", j=32)[:, :, 0:16],
                                                          in0=v3(st)[:, :, 80:96], scalar1=qn[:, c:c + 1],
                                                          scalar2=-1.0, op0=ALU.mult, op1=ALU.mult), ("wqrr", c)),
                ("dve", lambda e, st, c=c: e.tensor_scalar(out=wqrr[:, c, :, :].rearrange("p a (b j) -> p (a b) j", j=32)[:, :, 16:32],
                                                          in0=v3(st)[:, :, 64:80], scalar1=qn[:, c:c + 1],
                                                          scalar2=None, op0=ALU.mult), ("wqrr", c)),
            ], ["qn"])
        for c in range(2):
            def v4(st):
                return st[:, 0:2048].rearrange("p (h j) -> p h j", j=128)
            stage(T["o_w_ukv"][_sl(c, 128), :], 2048, [
                ("dve", lambda e, st, c=c: e.tensor_scalar(out=wkk[:, c, :, :], in0=v4(st)[:, :, 0:64], scalar1=kvn[:, c:c + 1],
                                                          scalar2=None, op0=ALU.mult), ("wkk", c)),
                ("dve", lambda e, st, c=c: e.tensor_scalar(out=wkv[:, c, :, :], in0=v4(st)[:, :, 64:128], scalar1=kvn[:, c:c + 1],
                                                          scalar2=None, op0=ALU.mult), ("wkv", c)),
            ], ["kvn"])
        for i_ in range(4):
            ld(cs[32 * i_:32 * i_ + 32, :], T["c_cos"][:, :], "cos")
            ld(sn[32 * i_:32 * i_ + 32, :], T["c_sin"][:, :], "sin")
        W1 = [("w1", k) for k in range(8)]
        WROT = [("wrot", k) for k in range(8)]
        WQN = [("wqn", c) for c in range(3)]
        WQR = [("wqr", c) for c in range(3)]
        WQRR = [("wqrr", c) for c in range(3)]
        WKK = [("wkk", c) for c in range(2)]
        WKV = [("wkv", c) for c in range(2)]
        cnt = dict(f=0, e=0, q=0)

        def nf():
            cnt["f"] += 1
            return (cnt["f"] - 1) % 3

        def ne():
            cnt["e"] += 1
            return (cnt["e"] - 1) % 4

        def mm8(pf, M, lhs_fn, rkeys, hs):
            for k in range(8):
                P.add("pe", lambda e, k=k: e.matmul(psF[pf][0:M, :], lhsT=lhs_fn(k), rhs=hT[hs][:, k, :], start=(k == 0), stop=(k == 7)),
                      r=[("hT", hs), rkeys[k]], w=[("psF", pf)])

        def store(q_tile, dst):
            P.add("pool", lambda e: e.dma_start(out=dst, in_=q_tile[0]), r=[q_tile[1]], dma=True, grp=q_tile[1])

        def group(g):
            hs = g % 2
            G = slice(g * 512, (g + 1) * 512)
            for t4 in range(4):
                tile_ops(g * 4 + t4)
            for c in range(3):
                def qlat(c=c):
                    pf = nf()
                    mm8(pf, 128, lambda k: w1[:, k, c * 128:(c + 1) * 128], W1, hs)
                    sq_ = cnt["q"] % 2
                    cnt["q"] += 1
                    P.add("act", lambda e: e.copy(out=qlT[:, c, :], in_=psF[pf][:]), r=[("psF", pf)], w=[("qlT", c)])
                    P.add("act", lambda e: e.activation(out=sqq[sq_][:], in_=psF[pf][:], func=AF.Square), r=[("psF", pf)], w=[("sqq", sq_)])
                    P.add("pe", lambda e: e.matmul(Rq[:], lhsT=onesf[:], rhs=sqq[sq_][:], start=(c == 0), stop=(c == 2)),
                          r=["onesf", ("sqq", sq_)], w=["Rq"])
                qlat()
            P.add("act", lambda e: e.activation(out=rq[:], in_=Rq[:], func=AF.Sqrt, bias=EPS, scale=1.0 / 384), r=["Rq"], w=["rq"])
            P.add("dve", lambda e: e.reciprocal(out=rq[:], in_=rq[:]), r=["rq"], w=["rq"])
            for c in range(2):
                def kvlat(c=c):
                    pf = nf()
                    mm8(pf, 128, lambda k: w1[:, k, 384 + c * 128:384 + (c + 1) * 128], W1, hs)
                    P.add("act", lambda e: e.copy(out=kvT[:, c, :], in_=psF[pf][:]), r=[("psF", pf)], w=[("kvT", c)])
                    P.add("act", lambda e: e.activation(out=sqkv[:, c, :], in_=psF[pf][:], func=AF.Square), r=[("psF", pf)], w=[("sqkv", c)])
                    P.add("pe", lambda e: e.matmul(Rkv[:], lhsT=onesf[:], rhs=sqkv[:, c, :], start=(c == 0), stop=(c == 1)),
                          r=["onesf", ("sqkv", c)], w=["Rkv"])
                kvlat()
            for t4 in range(4):
                for c in range(2):
                    P.add("pe", lambda e, t4=t4, c=c: e.matmul(Rt[:, t4:t4 + 1], lhsT=sqkv[:, c, _sl(t4, 128)], rhs=onesf[:, 0:1],
                                                               start=(c == 0), stop=(c == 1)),
                          r=["onesf", ("sqkv", 0), ("sqkv", 1)], w=["Rt"])
            P.add("act", lambda e: e.activation(out=rkv[:], in_=Rkv[:], func=AF.Sqrt, bias=EPS, scale=1.0 / 256), r=["Rkv"], w=["rkv"])
            P.add("dve", lambda e: e.reciprocal(out=rkv[:], in_=rkv[:]), r=["rkv"], w=["rkv"])
            P.add("act", lambda e: e.activation(out=rkt[:], in_=Rt[:], func=AF.Sqrt, bias=EPS, scale=1.0 / 256), r=["Rt"], w=["rkt"])
            P.add("dve", lambda e: e.reciprocal(out=rkt[:], in_=rkt[:]), r=["rkt"], w=["rkt"])
            def krope():
                ra, rb = ra2[0], rb2[0]
                pa, pb = nf(), nf()
                mm8(pa, 32, lambda k: w1[:, k, 640:672], W1, hs)
                mm8(pb, 32, lambda k: wrot[:, k, :], WROT, hs)
                e_ = ne()
                P.add("dve", lambda e: e.tensor_tensor(out=ra[0:32, :], in0=psF[pa][0:32, :], in1=cs[0:32, G], op=ALU.mult),
                      r=[("psF", pa), "cos"], w=[("ra", 0)])
                P.add("dve", lambda e: e.tensor_tensor(out=rb[0:32, :], in0=psF[pb][0:32, :], in1=sn[0:32, G], op=ALU.mult),
                      r=[("psF", pb), "sin"], w=[("rb", 0)])
                P.add("dve", lambda e: e.tensor_tensor(out=ev[e_][0:32, :], in0=ra[0:32, :], in1=rb[0:32, :], op=ALU.add),
                      r=[("ra", 0), ("rb", 0)], w=[("ev", e_)])
                store((ev[e_][0:32, :], ("ev", e_)), T["kTr"][:, G])
            krope()
            for c in range(8):
                def gate(c=c):
                    pf = nf()
                    mm8(pf, 128, lambda k: w1[:, k, 672 + c * 128:672 + (c + 1) * 128], W1, hs)
                    e_ = ne()
                    P.add("act", lambda e: e.activation(out=ev[e_][:], in_=psF[pf][:], func=AF.Silu), r=[("psF", pf)], w=[("ev", e_)])
                    store((ev[e_][:], ("ev", e_)), T["g1T"][_sl(c, 128), G])
                gate()
            QL = [("qlT", c) for c in range(3)]
            KV = [("kvT", c) for c in range(2)]
            for qd in range(4):
                def qrope(qd=qd):
                    ra, rb = ra2[qd % 2], rb2[qd % 2]
                    rak, rbk = ("ra", qd % 2), ("rb", qd % 2)
                    pa, pb = nf(), nf()
                    for c in range(3):
                        P.add("pe", lambda e, c=c: e.matmul(psF[pa][:], lhsT=wqr[:, c, qd, :], rhs=qlT[:, c, :], start=(c == 0), stop=(c == 2)),
                              r=QL + WQR, w=[("psF", pa)])
                    for c in range(3):
                        P.add("pe", lambda e, c=c: e.matmul(psF[pb][:], lhsT=wqrr[:, c, qd, :], rhs=qlT[:, c, :], start=(c == 0), stop=(c == 2)),
                              r=QL + WQRR, w=[("psF", pb)])
                    e_ = ne()
                    P.add("dve", lambda e: e.tensor_tensor(out=ra[:], in0=psF[pa][:], in1=cs[:, G], op=ALU.mult),
                          r=[("psF", pa), "cos"], w=[rak])
                    P.add("dve", lambda e: e.tensor_tensor(out=rb[:], in0=psF[pb][:], in1=sn[:, G], op=ALU.mult),
                          r=[("psF", pb), "sin"], w=[rbk])
                    P.add("dve", lambda e: e.tensor_tensor(out=ra[:], in0=ra[:], in1=rb[:], op=ALU.add), r=[rak, rbk], w=[rak])
                    P.add("dve", lambda e: e.tensor_tensor(out=ev[e_][:], in0=ra[:], in1=rq[:], op=ALU.mult),
                          r=[rak, "rq"], w=[("ev", e_)])
                    for i_ in range(4):
                        store((ev[e_][32 * i_:32 * i_ + 32, :], ("ev", e_)), T["qT"][4 * qd + i_, 64:96, G])
                qrope()
            for pr in range(8):
                def qnope(pr=pr):
                    pf = nf()
                    for c in range(3):
                        P.add("pe", lambda e, c=c: e.matmul(psF[pf][:], lhsT=wqn[:, c, pr, :], rhs=qlT[:, c, :], start=(c == 0), stop=(c == 2)),
                              r=QL + WQN, w=[("psF", pf)])
                    e_ = ne()
                    P.add("dve", lambda e: e.tensor_tensor(out=ev[e_][:], in0=psF[pf][:], in1=rq[:], op=ALU.mult),
                          r=[("psF", pf), "rq"], w=[("ev", e_)])
                    for i_ in range(2):
                        store((ev[e_][64 * i_:64 * i_ + 64, :], ("ev", e_)), T["qT"][2 * pr + i_, 0:64, G])
                qnope()

                def kpair(pr=pr):
                    pf = nf()
                    for c in range(2):
                        P.add("pe", lambda e, c=c: e.matmul(psF[pf][:], lhsT=wkk[:, c, 2 * pr:2 * pr + 2, :].rearrange("p h j -> p (h j)"),
                                                            rhs=kvT[:, c, :], start=(c == 0), stop=(c == 1)),
                              r=KV + WKK, w=[("psF", pf)])
                    e_ = ne()
                    P.add("dve", lambda e: e.tensor_tensor(out=ev[e_][:], in0=psF[pf][:], in1=rkv[:], op=ALU.mult),
                          r=[("psF", pf), "rkv"], w=[("ev", e_)])
                    for i_ in range(2):
                        store((ev[e_][64 * i_:64 * i_ + 64, :], ("ev", e_)), T["kT"][2 * pr + i_, :, G])
                kpair()
            for t4 in range(4):
                for half in range(2):
                    def vtile(t4=t4, half=half):
                        pf = nf()
                        for c in range(2):
                            P.add("pe", lambda e, c=c: e.matmul(
                                psF[pf][:], lhsT=kvT[:, c, _sl(t4, 128)],
                                rhs=wkv[:, c, 8 * half:8 * half + 8, :].rearrange("p h j -> p (h j)"), start=(c == 0), stop=(c == 1)),
                                r=KV + WKV, w=[("psF", pf)])
                        e_ = ne()
                        P.add("act", lambda e: e.activation(out=ev[e_][:], in_=psF[pf][:], func=AF.Copy, scale=rkt[:, t4:t4 + 1]),
                              r=[("psF", pf), "rkt"], w=[("ev", e_)])
                        store((ev[e_][:], ("ev", e_)), T["v1"][g * 512 + t4 * 128:g * 512 + (t4 + 1) * 128, _sl(half, 512)])
                    vtile()

        for g in range(NG):
            group(g)
        P.emit()


def phase_mla(P, nc, S, T):
    NT = S // 128
    NG = S // 512
    SCALE = 96 ** -0.5
    with ExitStack() as es:
        def sb(name, shape, dt):
            return es.enter_context(nc.sbuf_tensor("ml_" + name, shape, dt))

        def ps(name, shape, dt):
            return es.enter_context(nc.psum_tensor("ml_" + name, shape, dt))

        ident = sb("id", [128, 128], BF16)
        onesf = sb("1f", [128, 64], F32)
        cmf = sb("cmf", [128, 128], F32)
        cmb = sb("cmb", [128, 128], BF16)
        kt = [sb("kt%d" % i, [96, S], BF16) for i in range(2)]
        vt = [sb("vt%d" % i, [128, NT, 66], BF16) for i in range(2)]
        qb = [sb("q%d" % i, [96, 512], BF16) for i in range(2)]
        gb = [sb("g%d" % i, [64, 512], BF16) for i in range(3)]
        pT = [sb("pT%d" % i, [128, 512], BF16) for i in range(4)]
        rz2 = [sb("rz%d" % i, [128, 512], F32) for i in range(2)]
        osb2 = [sb("osb%d" % i, [64, 512], F32) for i in range(2)]
        on = sb("on", [64, 512], F32)
        mx = [sb("mx%d" % i, [64, 512], BF16) for i in range(2)]
        Sp = [ps("S%d" % i, [128, 512], F32) for i in range(3)]
        Op = [ps("O%d" % i, [128, 512], F32) for i in range(2)]
        BC = ps("BC", [128, 512], F32)

        P.add("sp", lambda e: e.dma_start(out=ident[:], in_=T["c_ident"][:, :]), w=["ident"], dma=True)
        P.add("sp", lambda e: e.dma_start(out=cmf[:], in_=T["c_mask"][:, :]), w=["cmf"], dma=True)
        P.add("dve", lambda e: e.tensor_copy(out=cmb[:], in_=cmf[:]), r=["cmf"], w=["cmb"])
        P.add("pool", lambda e: e.memset(onesf[:], 1.0), w=["onesf"])
        for i_ in range(2):
            P.add("pool", lambda e, i_=i_: e.memset(vt[i_][:, :, 64:66], 1.0), w=[("vt1", i_)])

        steps = []
        for hd in range(16):
            for J in range(NG):
                last = 4 * J + 3
                for i in range(last + 1):
                    steps.append((hd, J, i, last))

        def loads(hd, J, i):
            if J == 0 and i == 0:
                s1 = hd % 2
                P.add("sp", lambda e: e.dma_start(out=kt[s1][64:96, :], in_=T["kTr"][:, :]), w=[("kt", s1)], dma=True)
                P.add("sp", lambda e: e.dma_start(out=kt[s1][0:64, :], in_=T["kT"][hd, :, :]), w=[("kt", s1)], dma=True)
                for c8 in range(0, NT, 8):
                    n8 = min(8, NT - c8)
                    P.add("sp", lambda e, c8=c8, n8=n8: e.dma_start(
                        out=vt[s1][:, c8:c8 + n8, 0:64],
                        in_=T["v1"][c8 * 128:(c8 + n8) * 128, _sl(hd, 64)].rearrange("(t p) d -> p t d", p=128)),
                        w=[("vt", s1, c8 // 8)], dma=True)
            if i == 0:
                s2 = (hd * NG + J) % 2
                P.add("sp", lambda e: e.dma_start(out=qb[s2][:], in_=T["qT"][hd, :, _sl(J, 512)]), w=[("qb", s2)], dma=True)
                g3 = (hd * NG + J) % 3
                P.add("sp", lambda e: e.dma_start(out=gb[g3][:], in_=T["g1T"][_sl(hd, 64), _sl(J, 512)]), w=[("gb", g3)], dma=True)

        def qk(n):
            hd, J, i, last = steps[n]
            loads(hd, J, i)
            c0 = max(0, i - 4 * J) * 128
            sbk = n % 3
            qs = (hd * NG + J) % 2
            diag = i >= 4 * J
            P.add("pe", lambda e: e.matmul(Sp[sbk][:, c0:512], lhsT=kt[hd % 2][:, _sl(i, 128)], rhs=qb[qs][:, c0:512],
                                           start=True, stop=not diag),
                  r=[("kt", hd % 2), ("qb", qs)], w=[("Sp", sbk)])
            if diag:
                P.add("pe", lambda e: e.matmul(Sp[sbk][:, c0:c0 + 128], lhsT=ident[:], rhs=cmb[:], start=False, stop=True),
                      r=["ident", "cmb"], w=[("Sp", sbk)])

        def pv(n):
            hd, J, i, last = steps[n]
            c0 = max(0, i - 4 * J) * 128
            sbk = n % 3
            pt = n % 4
            osl = (hd * NG + J) % 2
            vs = hd % 2
            P.add("act", lambda e: e.activation(out=pT[pt][:, c0:512], in_=Sp[sbk][:, c0:512], func=AF.Exp, scale=SCALE),
                  r=[("Sp", sbk)], w=[("pT", pt)])
            P.add("pe", lambda e: e.matmul(Op[osl][0:65, c0:512], lhsT=vt[vs][:, i, 0:65], rhs=pT[pt][:, c0:512],
                                           start=(i == 0), stop=(i == last)),
                  r=[("vt", vs, i // 8), ("vt1", vs), ("pT", pt)], w=[("Op", osl)])
            if i != last:
                return
            gsl = (hd * NG + J) % 2
            rz, osb = rz2[gsl], osb2[gsl]
            g3 = (hd * NG + J) % 3
            P.add("dve", lambda e: e.reciprocal(out=rz[64:65, :], in_=Op[osl][64:65, :]), r=[("Op", osl)], w=[("rz", gsl)])
            P.add("act", lambda e: e.copy(out=osb[:], in_=Op[osl][0:64, :]), r=[("Op", osl)], w=[("osb", gsl)])

            def epi_b():
                P.add("pe", lambda e: e.matmul(BC[0:64, :], lhsT=onesf[64:65, :], rhs=rz[64:65, :], start=True, stop=True),
                      r=["onesf", ("rz", gsl)], w=["BC"])
                P.add("dve", lambda e: e.tensor_tensor(out=on[:], in0=osb[:], in1=BC[0:64, :], op=ALU.mult),
                      r=[("osb", gsl), "BC"], w=["on"])
                P.add("dve", lambda e: e.tensor_tensor(out=mx[gsl][:], in0=on[:], in1=gb[g3][:], op=ALU.mult),
                      r=["on", ("gb", g3)], w=[("mx", gsl)])
                P.add("pool", lambda e: e.dma_start(out=T["mix1T"][_sl(hd, 64), _sl(J, 512)], in_=mx[gsl][:]),
                      r=[("mx", gsl)], dma=True)
            deferred.append((n + 6, epi_b))

        deferred = []

        def run_deferred(n):
            while deferred and deferred[0][0] <= n:
                deferred.pop(0)[1]()

        LA = 2
        for n in range(min(LA, len(steps))):
            qk(n)
        for n in range(len(steps)):
            if n + LA < len(steps):
                qk(n + LA)
            run_deferred(n)
            pv(n)
        run_deferred(len(steps) + 10)
        P.emit()


SCRATCH0 = dict(
    qaT=([512, None], BF16), kaT=([512, None], BF16), va=([None, 512], BF16),
    qbT=([512, None], BF16), kbT=([64, None], BF16), vb=([None, 64], BF16),
    qiT=([512, None], BF16), kiT=([64, None], BF16), wi=([None, 8], F32),
    gT=([1024, None], BF16), mixT=([1024, None], BF16), x1=([None, 1024], F32),
    qT=([16, 96, None], BF16), kT=([16, 64, None], BF16), kTr=([32, None], BF16), v1=([None, 1024], BF16),
    g1T=([1024, None], BF16), mix1T=([1024, None], BF16),
)


def build(S, topk, phases, outs):
    nc = bass.Bass("TRN2", target_bir_lowering=False)
    T = {}

    def din(name, shape, dt=F32):
        T[name] = nc.dram_tensor(name, shape, dt, kind="ExternalInput").ap()

    din("x", [S, D])
    din("e_norm", [D])
    din("e_w_in", [D, L0_COLS])
    din("c_ident", [128, 128], BF16)
    din("c_gath", [128, 2, 12, 128])
    din("c_mask", [128, 128])
    din("c_pow2", [NIT])
    din("rel_bias", [32, 12])
    for nm in ("e_lam_q1", "e_lam_k1", "e_lam_q2", "e_lam_k2"):
        din(nm, [64])
    din("e_subln", [128])
    din("e_w_o", [D, D])
    din("o_norm", [D])
    din("o_w_in", [D, L1_COLS])
    din("o_q_norm", [384])
    din("o_w_uq", [384, 1536])
    din("o_kv_norm", [256])
    din("o_w_ukv", [256, 2048])
    din("o_w_o", [D, D])
    din("final_norm", [D])
    din("c_cos", [32, S])
    din("c_sin", [32, S])
    for name, (shape, dt) in SCRATCH0.items():
        shp = [S if v is None else v for v in shape]
        kind = "ExternalOutput" if name in outs else "Internal"
        T[name] = nc.dram_tensor(name, shp, dt, kind=kind).ap()
    T["out"] = nc.dram_tensor("out", [S, D], F32, kind="ExternalOutput").ap()
    if "dbg_nm" in outs:
        T["dbg_nm"] = nc.dram_tensor("dbg_nm", [S, S], BF16, kind="ExternalOutput").ap()
        T["dbg_acc"] = nc.dram_tensor("dbg_acc", [S, S], F32, kind="ExternalOutput").ap()
    with ExitStack() as es:
        P = Prog(nc, es)
        if "proj0" in phases:
            phase_proj0(P, nc, S, T)
        if "diff" in phases:
            phase_diff(P, nc, S, T)
        if "dsa" in phases:
            phase_dsa(P, nc, S, T, topk)
        if "op0" in phases:
            phase_outproj(P, nc, S, T, "mixT", "e_w_o", "x", "x1", False)
        if "proj1" in phases:
            phase_proj1(P, nc, S, T)
        if "mla" in phases:
            phase_mla(P, nc, S, T)
        if "op1" in phases:
            phase_outproj(P, nc, S, T, "mix1T", "o_w_o", "x1", "out", True)
    return nc


def t5_bucket_np(rel):
    nb = 16
    ret = np.where(rel > 0, nb, 0)
    n = np.abs(rel)
    max_exact = nb // 2
    n_f = np.maximum(n, 1).astype(np.float32)
    large = max_exact + (np.log(n_f / max_exact) / math.log(128 / max_exact) * (nb - max_exact)).astype(np.int32)
    large = np.minimum(large, nb - 1)
    return ret + np.where(n < max_exact, n, large)


def host_consts(rel_bias):
    kk = np.arange(128)[:, None]
    qq = np.arange(128)[None, :]
    gath = np.zeros((2, 128, 12, 128), np.float32)
    for r in range(2):
        idx = t5_bucket_np((kk - r * 128) - qq)
        gath[r] = np.transpose(np.asarray(rel_bias)[idx], (0, 2, 1))
    cmask = np.where((kk // 64) <= (qq // 64), 0.0, NEG).astype(np.float32)
    pow2 = (0.5 ** np.arange(1, NIT + 1)).astype(np.float32)
    gath = np.ascontiguousarray(np.transpose(gath, (1, 0, 2, 3)))
    return dict(c_ident=np.eye(128).astype(ml_dtypes.bfloat16), c_gath=gath, c_mask=cmask, c_pow2=pow2)


def rope_consts(S):
    inv = (np.float32(10000.0) ** (-np.arange(0, 32, 2, dtype=np.float32) / np.float32(32))).astype(np.float32)
    ang = (np.arange(S, dtype=np.float32)[:, None] * inv[None, :]).astype(np.float32)
    c = np.cos(ang).astype(np.float32).T
    s_ = np.sin(ang).astype(np.float32).T
    return dict(c_cos=np.ascontiguousarray(np.concatenate([c, c], 0)), c_sin=np.ascontiguousarray(np.concatenate([s_, s_], 0)))


ALL_PHASES = ("proj0", "diff", "dsa", "op0", "proj1", "mla", "op1")
S_FULL = 4096
TOPK = 256


def kernel(x, rel_bias, e_norm, e_w_in, e_lam_q1, e_lam_k1, e_lam_q2, e_lam_k2, e_subln, e_w_o,
           o_norm, o_w_in, o_q_norm, o_w_uq, o_kv_norm, o_w_ukv, o_w_o, final_norm):
    f = lambda a: np.ascontiguousarray(np.asarray(a, dtype=np.float32))
    x = f(x)
    B = x.shape[0]
    shared = dict(rel_bias=f(rel_bias), e_norm=f(e_norm)[0], e_w_in=f(e_w_in)[0], e_lam_q1=f(e_lam_q1)[0],
                  e_lam_k1=f(e_lam_k1)[0], e_lam_q2=f(e_lam_q2)[0], e_lam_k2=f(e_lam_k2)[0], e_subln=f(e_subln)[0],
                  e_w_o=f(e_w_o)[0], o_norm=f(o_norm)[0], o_w_in=f(o_w_in)[0], o_q_norm=f(o_q_norm)[0],
                  o_w_uq=f(o_w_uq)[0], o_kv_norm=f(o_kv_norm)[0], o_w_ukv=f(o_w_ukv)[0], o_w_o=f(o_w_o)[0],
                  final_norm=f(final_norm))
    shared.update(host_consts(shared["rel_bias"]))
    shared.update(rope_consts(S_FULL))
    nc = build(S_FULL, TOPK, ALL_PHASES, ())
    in_maps = [dict(shared, x=x[b]) for b in range(B)]
    res = run_bass_kernel_spmd(nc, in_maps, core_ids=list(range(B)))
    return np.stack([np.asarray(r["out"], dtype=np.float32) for r in res.results], axis=0)
```

```python
import math
from contextlib import ExitStack
import numpy as np
import ml_dtypes
import concourse.bass as bass
import concourse.mybir as mybir
from concourse.bass_utils import run_bass_kernel_spmd

F32 = mybir.dt.float32
BF16 = mybir.dt.bfloat16
AF = mybir.ActivationFunctionType
ALU = mybir.AluOpType
AX = mybir.AxisListType

D = 1024
EPS = 1e-6
L0_COLS = 3784
L1_COLS = 1696
NEG = -30000.0

SAME_ENGINE_SYNC = {"act", "dve", "pool"}


class Prog:
    QUEUES = ("pe", "act", "dve", "pool", "sp")
    NDSEM = 44
    NSP = 30

    def __init__(self, nc, es):
        self.nc = nc
        self.sem = {}
        self.count = {}
        for q in self.QUEUES:
            self.sem[q] = es.enter_context(nc.semaphore("s_" + q))
            self.count[q] = 0
        self.dsem = [es.enter_context(nc.semaphore("sd%d" % i)) for i in range(self.NDSEM)]
        self.dcount = [0] * self.NDSEM
        self.reset()

    def reset(self):
        self.ops = {q: [] for q in self.QUEUES}
        self.last_w = {}
        self.readers = {}
        self.nid = {}
        self.gmap = {}
        self.nsp = 0
        self.npool = 0

    def add(self, q, fn, r=(), w=(), dma=False, grp=None):
        if dma:
            if grp is None:
                grp = w[0] if len(w) else r[0]
            if grp not in self.gmap:
                if q == "pool":
                    self.gmap[grp] = self.NDSEM - 1 - self.npool
                    self.npool += 1
                else:
                    self.gmap[grp] = self.nsp
                    self.nsp += 1
                assert self.nsp <= self.NSP and self.npool <= self.NDSEM - self.NSP, "too many DMA groups"
            ident = ("g", grp)
        else:
            ident = q
        idx = self.nid.get(ident, 0)
        self.nid[ident] = idx + 1
        deps = {}

        def dep(e, i):
            if e == ident and not dma and q not in SAME_ENGINE_SYNC:
                return
            if deps.get(e, -1) < i:
                deps[e] = i

        for k in r:
            if k in self.last_w:
                dep(*self.last_w[k])
        for k in w:
            if k in self.last_w:
                dep(*self.last_w[k])
            for e, i in self.readers.get(k, {}).items():
                dep(e, i)
        for k in r:
            self.readers.setdefault(k, {})[ident] = idx
        for k in w:
            self.last_w[k] = (ident, idx)
            self.readers[k] = {}
        self.ops[q].append(dict(fn=fn, ident=ident, idx=idx, deps=deps, dma=dma, sig=dma))

    def _semof(self, ident):
        if isinstance(ident, tuple):
            return self.dsem[self.gmap[ident[1]]]
        return self.sem[ident]

    def emit(self):
        nc = self.nc
        byid = {}
        for q in self.QUEUES:
            for op in self.ops[q]:
                byid.setdefault(op["ident"], {})[op["idx"]] = op
        for q in self.QUEUES:
            for op in self.ops[q]:
                for e, i in op["deps"].items():
                    byid[e][i]["sig"] = True
        val = {}
        final = {}
        for ident, d in byid.items():
            isd = isinstance(ident, tuple)
            c = self.dcount[self.gmap[ident[1]]] if isd else self.count[ident]
            v = {}
            for i in range(len(d)):
                if d[i]["sig"]:
                    c += 16 if isd else 1
                v[i] = c
            val[ident] = v
            final[ident] = c
            if isd:
                self.dcount[self.gmap[ident[1]]] = c
            else:
                self.count[ident] = c
        with nc.Block() as block:
            engs = dict(pe=block.tensor, act=block.scalar, dve=block.vector, pool=block.gpsimd, sp=block.sync)
            for q in self.QUEUES:
                ops = self.ops[q]
                if not ops:
                    continue

                def body(e, ops=ops, q=q):
                    waited = {}
                    for op in ops:
                        for ident, i in op["deps"].items():
                            v = val[ident][i]
                            if waited.get(ident, -1) < v:
                                e.wait_ge(self._semof(ident), v)
                                waited[ident] = v
                        ins = op["fn"](e)
                        if op["sig"]:
                            ins.then_inc(self._semof(op["ident"]), 16 if op["dma"] else 1)
                    for ident in {o["ident"]: 1 for o in ops if o["dma"]}:
                        if waited.get(ident, -1) < final[ident]:
                            e.wait_ge(self._semof(ident), final[ident])

                engs[q](body)
        self.reset()


def _sl(i, n):
    return slice(i * n, (i + 1) * n)


def phase_proj0(P, nc, S, T):
    NT = S // 128
    NG = S // 512
    with ExitStack() as es:
        def sb(name, shape, dt):
            return es.enter_context(nc.sbuf_tensor(name, shape, dt))

        def ps(name, shape, dt):
            return es.enter_context(nc.psum_tensor(name, shape, dt))

        wsb = sb("p0_w", [128, 8, L0_COLS], BF16)
        wkk = sb("p0_wkk", [128, 8, 128], BF16)
        wvw = sb("p0_wvw", [128, 8, 72], BF16)
        wst = [sb("p0_wst%d" % i, [128, L0_COLS], F32) for i in range(2)]
        gcol = sb("p0_g", [128, 8], F32)
        ident = sb("p0_id", [128, 128], BF16)
        xt = [sb("p0_x%d" % i, [128, D], F32) for i in range(2)]
        hb = [sb("p0_h%d" % i, [128, D], BF16) for i in range(2)]
        hT = [sb("p0_hT%d" % i, [128, 8, 512], BF16) for i in range(2)]
        junk = sb("p0_junk", [128, D], BF16)
        ssq = sb("p0_ssq", [128, NT], F32)
        rs = sb("p0_rs", [128, NT], F32)
        ev = [sb("p0_ev%d" % i, [128, 512], BF16) for i in range(4)]
        evw = [sb("p0_evw%d" % i, [128, 8], F32) for i in range(2)]
        evb = [sb("p0_evb%d" % i, [128, 64], BF16) for i in range(2)]
        psT = [ps("p0_psT%d" % i, [128, D], BF16) for i in range(2)]
        psF = [ps("p0_psF%d" % i, [128, 512], F32) for i in range(4)]
        psW = [ps("p0_psW%d" % i, [128, 72], F32) for i in range(2)]

        P.add("sp", lambda e: e.dma_start(out=ident[:], in_=T["c_ident"][:, :]), w=["ident"], dma=True)
        P.add("sp", lambda e: e.dma_start(out=gcol[:], in_=T["e_norm"].rearrange("(k p) -> p k", p=128),
                                          allow_slow_non_contiguous=True),
              w=["gcol"], dma=True)
        done_tiles = set()

        def tile_ops(t):
            if t in done_tiles:
                return
            done_tiles.add(t)
            xs = t % 2
            hs = (t // 4) % 2
            t4 = t % 4
            P.add("sp", lambda e, t=t, xs=xs: e.dma_start(out=xt[xs][:], in_=T["x"][_sl(t, 128), :]),
                  w=[("xt", xs)], dma=True)
            P.add("act", lambda e, t=t, xs=xs: e.activation(out=junk[:], in_=xt[xs][:], func=AF.Square,
                                                            accum_out=ssq[:, t:t + 1]),
                  r=[("xt", xs)], w=["junk", ("ssq", t)])
            P.add("act", lambda e, t=t: e.activation(out=rs[:, t:t + 1], in_=ssq[:, t:t + 1], func=AF.Sqrt,
                                                     bias=EPS, scale=1.0 / D),
                  r=[("ssq", t)], w=[("rs", t)])
            P.add("dve", lambda e, t=t: e.reciprocal(out=rs[:, t:t + 1], in_=rs[:, t:t + 1]),
                  r=[("rs", t)], w=[("rs", t)])
            P.add("dve", lambda e, t=t, xs=xs: e.tensor_scalar(out=hb[xs][:], in0=xt[xs][:], scalar1=rs[:, t:t + 1],
                                                               scalar2=None, op0=ALU.mult),
                  r=[("xt", xs), ("rs", t)], w=[("hb", xs)])
            for k in range(8):
                P.add("pe", lambda e, k=k, xs=xs: e.transpose(out=psT[xs][:, _sl(k, 128)], in_=hb[xs][:, _sl(k, 128)],
                                                              identity=ident[:]),
                      r=[("hb", xs), "ident"], w=[("psT", xs)])
            P.add("act", lambda e, xs=xs, hs=hs, t4=t4: e.copy(
                out=hT[hs][:, :, _sl(t4, 128)], in_=psT[xs][:].rearrange("p (k t) -> p k t", k=8)),
                r=[("psT", xs)], w=[("hT", hs)])

        tile_ops(0)
        tile_ops(1)
        for k in range(8):
            s = k % 2
            P.add("sp", lambda e, k=k, s=s: e.dma_start(out=wst[s][:], in_=T["e_w_in"][_sl(k, 128), :]),
                  w=[("wst", s)], dma=True)
            P.add("dve", lambda e, k=k, s=s: e.tensor_scalar(out=wsb[:, k, :], in0=wst[s][:], scalar1=gcol[:, k:k + 1],
                                                            scalar2=None, op0=ALU.mult),
                  r=[("wst", s), "gcol"], w=[("w", k)])
            P.add("pool", lambda e, k=k: e.tensor_copy(out=wkk[:, k, 0:64], in_=wsb[:, k, 2048:2112]),
                  r=[("w", k)], w=[("wkk", k)])
            P.add("pool", lambda e, k=k: e.tensor_copy(out=wkk[:, k, 64:128], in_=wsb[:, k, 2688:2752]),
                  r=[("w", k)], w=[("wkk", k)])
            P.add("pool", lambda e, k=k: e.tensor_copy(out=wvw[:, k, 0:64], in_=wsb[:, k, 2112:2176]),
                  r=[("w", k)], w=[("wvw", k)])
            P.add("pool", lambda e, k=k: e.tensor_copy(out=wvw[:, k, 64:72], in_=wsb[:, k, 2752:2760]),
                  r=[("w", k)], w=[("wvw", k)])
        wall = [("w", k) for k in range(8)]
        wkk_all = [("wkk", k) for k in range(8)]
        wvw_all = [("wvw", k) for k in range(8)]

        chunks = []
        for c in range(4):
            chunks.append((c * 128, "qaT", c * 128, "q"))
        for c in range(4):
            chunks.append((512 + c * 128, "kaT", c * 128, "c"))
        for c in range(4):
            chunks.append((1536 + c * 128, "qbT", c * 128, "q"))
        for c in range(4):
            chunks.append((2176 + c * 128, "qiT", c * 128, "c"))
        chunks.append((None, "kk", 0, "c"))
        for c in range(8):
            chunks.append((2760 + c * 128, "gT", c * 128, "silu"))

        nev = 0
        nF = 0
        for g in range(NG):
            hs = g % 2
            for t4 in range(4):
                t = g * 4 + t4
                xs = t % 2
                tile_ops(t)
            for (c0, dst, drow, mode) in chunks:
                pf = nF % 4
                nF += 1
                for k in range(8):
                    if c0 is None:
                        lhs = (lambda k=k: wkk[:, k, :])
                        rk = wkk_all
                    else:
                        lhs = (lambda k=k, c0=c0: wsb[:, k, c0:c0 + 128])
                        rk = wall
                    P.add("pe", lambda e, k=k, lhs=lhs, pf=pf, hs=hs: e.matmul(
                        psF[pf][:], lhsT=lhs(), rhs=hT[hs][:, k, :], start=(k == 0), stop=(k == 7)),
                        r=[("hT", hs), rk[k]], w=[("psF", pf)])
                es_ = nev % 4
                nev += 1
                if mode == "silu":
                    P.add("act", lambda e, pf=pf, es_=es_: e.activation(out=ev[es_][:], in_=psF[pf][:], func=AF.Silu),
                          r=[("psF", pf)], w=[("ev", es_)])
                elif mode == "q":
                    P.add("dve", lambda e, pf=pf, es_=es_: e.tensor_scalar(out=ev[es_][:], in0=psF[pf][:], scalar1=0.125,
                                                                          scalar2=None, op0=ALU.mult),
                          r=[("psF", pf)], w=[("ev", es_)])
                else:
                    P.add("dve", lambda e, pf=pf, es_=es_: e.tensor_copy(out=ev[es_][:], in_=psF[pf][:]),
                          r=[("psF", pf)], w=[("ev", es_)])
                if dst == "kk":
                    P.add("pool", lambda e, es_=es_, g=g: e.dma_start(out=T["kbT"][:, _sl(g, 512)], in_=ev[es_][0:64, :]),
                          r=[("ev", es_)], dma=True)
                    P.add("pool", lambda e, es_=es_, g=g: e.dma_start(out=T["kiT"][:, _sl(g, 512)], in_=ev[es_][64:128, :]),
                          r=[("ev", es_)], dma=True)
                else:
                    P.add("pool", lambda e, es_=es_, g=g, dst=dst, drow=drow: e.dma_start(
                        out=T[dst][drow:drow + 128, _sl(g, 512)], in_=ev[es_][:]),
                        r=[("ev", es_)], dma=True)
            for t4 in range(4):
                t = g * 4 + t4
                pf = nF % 4
                nF += 1
                pw = t % 2
                for k in range(8):
                    P.add("pe", lambda e, k=k, pf=pf, hs=hs, t4=t4: e.matmul(
                        psF[pf][:], lhsT=hT[hs][:, k, _sl(t4, 128)], rhs=wsb[:, k, 1024:1536], start=(k == 0), stop=(k == 7)),
                        r=[("hT", hs), wall[k]], w=[("psF", pf)])
                for k in range(8):
                    P.add("pe", lambda e, k=k, pw=pw, hs=hs, t4=t4: e.matmul(
                        psW[pw][:], lhsT=hT[hs][:, k, _sl(t4, 128)], rhs=wvw[:, k, :], start=(k == 0), stop=(k == 7)),
                        r=[("hT", hs), wvw_all[k]], w=[("psW", pw)])
                es_ = nev % 4
                nev += 1
                P.add("act", lambda e, pf=pf, es_=es_: e.copy(out=ev[es_][:], in_=psF[pf][:]),
                      r=[("psF", pf)], w=[("ev", es_)])
                P.add("pool", lambda e, es_=es_, t=t: e.dma_start(out=T["va"][_sl(t, 128), :], in_=ev[es_][:]),
                      r=[("ev", es_)], dma=True)
                P.add("dve", lambda e, pw=pw: e.tensor_copy(out=evb[pw][:], in_=psW[pw][:, 0:64]),
                      r=[("psW", pw)], w=[("evb", pw)])
                P.add("dve", lambda e, pw=pw: e.tensor_copy(out=evw[pw][:], in_=psW[pw][:, 64:72]),
                      r=[("psW", pw)], w=[("evw", pw)])
                P.add("pool", lambda e, pw=pw, t=t: e.dma_start(out=T["vb"][_sl(t, 128), :], in_=evb[pw][:]),
                      r=[("evb", pw)], dma=True)
                P.add("pool", lambda e, pw=pw, t=t: e.dma_start(out=T["wi"][_sl(t, 128), :], in_=evw[pw][:]),
                      r=[("evw", pw)], dma=True)
        P.emit()


HB = [0, 2, 4, 6, 1, 3, 5, 7]


def load_bias_tiles(P, nc, sb, T, pfx):
    gsb = sb(pfx + "gsb", [128, 2, 12, 128], F32)
    cm = sb(pfx + "cm", [128, 128], F32)
    c12 = sb(pfx + "c12", [128, 12], F32)
    bT = sb(pfx + "bT", [128, 2, 12, 128], BF16)
    tmp = sb(pfx + "btmp", [128, 128], F32)
    P.add("sp", lambda e: e.dma_start(out=gsb[:], in_=T["c_gath"]), w=["gsb"], dma=True)
    P.add("sp", lambda e: e.dma_start(out=cm[:], in_=T["c_mask"][:, :]), w=["cm"], dma=True)
    P.add("sp", lambda e: e.dma_start(out=c12[:], in_=T["rel_bias"][15, :].partition_broadcast(128)), w=["c12"], dma=True)
    for h in range(12):
        ho = h if h < 4 else 4 + HB.index(h - 4)
        P.add("dve", lambda e, h=h: e.tensor_scalar(out=tmp[:], in0=gsb[:, 0, h, :], scalar1=c12[:, h:h + 1], scalar2=None,
                                                    op0=ALU.subtract), r=["gsb", "c12"], w=["btmp"])
        P.add("dve", lambda e, h=h, ho=ho: e.tensor_tensor(out=bT[:, 0, ho, :], in0=tmp[:], in1=cm[:], op=ALU.add),
              r=["btmp", "cm"], w=["bT"])
        P.add("dve", lambda e, h=h, ho=ho: e.tensor_scalar(out=bT[:, 1, ho, :], in0=gsb[:, 1, h, :], scalar1=c12[:, h:h + 1],
                                                           scalar2=None, op0=ALU.subtract), r=["gsb", "c12"], w=["bT"])
    return bT


def phase_diff(P, nc, S, T):
    NT = S // 128
    NG = S // 512
    LAM_INIT = 0.8 - 0.6 * math.exp(-0.3 * 0)
    with ExitStack() as es:
        def sb(name, shape, dt):
            return es.enter_context(nc.sbuf_tensor(name, shape, dt))

        def ps(name, shape, dt):
            return es.enter_context(nc.psum_tensor(name, shape, dt))

        ident = sb("da_id", [128, 128], BF16)
        onesb = sb("da_1b", [128, 128], BF16)
        onesf = sb("da_1f", [128, 128], F32)
        bT = load_bias_tiles(P, nc, sb, T, "da_")
        lp = sb("da_lp", [128, 4, 64], F32)
        lpp = sb("da_lpp", [128, 2, 64], F32)
        ls = sb("da_ls", [128, 2], F32)
        le = sb("da_le", [128, 2], F32)
        nlam = sb("da_nlam", [128, 1], F32)
        gs = sb("da_gs", [128, 1], F32)
        ka = [sb("da_ka%d" % i, [128, S], BF16) for i in range(2)]
        va = [sb("da_va%d" % i, [128, NT, 128], BF16) for i in range(2)]
        qb = [sb("da_q%d" % i, [128, 2, 512], BF16) for i in range(2)]
        gb = [sb("da_g%d" % i, [128, 512], BF16) for i in range(2)]
        pT = [sb("da_pT%d" % i, [128, 512], BF16) for i in range(4)]
        rzt = sb("da_rz", [128, 512], F32)
        tm = [sb("da_tm%d" % i, [128, 512], F32) for i in range(2)]
        ot = sb("da_o", [128, 512], F32)
        sq = sb("da_sq", [128, 512], F32)
        rt = sb("da_rt", [128, 512], F32)
        mx = [sb("da_mx%d" % i, [128, 512], BF16) for i in range(2)]
        Sp = [ps("da_S%d" % i, [128, 512], F32) for i in range(3)]
        Op = [ps("da_O%d" % i, [128, 512], F32) for i in range(2)]
        Zp = [ps("da_Z%d" % i, [128, 512], F32) for i in range(2)]
        Rp = ps("da_R", [128, 512], F32)

        P.add("sp", lambda e: e.dma_start(out=ident[:], in_=T["c_ident"][:, :]), w=["ident"], dma=True)
        P.add("pool", lambda e: e.memset(onesb[:], 1.0), w=["onesb"])
        P.add("pool", lambda e: e.memset(onesf[:], 1.0), w=["onesf"])
        for i_ in range(2):
            P.add("pool", lambda e, i_=i_: e.memset(qb[i_][:], 0.0), w=[("qb", i_)])
        for n, nm in enumerate(["e_lam_q1", "e_lam_k1", "e_lam_q2", "e_lam_k2"]):
            P.add("sp", lambda e, n=n, nm=nm: e.dma_start(out=lp[:, n, :], in_=T[nm].partition_broadcast(128)),
                  w=[("lp", n)], dma=True)
        P.add("sp", lambda e: e.dma_start(out=gs[:], in_=T["e_subln"].rearrange("(p o) -> p o", o=1)), w=["gs"], dma=True)
        for n in range(2):
            P.add("dve", lambda e, n=n: e.tensor_tensor(out=lpp[:, n, :], in0=lp[:, 2 * n, :], in1=lp[:, 2 * n + 1, :], op=ALU.mult),
                  r=[("lp", 2 * n), ("lp", 2 * n + 1)], w=[("lpp", n)])
            P.add("dve", lambda e, n=n: e.reduce_sum(out=ls[:, n:n + 1], in_=lpp[:, n, :], axis=AX.X),
                  r=[("lpp", n)], w=[("ls", n)])
            P.add("act", lambda e, n=n: e.activation(out=le[:, n:n + 1], in_=ls[:, n:n + 1], func=AF.Exp),
                  r=[("ls", n)], w=[("le", n)])
        P.add("dve", lambda e: e.tensor_tensor(out=nlam[:], in0=le[:, 1:2], in1=le[:, 0:1], op=ALU.subtract),
              r=[("le", 0), ("le", 1)], w=["nlam"])
        P.add("dve", lambda e: e.tensor_scalar(out=nlam[:], in0=nlam[:], scalar1=-LAM_INIT, scalar2=None, op0=ALU.add),
              r=["nlam"], w=["nlam"])
        P.add("dve", lambda e: e.tensor_scalar(out=gs[:], in0=gs[:], scalar1=1.0 - LAM_INIT, scalar2=None, op0=ALU.mult),
              r=["gs"], w=["gs"])

        steps = []
        for h in range(4):
            for J in range(NG):
                for m in range(2):
                    last = 4 * J + 3
                    for i in range(last + 1):
                        steps.append((h, J, m, i, last))

        def loads(h, J, m, i):
            if J == 0 and m == 0 and i == 0:
                s1 = h % 2
                P.add("sp", lambda e: e.dma_start(out=ka[s1][:], in_=T["kaT"][_sl(h, 128), :]), w=[("ka", s1)], dma=True)
                for c8 in range(0, NT, 8):
                    n8 = min(8, NT - c8)
                    P.add("sp", lambda e, c8=c8, n8=n8: e.dma_start(
                        out=va[s1][:, c8:c8 + n8, :],
                        in_=T["va"][c8 * 128:(c8 + n8) * 128, _sl(h, 128)].rearrange("(t p) e -> p t e", p=128)),
                        w=[("va", s1, c8 // 8)], dma=True)
            if m == 0 and i == 0:
                s2 = (h * NG + J) % 2
                P.add("sp", lambda e: e.dma_start(out=qb[s2][0:64, 0, :], in_=T["qaT"][h * 128:h * 128 + 64, _sl(J, 512)]),
                      w=[("qb", s2)], dma=True)
                P.add("sp", lambda e: e.dma_start(out=qb[s2][64:128, 1, :], in_=T["qaT"][h * 128 + 64:h * 128 + 128, _sl(J, 512)]),
                      w=[("qb", s2)], dma=True)
                P.add("sp", lambda e: e.dma_start(out=gb[s2][:], in_=T["gT"][_sl(h, 128), _sl(J, 512)]), w=[("gb", s2)], dma=True)

        def qk(n):
            h, J, m, i, last = steps[n]
            loads(h, J, m, i)
            c0 = max(0, i - 4 * J) * 128
            sbk = n % 3
            qs = (h * NG + J) % 2
            adds = []
            if i >= 4 * J:
                adds.append((c0, 0))
            if i + 1 >= 4 * J and i + 1 <= last:
                adds.append(((i + 1 - 4 * J) * 128, 1))
            P.add("pe", lambda e: e.matmul(Sp[sbk][:, c0:512], lhsT=ka[h % 2][:, _sl(i, 128)],
                                           rhs=qb[qs][:, m, c0:512], start=True, stop=(len(adds) == 0)),
                  r=[("ka", h % 2), ("qb", qs)], w=[("Sp", sbk)])
            for a, (cs, r) in enumerate(adds):
                P.add("pe", lambda e, cs=cs, r=r, a=a: e.matmul(Sp[sbk][:, cs:cs + 128], lhsT=ident[:], rhs=bT[:, r, h, :],
                                                                start=False, stop=(a == len(adds) - 1)),
                      r=["ident", "bT"], w=[("Sp", sbk)])

        def pv(n):
            h, J, m, i, last = steps[n]
            c0 = max(0, i - 4 * J) * 128
            sbk = n % 3
            pt = n % 4
            osl = (2 * (h * NG + J) + m) % 2
            P.add("act", lambda e: e.activation(out=pT[pt][:, c0:512], in_=Sp[sbk][:, c0:512], func=AF.Exp),
                  r=[("Sp", sbk)], w=[("pT", pt)])
            P.add("pe", lambda e: e.matmul(Op[osl][:, c0:512], lhsT=va[h % 2][:, i, :], rhs=pT[pt][:, c0:512],
                                           start=(i == 0), stop=(i == last)),
                  r=[("va", h % 2, i // 8), ("pT", pt)], w=[("Op", osl)])
            P.add("pe", lambda e: e.matmul(Zp[osl][:, c0:512], lhsT=onesb[:], rhs=pT[pt][:, c0:512],
                                           start=(i == 0), stop=(i == last)),
                  r=["onesb", ("pT", pt)], w=[("Zp", osl)])
            if i != last:
                return
            P.add("dve", lambda e: e.reciprocal(out=rzt[:], in_=Zp[osl][:]), r=[("Zp", osl)], w=["rzt"])
            P.add("dve", lambda e: e.tensor_tensor(out=tm[m][:], in0=Op[osl][:], in1=rzt[:], op=ALU.mult),
                  r=[("Op", osl), "rzt"], w=[("tm", m)])
            if m == 0:
                return
            gsl = (h * NG + J) % 2
            P.add("dve", lambda e: e.scalar_tensor_tensor(out=ot[:], in0=tm[1][:], scalar=nlam[:, 0:1], in1=tm[0][:],
                                                          op0=ALU.mult, op1=ALU.add),
                  r=[("tm", 0), ("tm", 1), "nlam"], w=["ot"])
            P.add("act", lambda e: e.activation(out=sq[:], in_=ot[:], func=AF.Square), r=["ot"], w=["sq"])
            def epi_b():
                P.add("pe", lambda e: e.matmul(Rp[:], lhsT=onesf[:], rhs=sq[:], start=True, stop=True),
                      r=["onesf", "sq"], w=["Rp"])
                P.add("act", lambda e: e.activation(out=rt[:], in_=Rp[:], func=AF.Sqrt, bias=EPS, scale=1.0 / 128),
                      r=["Rp"], w=["rt"])
                P.add("dve", lambda e: e.reciprocal(out=rt[:], in_=rt[:]), r=["rt"], w=["rt"])
                P.add("dve", lambda e: e.scalar_tensor_tensor(out=ot[:], in0=ot[:], scalar=gs[:, 0:1], in1=rt[:],
                                                              op0=ALU.mult, op1=ALU.mult),
                      r=["ot", "gs", "rt"], w=["ot"])
                P.add("dve", lambda e: e.tensor_tensor(out=mx[gsl][:], in0=ot[:], in1=gb[gsl][:], op=ALU.mult),
                      r=["ot", ("gb", gsl)], w=[("mx", gsl)])
                P.add("pool", lambda e: e.dma_start(out=T["mixT"][_sl(h, 128), _sl(J, 512)], in_=mx[gsl][:]),
                      r=[("mx", gsl)], dma=True)
            deferred.append((n + 6, epi_b))

        deferred = []

        def run_deferred(n):
            while deferred and deferred[0][0] <= n:
                deferred.pop(0)[1]()

        LA = 2
        for n in range(min(LA, len(steps))):
            qk(n)
        for n in range(len(steps)):
            if n + LA < len(steps):
                qk(n + LA)
            run_deferred(n)
            pv(n)
        run_deferred(len(steps) + 10)
        P.emit()


NIT = 20


def phase_dsa(P, nc, S, T, topk):
    NT = S // 128
    with ExitStack() as es:
        def sb(name, shape, dt):
            return es.enter_context(nc.sbuf_tensor(name, shape, dt))

        def ps(name, shape, dt):
            return es.enter_context(nc.psum_tensor(name, shape, dt))

        ident4 = sb("ds_id4", [128, 4, 128], BF16)
        onesf = sb("ds_1f", [128, 64], F32)
        pw2 = sb("ds_pw2", [128, NIT], F32)
        ki2 = sb("ds_ki2", [128, S], BF16)
        kb2 = sb("ds_kb2", [128, S], BF16)
        vb1 = sb("ds_vb1", [128, NT, 66], BF16)
        wis = sb("ds_wi", [128, NT, 8], F32)
        qi = [sb("ds_qi%d" % i, [128, 8, 128], BF16) for i in range(2)]
        qb = [sb("ds_qb%d" % i, [128, 8, 128], BF16) for i in range(3)]
        junk2 = sb("ds_junk2", [128, S], BF16)
        gB = [sb("ds_gB%d" % i, [64, 8, 128], BF16) for i in range(4)]
        acc = [sb("ds_acc%d" % i, [128, S], F32) for i in range(2)]
        R = [sb("ds_R%d" % i, [128, 512], F32) for i in range(4)]
        NM = [sb("ds_NM%d" % i, [128, S], BF16) for i in range(2)]
        junk = sb("ds_junk", [128, S], BF16)
        pT = [sb("ds_pT%d" % i, [128, 512], BF16) for i in range(4)]
        sm = sb("ds_sm", [128, 8], F32)
        Wk = sb("ds_Wk", [128, NIT], F32)
        rz = sb("ds_rz", [128, 1024], F32)
        osb = [sb("ds_osb%d" % i, [64, 512], F32) for i in range(2)]
        on = sb("ds_on", [64, 512], F32)
        mxB = [sb("ds_mxB%d" % i, [64, 512], BF16) for i in range(2)]
        Zp = [ps("ds_Z%d" % i, [128, 512], F32) for i in range(3)]
        Sp = [ps("ds_S%d" % i, [128, 512], F32) for i in range(3)]
        Op = [ps("ds_O%d" % i, [128, 512], F32) for i in range(2)]
        P.add("pool", lambda e: e.memset(onesf[:], 1.0), w=["onesf"])
        for i_ in range(3):
            P.add("pool", lambda e, i_=i_: e.memset(qb[i_][:], 0.0), w=[("qb", i_)])
        for i_ in range(2):
            P.add("pool", lambda e, i_=i_: e.memset(qi[i_][:], 0.0), w=[("qi", i_)])
        for half in range(2):
            P.add("sp", lambda e, half=half: e.dma_start(out=ki2[_sl(half, 64), :], in_=T["kiT"][:, :]), w=["ki2"], dma=True)

        def wis_chunk(c8):
            n8 = min(8, NT - c8)
            P.add("sp", lambda e: e.dma_start(
                out=wis[:, c8:c8 + n8, :], in_=T["wi"][c8 * 128:(c8 + n8) * 128, :].rearrange("(t p) h -> p t h", p=128)),
                w=[("wis", c8 // 8)], dma=True)
        wis_chunk(0)
        late = {}

        def late_setup():
            for half in range(2):
                P.add("sp", lambda e, half=half: e.dma_start(out=kb2[_sl(half, 64), :], in_=T["kbT"][:, :]), w=["kb2"], dma=True)
            for a in range(4):
                P.add("sp", lambda e, a=a: e.dma_start(out=ident4[:, a, :], in_=T["c_ident"][:, :]), w=["ident4"], dma=True)
            late["bT"] = load_bias_tiles(P, nc, sb, T, "ds_")
            P.add("sp", lambda e: e.dma_start(out=pw2[:], in_=T["c_pow2"].partition_broadcast(128)), w=["pw2"], dma=True)
            P.add("pool", lambda e: e.memset(vb1[:, :, 64:66], 1.0), w=["vb1"])
            for c8 in range(0, NT, 8):
                n8 = min(8, NT - c8)
                P.add("sp", lambda e, c8=c8, n8=n8: e.dma_start(
                    out=vb1[:, c8:c8 + n8, 0:64], in_=T["vb"][c8 * 128:(c8 + n8) * 128, :].rearrange("(t p) d -> p t d", p=128)),
                    w=["vb1"], dma=True)
                if c8 > 0:
                    wis_chunk(c8)

        cnt = dict(z=0, r=0, s=0, p=0)

        def indexer_items(j):
            s = j % 2
            s3 = j % 3
            nk = (j + 1) * 128
            nch = (nk + 511) // 512

            def u_load():
                P.add("sp", lambda e: e.dma_start(out=qi[s][0:64, 0:8:2, :],
                                                  in_=T["qiT"].rearrange("(c p) s -> p c s", p=128)[0:64, :, _sl(j, 128)]),
                      w=[("qi", s)], dma=True)
                P.add("sp", lambda e: e.dma_start(out=qi[s][64:128, 1:8:2, :],
                                                  in_=T["qiT"].rearrange("(c p) s -> p c s", p=128)[64:128, :, _sl(j, 128)]),
                      w=[("qi", s)], dma=True)
                P.add("sp", lambda e: e.dma_start(out=qb[s3][0:64, 0:8:2, :],
                                                  in_=T["qbT"].rearrange("(c p) s -> p c s", p=128)[0:64, :, _sl(j, 128)]),
                      w=[("qb", s3)], dma=True)
                P.add("sp", lambda e: e.dma_start(out=qb[s3][64:128, 1:8:2, :],
                                                  in_=T["qbT"].rearrange("(c p) s -> p c s", p=128)[64:128, :, _sl(j, 128)]),
                      w=[("qb", s3)], dma=True)
                P.add("sp", lambda e: e.dma_start(out=gB[j % 4][:], in_=T["gT"][512:1024, _sl(j, 128)].rearrange("(h d) q -> d h q", d=64)),
                      w=[("gB", j % 4)], dma=True)

            def mk(h, kc):
                st = {}
                N = min(512, nk - kc * 512)
                k0 = kc * 512

                def pe():
                    zs = cnt["z"] % 3
                    cnt["z"] += 1
                    st["zs"] = zs
                    P.add("pe", lambda e: e.matmul(Zp[zs][:, 0:N], lhsT=qi[s][:, h, :], rhs=ki2[:, k0:k0 + N],
                                                   start=True, stop=True),
                          r=[("qi", s), "ki2"], w=[("Zp", zs)])

                def rest():
                    zs = st["zs"]
                    rs_ = cnt["r"] % 4
                    cnt["r"] += 1
                    P.add("act", lambda e: e.activation(out=R[rs_][:, 0:N], in_=Zp[zs][:, 0:N], func=AF.Relu),
                          r=[("Zp", zs)], w=[("R", rs_)])
                    if h == 0:
                        P.add("dve", lambda e: e.tensor_scalar(out=acc[s][:, k0:k0 + N], in0=R[rs_][:, 0:N], scalar1=wis[:, j, 0:1],
                                                               scalar2=None, op0=ALU.mult),
                              r=[("R", rs_), ("wis", j // 8)], w=[("acc", s)])
                    else:
                        P.add("dve", lambda e: e.scalar_tensor_tensor(out=acc[s][:, k0:k0 + N], in0=R[rs_][:, 0:N],
                                                                      scalar=wis[:, j, h:h + 1], in1=acc[s][:, k0:k0 + N],
                                                                      op0=ALU.mult, op1=ALU.add),
                              r=[("R", rs_), ("wis", j // 8), ("acc", s)], w=[("acc", s)])
                    if h == 7 and kc == nch - 1:
                        P.add("dve", lambda e: e.memset(acc[s][0:64, nk - 64:nk], -1e30), w=[("acc", s)])
                return (pe, rest)
            items = [mk(h, kc) for h in range(8) for kc in range(nch)]
            return u_load, items

        def bisect_units(j):
            s = j % 2
            nk = (j + 1) * 128
            A = [("acc", s)]
            units = []

            def u_nm():
                P.add("dve", lambda e: e.tensor_scalar(out=NM[s][:, 0:nk], in0=acc[s][:, 0:nk], scalar1=sm[:, 1:2], scalar2=NEG,
                                                       op0=ALU.is_lt, op1=ALU.mult),
                      r=A + ["lo"], w=[("NM", s)])
                if "dbg_nm" in T:
                    P.add("pool", lambda e: e.dma_start(out=T["dbg_nm"][_sl(j, 128), 0:nk], in_=NM[s][:, 0:nk]), r=[("NM", s)], dma=True)
                    P.add("pool", lambda e: e.dma_start(out=T["dbg_acc"][_sl(j, 128), 0:nk], in_=acc[s][:, 0:nk]), r=[("acc", s)], dma=True)

            if nk <= topk:
                def u0():
                    P.add("dve", lambda e: e.memset(sm[:, 1:2], -1e29), w=["lo"])
                    u_nm()
                return [u0]
            assert nk - 64 >= topk
            h1 = max(64, int(round(0.38 * nk / 64.0)) * 64)
            n2 = nk - h1

            def u_init():
                P.add("dve", lambda e: e.reduce_max(out=sm[:, 0:1], in_=acc[s][:, 0:nk], axis=AX.X), r=A, w=["hi"])
                P.add("dve", lambda e: e.tensor_reduce(out=sm[:, 1:2], in_=acc[s][:, 0:nk - 64], axis=AX.X, op=ALU.min),
                      r=A, w=["lo"])
                P.add("dve", lambda e: e.tensor_tensor(out=sm[:, 2:3], in0=sm[:, 0:1], in1=sm[:, 1:2], op=ALU.subtract),
                      r=["hi", "lo"], w=["w0"])
                P.add("dve", lambda e: e.tensor_scalar(out=Wk[:], in0=pw2[:], scalar1=sm[:, 2:3], scalar2=None, op0=ALU.mult),
                      r=["pw2", "w0"], w=["Wk"])
                P.add("dve", lambda e: e.tensor_tensor(out=sm[:, 3:4], in0=sm[:, 1:2], in1=Wk[:, 0:1], op=ALU.add),
                      r=["lo", "Wk"], w=["mid"])
            units.append(u_init)

            def mk(k):
                def u():
                    kk_ = k + 1 if k + 1 < NIT else k
                    P.add("dve", lambda e: e.tensor_scalar(out=junk[:, 0:h1], in0=acc[s][:, 0:h1], scalar1=sm[:, 3:4],
                                                           scalar2=-(topk - 0.5 - n2 / 2.0),
                                                           op0=ALU.is_ge, op1=ALU.add, accum_out=sm[:, 4:5]),
                          r=A + ["mid"], w=["junk", "cnt"])
                    P.add("act", lambda e: e.activation(out=junk2[:, 0:n2], in_=acc[s][:, h1:nk], func=AF.Sign, scale=-1.0,
                                                        bias=sm[:, 3:4], accum_out=sm[:, 6:7]),
                          r=A + ["mid"], w=["junk2", "sgn"])
                    P.add("dve", lambda e: e.tensor_tensor(out=sm[:, 7:8], in0=sm[:, 3:4], in1=Wk[:, kk_:kk_ + 1], op=ALU.subtract),
                          r=["mid", "Wk"], w=["m2"])
                    P.add("dve", lambda e: e.scalar_tensor_tensor(out=sm[:, 5:6], in0=sm[:, 4:5], scalar=2.0, in1=sm[:, 6:7],
                                                                  op0=ALU.mult, op1=ALU.is_ge),
                          r=["cnt", "sgn"], w=["pw"])
                    dst = (3, "mid") if k + 1 < NIT else (1, "lo")
                    P.add("dve", lambda e: e.scalar_tensor_tensor(out=sm[:, dst[0]:dst[0] + 1], in0=sm[:, 5:6], scalar=Wk[:, k:k + 1],
                                                                  in1=sm[:, 7:8], op0=ALU.mult, op1=ALU.add),
                          r=["pw", "Wk", "m2"], w=[dst[1]])
                    if k + 1 >= NIT:
                        u_nm()
                return u
            for k in range(NIT):
                units.append(mk(k))
            return units

        def att_qk(j, i, b):
            s = j % 2
            s3 = j % 3
            ss = cnt["s"] % 3
            cnt["s"] += 1
            near = i >= j - 1
            P.add("pe", lambda e: e.matmul(Sp[ss][:], lhsT=NM[s][:, _sl(i, 128)],
                                           rhs=ident4[:].rearrange("p a q -> p (a q)"), start=True, stop=False),
                  r=[("NM", s), "ident4"], w=[("Sp", ss)])
            for hh in range(4):
                h = HB[b * 4 + hh]
                c, half = h // 2, h % 2
                P.add("pe", lambda e, hh=hh, c=c, half=half: e.matmul(
                    Sp[ss][:, _sl(hh, 128)], lhsT=kb2[:, _sl(i, 128)], rhs=qb[s3][:, 2 * c + half, :],
                    start=False, stop=(hh == 3 and not near)),
                    r=["kb2", ("qb", s3)], w=[("Sp", ss)])
            if near:
                r_ = j - i
                P.add("pe", lambda e: e.matmul(
                    Sp[ss][:], lhsT=ident4[:, 0, :], rhs=late["bT"][:, r_, 4 + 4 * b:8 + 4 * b, :].rearrange("p h q -> p (h q)"),
                    start=False, stop=True),
                    r=["ident4", "bT"], w=[("Sp", ss)])
            return ss

        def att_pv(j, i, b, ss):
            pp = cnt["p"] % 4
            cnt["p"] += 1
            P.add("act", lambda e: e.activation(out=pT[pp][:], in_=Sp[ss][:], func=AF.Exp),
                  r=[("Sp", ss)], w=[("pT", pp)])
            P.add("pe", lambda e: e.matmul(Op[b][0:65, :], lhsT=vb1[:, i, 0:65], rhs=pT[pp][:],
                                           start=(i == 0), stop=(i == j)),
                  r=["vb1", ("pT", pp)], w=[("Op", b)])

        def att_epi_a(j, b):
            P.add("dve", lambda e: e.reciprocal(out=rz[64:65, _sl(b, 512)], in_=Op[b][64:65, :]), r=[("Op", b)], w=[("rz", b)])
            P.add("act", lambda e: e.copy(out=osb[b][:], in_=Op[b][0:64, :]), r=[("Op", b)], w=[("osb", b)])

        def att_epi_b(j, b):
            s = j % 4
            zs = cnt["z"] % 3
            cnt["z"] += 1
            BC = Zp[zs]
            P.add("pe", lambda e: e.matmul(BC[0:64, :], lhsT=onesf[64:65, :], rhs=rz[64:65, _sl(b, 512)], start=True, stop=True),
                  r=["onesf", ("rz", b)], w=[("Zp", zs)])
            P.add("dve", lambda e: e.tensor_tensor(out=on[:], in0=osb[b][:], in1=BC[0:64, :], op=ALU.mult),
                  r=[("osb", b), ("Zp", zs)], w=["on"])
            P.add("dve", lambda e: e.tensor_tensor(out=mxB[b][:].rearrange("p (h q) -> p h q", h=4),
                                                   in0=on[:].rearrange("p (h q) -> p h q", h=4),
                                                   in1=gB[s][:, b:8:2, :], op=ALU.mult),
                  r=["on", ("gB", s)], w=[("mxB", b)])
            P.add("pool", lambda e: e.dma_start(
                out=T["mixT"][512:1024, _sl(j, 128)].rearrange("(h d) q -> d h q", d=64)[:, b:8:2, :],
                in_=mxB[b][:].rearrange("p (h q) -> p h q", h=4)),
                r=[("mxB", b)], dma=True)

        LA_ = 2
        pending = []

        def run_slot(t):
            Bt = bisect_units(t) if 0 <= t < NT else []
            ja = t - 1 if 0 <= t - 1 < NT else None
            steps_ = [(i, b) for i in range(ja + 1) for b in range(2)] if ja is not None else []
            info = {}
            nA = len(steps_)
            u_load, items = indexer_items(t + 1) if 0 <= t + 1 < NT else (None, [])
            nI = len(items)
            nr = max(len(Bt), 1)
            if u_load is not None:
                u_load()
            qk_done = [0]

            def qk_upto(m):
                while qk_done[0] < min(nA, m):
                    n_ = qk_done[0]
                    info[n_] = att_qk(ja, *steps_[n_])
                    qk_done[0] += 1

            for r in range(nr):
                if r < len(Bt):
                    Bt[r]()
                il, ih = r * nI // nr, (r + 1) * nI // nr
                al, ah = r * nA // nr, (r + 1) * nA // nr
                if pending and r == min(2, nr - 1):
                    pending.pop(0)()
                first = True
                x = il
                while first or x < ih:
                    batch = items[x:min(ih, x + 3)]
                    x += 3
                    for pe_, _ in batch:
                        pe_()
                    if first:
                        for n_ in range(al, ah):
                            qk_upto(n_ + LA_ + 1)
                            att_pv(ja, steps_[n_][0], steps_[n_][1], info[n_])
                        first = False
                    for _, rest_ in batch:
                        rest_()
            if ja is not None:
                att_epi_a(ja, 0)
                att_epi_a(ja, 1)
                pending.append(lambda: (att_epi_b(ja, 0), att_epi_b(ja, 1)))

        for t in range(-1, NT + 1):
            run_slot(t)
            if t == -1:
                late_setup()
        while pending:
            pending.pop(0)()
        P.emit()


def phase_outproj(P, nc, S, T, mix, wname, xin, xout, final):
    NT = S // 128
    NG = S // 512
    pfx = "o%d_" % (1 if final else 0)
    with ExitStack() as es:
        def sb(name, shape, dt):
            return es.enter_context(nc.sbuf_tensor(pfx + name, shape, dt))

        def ps(name, shape, dt):
            return es.enter_context(nc.psum_tensor(pfx + name, shape, dt))

        wo = sb("wo", [128, 8, D], BF16)
        wst = [sb("wst%d" % i, [128, D], F32) for i in range(4)]
        mt = [sb("mt%d" % i, [128, 8, 512], BF16) for i in range(2)]
        xt = [sb("xt%d" % i, [128, D], F32) for i in range(2)]
        xo = [sb("xo%d" % i, [128, D], F32) for i in range(2)]
        acc = [ps("acc%d" % i, [128, 512], F32) for i in range(4)]
        if final:
            gbc = sb("gbc", [128, D], F32)
            junk = sb("junk", [128, D], BF16)
            ssq = sb("ssq", [128, NT], F32)
            P.add("sp", lambda e: e.dma_start(out=gbc[:], in_=T["final_norm"].partition_broadcast(128)), w=["gbc"], dma=True)
        for k in range(8):
            s = k % 4
            P.add("sp", lambda e, k=k, s=s: e.dma_start(out=wst[s][:], in_=T[wname][_sl(k, 128), :]), w=[("wst", s)], dma=True)
            P.add("act", lambda e, k=k, s=s: e.copy(out=wo[:, k, :], in_=wst[s][:]), r=[("wst", s)], w=[("wo", k)])
        wall = [("wo", k) for k in range(8)]
        na = 0
        for g in range(NG):
            ms = g % 2
            P.add("sp", lambda e, g=g, ms=ms: e.dma_start(
                out=mt[ms][:], in_=T[mix].rearrange("(c p) s -> p c s", p=128)[:, :, _sl(g, 512)]), w=[("mt", ms)], dma=True)
            for t4 in range(4):
                t = g * 4 + t4
                xs = t % 2
                P.add("sp", lambda e, t=t, xs=xs: e.dma_start(out=xt[xs][:], in_=T[xin][_sl(t, 128), :]), w=[("xt", xs)], dma=True)
                for nh in range(2):
                    a = na % 4
                    na += 1
                    for k in range(8):
                        P.add("pe", lambda e, k=k, a=a, ms=ms, t4=t4, nh=nh: e.matmul(
                            acc[a][:], lhsT=mt[ms][:, k, _sl(t4, 128)], rhs=wo[:, k, _sl(nh, 512)], start=(k == 0), stop=(k == 7)),
                            r=[("mt", ms)] + wall, w=[("acc", a)])
                    P.add("dve", lambda e, a=a, xs=xs, nh=nh: e.tensor_tensor(
                        out=xo[xs][:, _sl(nh, 512)], in0=acc[a][:], in1=xt[xs][:, _sl(nh, 512)], op=ALU.add),
                        r=[("acc", a), ("xt", xs)], w=[("xo", xs, nh)])
                X = [("xo", xs, 0), ("xo", xs, 1)]
                if not final:
                    P.add("pool", lambda e, t=t, xs=xs: e.dma_start(out=T[xout][_sl(t, 128), :], in_=xo[xs][:]),
                          r=X, dma=True, grp=("xo", xs))
                else:
                    P.add("act", lambda e, t=t, xs=xs: e.activation(out=junk[:], in_=xo[xs][:], func=AF.Square,
                                                                    accum_out=ssq[:, t:t + 1]), r=X, w=["junk", ("ssq", t)])
                    P.add("act", lambda e, t=t: e.activation(out=ssq[:, t:t + 1], in_=ssq[:, t:t + 1], func=AF.Sqrt,
                                                             bias=EPS, scale=1.0 / D), r=[("ssq", t)], w=[("ssq", t)])
                    P.add("dve", lambda e, t=t: e.reciprocal(out=ssq[:, t:t + 1], in_=ssq[:, t:t + 1]),
                          r=[("ssq", t)], w=[("ssq", t)])
                    P.add("dve", lambda e, t=t, xs=xs: e.scalar_tensor_tensor(
                        out=xo[xs][:], in0=xo[xs][:], scalar=ssq[:, t:t + 1], in1=gbc[:], op0=ALU.mult, op1=ALU.mult),
                        r=X + [("ssq", t), "gbc"], w=X)
                    P.add("pool", lambda e, t=t, xs=xs: e.dma_start(out=T[xout][_sl(t, 128), :], in_=xo[xs][:]),
                          r=X, dma=True, grp=("xo", xs))
        P.emit()


def phase_proj1(P, nc, S, T):
    NT = S // 128
    NG = S // 512
    with ExitStack() as es:
        def sb(name, shape, dt):
            return es.enter_context(nc.sbuf_tensor("p1_" + name, shape, dt))

        def ps(name, shape, dt):
            return es.enter_context(nc.psum_tensor("p1_" + name, shape, dt))

        w1 = sb("w", [128, 8, L1_COLS], BF16)
        wrot = sb("wrot", [128, 8, 32], BF16)
        wqn = sb("wqn", [128, 3, 8, 128], BF16)
        wqr = sb("wqr", [128, 3, 4, 128], BF16)
        wqrr = sb("wqrr", [128, 3, 4, 128], BF16)
        wkk = sb("wkk", [128, 2, 16, 64], BF16)
        wkv = sb("wkv", [128, 2, 16, 64], BF16)
        wst = [sb("wst%d" % i, [128, 2048], F32) for i in range(2)]
        gcol = sb("g", [128, 8], F32)
        qn = sb("qn", [128, 3], F32)
        kvn = sb("kvn", [128, 2], F32)
        cs = sb("cos", [128, S], F32)
        sn = sb("sin", [128, S], F32)
        onesf = sb("1f", [128, 128], F32)
        ident = sb("id", [128, 128], BF16)
        xt = [sb("x%d" % i, [128, D], F32) for i in range(2)]
        hb = [sb("h%d" % i, [128, D], BF16) for i in range(2)]
        hT = [sb("hT%d" % i, [128, 8, 512], BF16) for i in range(2)]
        junk = sb("junk", [128, D], BF16)
        ssq = sb("ssq", [128, NT], F32)
        rs = sb("rs", [128, NT], F32)
        qlT = sb("qlT", [128, 3, 512], BF16)
        kvT = sb("kvT", [128, 2, 512], BF16)
        sqq = [sb("sqq%d" % i, [128, 512], F32) for i in range(2)]
        sqkv = sb("sqkv", [128, 2, 512], F32)
        rq = sb("rq", [128, 512], F32)
        rkv = sb("rkv", [128, 512], F32)
        rkt = sb("rkt", [128, 4], F32)
        ra2 = [sb("ra%d" % i, [128, 512], F32) for i in range(2)]
        rb2 = [sb("rb%d" % i, [128, 512], F32) for i in range(2)]
        ev = [sb("ev%d" % i, [128, 512], BF16) for i in range(4)]
        psT = [ps("psT%d" % i, [128, D], BF16) for i in range(2)]
        psF = [ps("psF%d" % i, [128, 512], F32) for i in range(3)]
        Rq = ps("Rq", [128, 512], F32)
        Rkv = ps("Rkv", [128, 512], F32)
        Rt = ps("Rt", [128, 4], F32)

        def ld(dst, src, key, **kw):
            P.add("sp", lambda e: e.dma_start(out=dst, in_=src, **kw), w=[key], dma=True)

        ld(ident[:], T["c_ident"][:, :], "ident")
        P.add("pool", lambda e: e.memset(onesf[:], 1.0), w=["onesf"])
        ld(gcol[:], T["o_norm"].rearrange("(k p) -> p k", p=128), "gcol", allow_slow_non_contiguous=True)
        ld(qn[:], T["o_q_norm"].rearrange("(k p) -> p k", p=128), "qn", allow_slow_non_contiguous=True)
        ld(kvn[:], T["o_kv_norm"].rearrange("(k p) -> p k", p=128), "kvn", allow_slow_non_contiguous=True)
        nst = [0]

        def stage(src, ncols, fn_list, rkeys):
            s_ = nst[0] % 2
            nst[0] += 1
            P.add("sp", lambda e: e.dma_start(out=wst[s_][:, 0:ncols], in_=src), w=[("wst", s_)], dma=True)
            for (q, fn, wk) in fn_list:
                P.add(q, (lambda e, fn=fn: fn(e, wst[s_])), r=[("wst", s_)] + rkeys, w=[wk])

        done_tiles = set()

        def tile_ops(t):
            if t in done_tiles:
                return
            done_tiles.add(t)
            xs = t % 2
            hs = (t // 4) % 2
            t4 = t % 4
            P.add("sp", lambda e: e.dma_start(out=xt[xs][:], in_=T["x1"][_sl(t, 128), :]), w=[("xt", xs)], dma=True)
            P.add("act", lambda e: e.activation(out=junk[:], in_=xt[xs][:], func=AF.Square, accum_out=ssq[:, t:t + 1]),
                  r=[("xt", xs)], w=["junk", ("ssq", t)])
            P.add("act", lambda e: e.activation(out=rs[:, t:t + 1], in_=ssq[:, t:t + 1], func=AF.Sqrt, bias=EPS, scale=1.0 / D),
                  r=[("ssq", t)], w=[("rs", t)])
            P.add("dve", lambda e: e.reciprocal(out=rs[:, t:t + 1], in_=rs[:, t:t + 1]), r=[("rs", t)], w=[("rs", t)])
            P.add("dve", lambda e: e.tensor_scalar(out=hb[xs][:], in0=xt[xs][:], scalar1=rs[:, t:t + 1], scalar2=None, op0=ALU.mult),
                  r=[("xt", xs), ("rs", t)], w=[("hb", xs)])
            for k in range(8):
                P.add("pe", lambda e, k=k: e.transpose(out=psT[xs][:, _sl(k, 128)], in_=hb[xs][:, _sl(k, 128)], identity=ident[:]),
                      r=[("hb", xs), "ident"], w=[("psT", xs)])
            P.add("act", lambda e: e.copy(out=hT[hs][:, :, _sl(t4, 128)], in_=psT[xs][:].rearrange("p (k t) -> p k t", k=8)),
                  r=[("psT", xs)], w=[("hT", hs)])

        tile_ops(0)
        tile_ops(1)
        for k in range(8):
            stage(T["o_w_in"][_sl(k, 128), :], L1_COLS, [
                ("dve", lambda e, st, k=k: e.tensor_scalar(out=w1[:, k, :], in0=st[:, 0:L1_COLS], scalar1=gcol[:, k:k + 1],
                                                          scalar2=None, op0=ALU.mult), ("w1", k)),
                ("dve", lambda e, st, k=k: e.tensor_scalar(out=wrot[:, k, 0:16], in0=st[:, 656:672], scalar1=gcol[:, k:k + 1],
                                                          scalar2=-1.0, op0=ALU.mult, op1=ALU.mult), ("wrot", k)),
                ("dve", lambda e, st, k=k: e.tensor_scalar(out=wrot[:, k, 16:32], in0=st[:, 640:656], scalar1=gcol[:, k:k + 1],
                                                          scalar2=None, op0=ALU.mult), ("wrot", k)),
            ], ["gcol"])
        for c in range(3):
            def v3(st):
                return st[:, 0:1536].rearrange("p (h j) -> p h j", j=96)
            stage(T["o_w_uq"][_sl(c, 128), :], 1536, [
                ("dve", lambda e, st, c=c: e.tensor_scalar(out=wqn[:, c, :, :].rearrange("p a (b j) -> p (a b) j", j=64),
                                                          in0=v3(st)[:, :, 0:64], scalar1=qn[:, c:c + 1],
                                                          scalar2=None, op0=ALU.mult), ("wqn", c)),
                ("dve", lambda e, st, c=c: e.tensor_scalar(out=wqr[:, c, :, :].rearrange("p a (b j) -> p (a b) j", j=32),
                                                          in0=v3(st)[:, :, 64:96], scalar1=qn[:, c:c + 1],
                                                          scalar2=None, op0=ALU.mult), ("wqr", c)),
                ("dve", lambda e, st, c=c: e.tensor_scalar(out=wqrr[:, c, :, :].rearrange("p a (b j) -> p (a b) j", j=32)[:, :, 0:16],
                                                          in0=v3(st)[:, :, 80:96], scalar1=qn[:, c:c + 1],
                                                          scalar2=-1.0, op0=ALU.mult, op1=ALU.mult), ("wqrr", c)),
                ("dve", lambda e, st, c=c: e.tensor_scalar(out=wqrr[:, c, :, :].rearrange("p a (b j) -> p (a b) j", j=32)[:, :, 16:32],
                                                          in0=v3(st)[:, :, 64:80], scalar1=qn[:, c:c + 1],
                                                          scalar2=None, op0=ALU.mult), ("wqrr", c)),
            ], ["qn"])
        for c in range(2):
            def v4(st):
                return st[:, 0:2048].rearrange("p (h j) -> p h j", j=128)
            stage(T["o_w_ukv"][_sl(c, 128), :], 2048, [
                ("dve", lambda e, st, c=c: e.tensor_scalar(out=wkk[:, c, :, :], in0=v4(st)[:, :, 0:64], scalar1=kvn[:, c:c + 1],
                                                          scalar2=None, op0=ALU.mult), ("wkk", c)),
                ("dve", lambda e, st, c=c: e.tensor_scalar(out=wkv[:, c, :, :], in0=v4(st)[:, :, 64:128], scalar1=kvn[:, c:c + 1],
                                                          scalar2=None, op0=ALU.mult), ("wkv", c)),
            ], ["kvn"])
        for i_ in range(4):
            ld(cs[32 * i_:32 * i_ + 32, :], T["c_cos"][:, :], "cos")
            ld(sn[32 * i_:32 * i_ + 32, :], T["c_sin"][:, :], "sin")
        W1 = [("w1", k) for k in range(8)]
        WROT = [("wrot", k) for k in range(8)]
        WQN = [("wqn", c) for c in range(3)]
        WQR = [("wqr", c) for c in range(3)]
        WQRR = [("wqrr", c) for c in range(3)]
        WKK = [("wkk", c) for c in range(2)]
        WKV = [("wkv", c) for c in range(2)]
        cnt = dict(f=0, e=0, q=0)

        def nf():
            cnt["f"] += 1
            return (cnt["f"] - 1) % 3

        def ne():
            cnt["e"] += 1
            return (cnt["e"] - 1) % 4

        def mm8(pf, M, lhs_fn, rkeys, hs):
            for k in range(8):
                P.add("pe", lambda e, k=k: e.matmul(psF[pf][0:M, :], lhsT=lhs_fn(k), rhs=hT[hs][:, k, :], start=(k == 0), stop=(k == 7)),
                      r=[("hT", hs), rkeys[k]], w=[("psF", pf)])

        def store(q_tile, dst):
            P.add("pool", lambda e: e.dma_start(out=dst, in_=q_tile[0]), r=[q_tile[1]], dma=True, grp=q_tile[1])

        def group(g):
            hs = g % 2
            G = slice(g * 512, (g + 1) * 512)
            for t4 in range(4):
                tile_ops(g * 4 + t4)
            for c in range(3):
                def qlat(c=c):
                    pf = nf()
                    mm8(pf, 128, lambda k: w1[:, k, c * 128:(c + 1) * 128], W1, hs)
                    sq_ = cnt["q"] % 2
                    cnt["q"] += 1
                    P.add("act", lambda e: e.copy(out=qlT[:, c, :], in_=psF[pf][:]), r=[("psF", pf)], w=[("qlT", c)])
                    P.add("act", lambda e: e.activation(out=sqq[sq_][:], in_=psF[pf][:], func=AF.Square), r=[("psF", pf)], w=[("sqq", sq_)])
                    P.add("pe", lambda e: e.matmul(Rq[:], lhsT=onesf[:], rhs=sqq[sq_][:], start=(c == 0), stop=(c == 2)),
                          r=["onesf", ("sqq", sq_)], w=["Rq"])
                qlat()
            P.add("act", lambda e: e.activation(out=rq[:], in_=Rq[:], func=AF.Sqrt, bias=EPS, scale=1.0 / 384), r=["Rq"], w=["rq"])
            P.add("dve", lambda e: e.reciprocal(out=rq[:], in_=rq[:]), r=["rq"], w=["rq"])
            for c in range(2):
                def kvlat(c=c):
                    pf = nf()
                    mm8(pf, 128, lambda k: w1[:, k, 384 + c * 128:384 + (c + 1) * 128], W1, hs)
                    P.add("act", lambda e: e.copy(out=kvT[:, c, :], in_=psF[pf][:]), r=[("psF", pf)], w=[("kvT", c)])
                    P.add("act", lambda e: e.activation(out=sqkv[:, c, :], in_=psF[pf][:], func=AF.Square), r=[("psF", pf)], w=[("sqkv", c)])
                    P.add("pe", lambda e: e.matmul(Rkv[:], lhsT=onesf[:], rhs=sqkv[:, c, :], start=(c == 0), stop=(c == 1)),
                          r=["onesf", ("sqkv", c)], w=["Rkv"])
                kvlat()
            for t4 in range(4):
                for c in range(2):
                    P.add("pe", lambda e, t4=t4, c=c: e.matmul(Rt[:, t4:t4 + 1], lhsT=sqkv[:, c, _sl(t4, 128)], rhs=onesf[:, 0:1],
                                                               start=(c == 0), stop=(c == 1)),
                          r=["onesf", ("sqkv", 0), ("sqkv", 1)], w=["Rt"])
            P.add("act", lambda e: e.activation(out=rkv[:], in_=Rkv[:], func=AF.Sqrt, bias=EPS, scale=1.0 / 256), r=["Rkv"], w=["rkv"])
            P.add("dve", lambda e: e.reciprocal(out=rkv[:], in_=rkv[:]), r=["rkv"], w=["rkv"])
            P.add("act", lambda e: e.activation(out=rkt[:], in_=Rt[:], func=AF.Sqrt, bias=EPS, scale=1.0 / 256), r=["Rt"], w=["rkt"])
            P.add("dve", lambda e: e.reciprocal(out=rkt[:], in_=rkt[:]), r=["rkt"], w=["rkt"])
            def krope():
                ra, rb = ra2[0], rb2[0]
                pa, pb = nf(), nf()
                mm8(pa, 32, lambda k: w1[:, k, 640:672], W1, hs)
                mm8(pb, 32, lambda k: wrot[:, k, :], WROT, hs)
                e_ = ne()
                P.add("dve", lambda e: e.tensor_tensor(out=ra[0:32, :], in0=psF[pa][0:32, :], in1=cs[0:32, G], op=ALU.mult),
                      r=[("psF", pa), "cos"], w=[("ra", 0)])
                P.add("dve", lambda e: e.tensor_tensor(out=rb[0:32, :], in0=psF[pb][0:32, :], in1=sn[0:32, G], op=ALU.mult),
                      r=[("psF", pb), "sin"], w=[("rb", 0)])
                P.add("dve", lambda e: e.tensor_tensor(out=ev[e_][0:32, :], in0=ra[0:32, :], in1=rb[0:32, :], op=ALU.add),
                      r=[("ra", 0), ("rb", 0)], w=[("ev", e_)])
                store((ev[e_][0:32, :], ("ev", e_)), T["kTr"][:, G])
            krope()
            for c in range(8):
                def gate(c=c):
                    pf = nf()
                    mm8(pf, 128, lambda k: w1[:, k, 672 + c * 128:672 + (c + 1) * 128], W1, hs)
                    e_ = ne()
                    P.add("act", lambda e: e.activation(out=ev[e_][:], in_=psF[pf][:], func=AF.Silu), r=[("psF", pf)], w=[("ev", e_)])
                    store((ev[e_][:], ("ev", e_)), T["g1T"][_sl(c, 128), G])
                gate()
            QL = [("qlT", c) for c in range(3)]
            KV = [("kvT", c) for c in range(2)]
            for qd in range(4):
                def qrope(qd=qd):
                    ra, rb = ra2[qd % 2], rb2[qd % 2]
                    rak, rbk = ("ra", qd % 2), ("rb", qd % 2)
                    pa, pb = nf(), nf()
                    for c in range(3):
                        P.add("pe", lambda e, c=c: e.matmul(psF[pa][:], lhsT=wqr[:, c, qd, :], rhs=qlT[:, c, :], start=(c == 0), stop=(c == 2)),
                              r=QL + WQR, w=[("psF", pa)])
                    for c in range(3):
                        P.add("pe", lambda e, c=c: e.matmul(psF[pb][:], lhsT=wqrr[:, c, qd, :], rhs=qlT[:, c, :], start=(c == 0), stop=(c == 2)),
                              r=QL + WQRR, w=[("psF", pb)])
                    e_ = ne()
                    P.add("dve", lambda e: e.tensor_tensor(out=ra[:], in0=psF[pa][:], in1=cs[:, G], op=ALU.mult),
                          r=[("psF", pa), "cos"], w=[rak])
                    P.add("dve", lambda e: e.tensor_tensor(out=rb[:], in0=psF[pb][:], in1=sn[:, G], op=ALU.mult),
                          r=[("psF", pb), "sin"], w=[rbk])
                    P.add("dve", lambda e: e.tensor_tensor(out=ra[:], in0=ra[:], in1=rb[:], op=ALU.add), r=[rak, rbk], w=[rak])
                    P.add("dve", lambda e: e.tensor_tensor(out=ev[e_][:], in0=ra[:], in1=rq[:], op=ALU.mult),
                          r=[rak, "rq"], w=[("ev", e_)])
                    for i_ in range(4):
                        store((ev[e_][32 * i_:32 * i_ + 32, :], ("ev", e_)), T["qT"][4 * qd + i_, 64:96, G])
                qrope()
            for pr in range(8):
                def qnope(pr=pr):
                    pf = nf()
                    for c in range(3):
                        P.add("pe", lambda e, c=c: e.matmul(psF[pf][:], lhsT=wqn[:, c, pr, :], rhs=qlT[:, c, :], start=(c == 0), stop=(c == 2)),
                              r=QL + WQN, w=[("psF", pf)])
                    e_ = ne()
                    P.add("dve", lambda e: e.tensor_tensor(out=ev[e_][:], in0=psF[pf][:], in1=rq[:], op=ALU.mult),
                          r=[("psF", pf), "rq"], w=[("ev", e_)])
                    for i_ in range(2):
                        store((ev[e_][64 * i_:64 * i_ + 64, :], ("ev", e_)), T["qT"][2 * pr + i_, 0:64, G])
                qnope()

                def kpair(pr=pr):
                    pf = nf()
                    for c in range(2):
                        P.add("pe", lambda e, c=c: e.matmul(psF[pf][:], lhsT=wkk[:, c, 2 * pr:2 * pr + 2, :].rearrange("p h j -> p (h j)"),
                                                            rhs=kvT[:, c, :], start=(c == 0), stop=(c == 1)),
                              r=KV + WKK, w=[("psF", pf)])
                    e_ = ne()
                    P.add("dve", lambda e: e.tensor_tensor(out=ev[e_][:], in0=psF[pf][:], in1=rkv[:], op=ALU.mult),
                          r=[("psF", pf), "rkv"], w=[("ev", e_)])
                    for i_ in range(2):
                        store((ev[e_][64 * i_:64 * i_ + 64, :], ("ev", e_)), T["kT"][2 * pr + i_, :, G])
                kpair()
            for t4 in range(4):
                for half in range(2):
                    def vtile(t4=t4, half=half):
                        pf = nf()
                        for c in range(2):
                            P.add("pe", lambda e, c=c: e.matmul(
                                psF[pf][:], lhsT=kvT[:, c, _sl(t4, 128)],
                                rhs=wkv[:, c, 8 * half:8 * half + 8, :].rearrange("p h j -> p (h j)"), start=(c == 0), stop=(c == 1)),
                                r=KV + WKV, w=[("psF", pf)])
                        e_ = ne()
                        P.add("act", lambda e: e.activation(out=ev[e_][:], in_=psF[pf][:], func=AF.Copy, scale=rkt[:, t4:t4 + 1]),
                              r=[("psF", pf), "rkt"], w=[("ev", e_)])
                        store((ev[e_][:], ("ev", e_)), T["v1"][g * 512 + t4 * 128:g * 512 + (t4 + 1) * 128, _sl(half, 512)])
                    vtile()

        for g in range(NG):
            group(g)
        P.emit()


def phase_mla(P, nc, S, T):
    NT = S // 128
    NG = S // 512
    SCALE = 96 ** -0.5
    with ExitStack() as es:
        def sb(name, shape, dt):
            return es.enter_context(nc.sbuf_tensor("ml_" + name, shape, dt))

        def ps(name, shape, dt):
            return es.enter_context(nc.psum_tensor("ml_" + name, shape, dt))

        ident = sb("id", [128, 128], BF16)
        onesf = sb("1f", [128, 64], F32)
        cmf = sb("cmf", [128, 128], F32)
        cmb = sb("cmb", [128, 128], BF16)
        kt = [sb("kt%d" % i, [96, S], BF16) for i in range(2)]
        vt = [sb("vt%d" % i, [128, NT, 66], BF16) for i in range(2)]
        qb = [sb("q%d" % i, [96, 512], BF16) for i in range(2)]
        gb = [sb("g%d" % i, [64, 512], BF16) for i in range(3)]
        pT = [sb("pT%d" % i, [128, 512], BF16) for i in range(4)]
        rz2 = [sb("rz%d" % i, [128, 512], F32) for i in range(2)]
        osb2 = [sb("osb%d" % i, [64, 512], F32) for i in range(2)]
        on = sb("on", [64, 512], F32)
        mx = [sb("mx%d" % i, [64, 512], BF16) for i in range(2)]
        Sp = [ps("S%d" % i, [128, 512], F32) for i in range(3)]
        Op = [ps("O%d" % i, [128, 512], F32) for i in range(2)]
        BC = ps("BC", [128, 512], F32)

        P.add("sp", lambda e: e.dma_start(out=ident[:], in_=T["c_ident"][:, :]), w=["ident"], dma=True)
        P.add("sp", lambda e: e.dma_start(out=cmf[:], in_=T["c_mask"][:, :]), w=["cmf"], dma=True)
        P.add("dve", lambda e: e.tensor_copy(out=cmb[:], in_=cmf[:]), r=["cmf"], w=["cmb"])
        P.add("pool", lambda e: e.memset(onesf[:], 1.0), w=["onesf"])
        for i_ in range(2):
            P.add("pool", lambda e, i_=i_: e.memset(vt[i_][:, :, 64:66], 1.0), w=[("vt1", i_)])

        steps = []
        for hd in range(16):
            for J in range(NG):
                last = 4 * J + 3
                for i in range(last + 1):
                    steps.append((hd, J, i, last))

        def loads(hd, J, i):
            if J == 0 and i == 0:
                s1 = hd % 2
                P.add("sp", lambda e: e.dma_start(out=kt[s1][64:96, :], in_=T["kTr"][:, :]), w=[("kt", s1)], dma=True)
                P.add("sp", lambda e: e.dma_start(out=kt[s1][0:64, :], in_=T["kT"][hd, :, :]), w=[("kt", s1)], dma=True)
                for c8 in range(0, NT, 8):
                    n8 = min(8, NT - c8)
                    P.add("sp", lambda e, c8=c8, n8=n8: e.dma_start(
                        out=vt[s1][:, c8:c8 + n8, 0:64],
                        in_=T["v1"][c8 * 128:(c8 + n8) * 128, _sl(hd, 64)].rearrange("(t p) d -> p t d", p=128)),
                        w=[("vt", s1, c8 // 8)], dma=True)
            if i == 0:
                s2 = (hd * NG + J) % 2
                P.add("sp", lambda e: e.dma_start(out=qb[s2][:], in_=T["qT"][hd, :, _sl(J, 512)]), w=[("qb", s2)], dma=True)
                g3 = (hd * NG + J) % 3
                P.add("sp", lambda e: e.dma_start(out=gb[g3][:], in_=T["g1T"][_sl(hd, 64), _sl(J, 512)]), w=[("gb", g3)], dma=True)

        def qk(n):
            hd, J, i, last = steps[n]
            loads(hd, J, i)
            c0 = max(0, i - 4 * J) * 128
            sbk = n % 3
            qs = (hd * NG + J) % 2
            diag = i >= 4 * J
            P.add("pe", lambda e: e.matmul(Sp[sbk][:, c0:512], lhsT=kt[hd % 2][:, _sl(i, 128)], rhs=qb[qs][:, c0:512],
                                           start=True, stop=not diag),
                  r=[("kt", hd % 2), ("qb", qs)], w=[("Sp", sbk)])
            if diag:
                P.add("pe", lambda e: e.matmul(Sp[sbk][:, c0:c0 + 128], lhsT=ident[:], rhs=cmb[:], start=False, stop=True),
                      r=["ident", "cmb"], w=[("Sp", sbk)])

        def pv(n):
            hd, J, i, last = steps[n]
            c0 = max(0, i - 4 * J) * 128
            sbk = n % 3
            pt = n % 4
            osl = (hd * NG + J) % 2
            vs = hd % 2
            P.add("act", lambda e: e.activation(out=pT[pt][:, c0:512], in_=Sp[sbk][:, c0:512], func=AF.Exp, scale=SCALE),
                  r=[("Sp", sbk)], w=[("pT", pt)])
            P.add("pe", lambda e: e.matmul(Op[osl][0:65, c0:512], lhsT=vt[vs][:, i, 0:65], rhs=pT[pt][:, c0:512],
                                           start=(i == 0), stop=(i == last)),
                  r=[("vt", vs, i // 8), ("vt1", vs), ("pT", pt)], w=[("Op", osl)])
            if i != last:
                return
            gsl = (hd * NG + J) % 2
            rz, osb = rz2[gsl], osb2[gsl]
            g3 = (hd * NG + J) % 3
            P.add("dve", lambda e: e.reciprocal(out=rz[64:65, :], in_=Op[osl][64:65, :]), r=[("Op", osl)], w=[("rz", gsl)])
            P.add("act", lambda e: e.copy(out=osb[:], in_=Op[osl][0:64, :]), r=[("Op", osl)], w=[("osb", gsl)])

            def epi_b():
                P.add("pe", lambda e: e.matmul(BC[0:64, :], lhsT=onesf[64:65, :], rhs=rz[64:65, :], start=True, stop=True),
                      r=["onesf", ("rz", gsl)], w=["BC"])
                P.add("dve", lambda e: e.tensor_tensor(out=on[:], in0=osb[:], in1=BC[0:64, :], op=ALU.mult),
                      r=[("osb", gsl), "BC"], w=["on"])
                P.add("dve", lambda e: e.tensor_tensor(out=mx[gsl][:], in0=on[:], in1=gb[g3][:], op=ALU.mult),
                      r=["on", ("gb", g3)], w=[("mx", gsl)])
                P.add("pool", lambda e: e.dma_start(out=T["mix1T"][_sl(hd, 64), _sl(J, 512)], in_=mx[gsl][:]),
                      r=[("mx", gsl)], dma=True)
            deferred.append((n + 6, epi_b))

        deferred = []

        def run_deferred(n):
            while deferred and deferred[0][0] <= n:
                deferred.pop(0)[1]()

        LA = 2
        for n in range(min(LA, len(steps))):
            qk(n)
        for n in range(len(steps)):
            if n + LA < len(steps):
                qk(n + LA)
            run_deferred(n)
            pv(n)
        run_deferred(len(steps) + 10)
        P.emit()


SCRATCH0 = dict(
    qaT=([512, None], BF16), kaT=([512, None], BF16), va=([None, 512], BF16),
    qbT=([512, None], BF16), kbT=([64, None], BF16), vb=([None, 64], BF16),
    qiT=([512, None], BF16), kiT=([64, None], BF16), wi=([None, 8], F32),
    gT=([1024, None], BF16), mixT=([1024, None], BF16), x1=([None, 1024], F32),
    qT=([16, 96, None], BF16), kT=([16, 64, None], BF16), kTr=([32, None], BF16), v1=([None, 1024], BF16),
    g1T=([1024, None], BF16), mix1T=([1024, None], BF16),
)


def build(S, topk, phases, outs):
    nc = bass.Bass("TRN2", target_bir_lowering=False)
    T = {}

    def din(name, shape, dt=F32):
        T[name] = nc.dram_tensor(name, shape, dt, kind="ExternalInput").ap()

    din("x", [S, D])
    din("e_norm", [D])
    din("e_w_in", [D, L0_COLS])
    din("c_ident", [128, 128], BF16)
    din("c_gath", [128, 2, 12, 128])
    din("c_mask", [128, 128])
    din("c_pow2", [NIT])
    din("rel_bias", [32, 12])
    for nm in ("e_lam_q1", "e_lam_k1", "e_lam_q2", "e_lam_k2"):
        din(nm, [64])
    din("e_subln", [128])
    din("e_w_o", [D, D])
    din("o_norm", [D])
    din("o_w_in", [D, L1_COLS])
    din("o_q_norm", [384])
    din("o_w_uq", [384, 1536])
    din("o_kv_norm", [256])
    din("o_w_ukv", [256, 2048])
    din("o_w_o", [D, D])
    din("final_norm", [D])
    din("c_cos", [32, S])
    din("c_sin", [32, S])
    for name, (shape, dt) in SCRATCH0.items():
        shp = [S if v is None else v for v in shape]
        kind = "ExternalOutput" if name in outs else "Internal"
        T[name] = nc.dram_tensor(name, shp, dt, kind=kind).ap()
    T["out"] = nc.dram_tensor("out", [S, D], F32, kind="ExternalOutput").ap()
    if "dbg_nm" in outs:
        T["dbg_nm"] = nc.dram_tensor("dbg_nm", [S, S], BF16, kind="ExternalOutput").ap()
        T["dbg_acc"] = nc.dram_tensor("dbg_acc", [S, S], F32, kind="ExternalOutput").ap()
    with ExitStack() as es:
        P = Prog(nc, es)
        if "proj0" in phases:
            phase_proj0(P, nc, S, T)
        if "diff" in phases:
            phase_diff(P, nc, S, T)
        if "dsa" in phases:
            phase_dsa(P, nc, S, T, topk)
        if "op0" in phases:
            phase_outproj(P, nc, S, T, "mixT", "e_w_o", "x", "x1", False)
        if "proj1" in phases:
            phase_proj1(P, nc, S, T)
        if "mla" in phases:
            phase_mla(P, nc, S, T)
        if "op1" in phases:
            phase_outproj(P, nc, S, T, "mix1T", "o_w_o", "x1", "out", True)
    return nc


def t5_bucket_np(rel):
    nb = 16
    ret = np.where(rel > 0, nb, 0)
    n = np.abs(rel)
    max_exact = nb // 2
    n_f = np.maximum(n, 1).astype(np.float32)
    large = max_exact + (np.log(n_f / max_exact) / math.log(128 / max_exact) * (nb - max_exact)).astype(np.int32)
    large = np.minimum(large, nb - 1)
    return ret + np.where(n < max_exact, n, large)


def host_consts(rel_bias):
    kk = np.arange(128)[:, None]
    qq = np.arange(128)[None, :]
    gath = np.zeros((2, 128, 12, 128), np.float32)
    for r in range(2):
        idx = t5_bucket_np((kk - r * 128) - qq)
        gath[r] = np.transpose(np.asarray(rel_bias)[idx], (0, 2, 1))
    cmask = np.where((kk // 64) <= (qq // 64), 0.0, NEG).astype(np.float32)
    pow2 = (0.5 ** np.arange(1, NIT + 1)).astype(np.float32)
    gath = np.ascontiguousarray(np.transpose(gath, (1, 0, 2, 3)))
    return dict(c_ident=np.eye(128).astype(ml_dtypes.bfloat16), c_gath=gath, c_mask=cmask, c_pow2=pow2)


def rope_consts(S):
    inv = (np.float32(10000.0) ** (-np.arange(0, 32, 2, dtype=np.float32) / np.float32(32))).astype(np.float32)
    ang = (np.arange(S, dtype=np.float32)[:, None] * inv[None, :]).astype(np.float32)
    c = np.cos(ang).astype(np.float32).T
    s_ = np.sin(ang).astype(np.float32).T
    return dict(c_cos=np.ascontiguousarray(np.concatenate([c, c], 0)), c_sin=np.ascontiguousarray(np.concatenate([s_, s_], 0)))


ALL_PHASES = ("proj0", "diff", "dsa", "op0", "proj1", "mla", "op1")
S_FULL = 4096
TOPK = 256


def kernel(x, rel_bias, e_norm, e_w_in, e_lam_q1, e_lam_k1, e_lam_q2, e_lam_k2, e_subln, e_w_o,
           o_norm, o_w_in, o_q_norm, o_w_uq, o_kv_norm, o_w_ukv, o_w_o, final_norm):
    f = lambda a: np.ascontiguousarray(np.asarray(a, dtype=np.float32))
    x = f(x)
    B = x.shape[0]
    shared = dict(rel_bias=f(rel_bias), e_norm=f(e_norm)[0], e_w_in=f(e_w_in)[0], e_lam_q1=f(e_lam_q1)[0],
                  e_lam_k1=f(e_lam_k1)[0], e_lam_q2=f(e_lam_q2)[0], e_lam_k2=f(e_lam_k2)[0], e_subln=f(e_subln)[0],
                  e_w_o=f(e_w_o)[0], o_norm=f(o_norm)[0], o_w_in=f(o_w_in)[0], o_q_norm=f(o_q_norm)[0],
                  o_w_uq=f(o_w_uq)[0], o_kv_norm=f(o_kv_norm)[0], o_w_ukv=f(o_w_ukv)[0], o_w_o=f(o_w_o)[0],
                  final_norm=f(final_norm))
    shared.update(host_consts(shared["rel_bias"]))
    shared.update(rope_consts(S_FULL))
    nc = build(S_FULL, TOPK, ALL_PHASES, ())
    in_maps = [dict(shared, x=x[b]) for b in range(B)]
    res = run_bass_kernel_spmd(nc, in_maps, core_ids=list(range(B)))
    return np.stack([np.asarray(r["out"], dtype=np.float32) for r in res.results], axis=0)
```
